# Optimizing a Trainium2 kernel written in Bass

```python
import math
import jax, jax.numpy as jnp
from jax import lax
import numpy as np

D_MODEL = 2048
BATCH = 8
SEQ = 4096
DEPTH = 4

CTX_LEN = 256
GRID_W = 64
MIX_WIDTH = D_MODEL
POOL_WIDTH = 3 * MIX_WIDTH // 4
SSM_WIDTH = MIX_WIDTH - POOL_WIDTH
POOL_WINDOWS = (2, 4, 8, 16)
N_POOL_GROUPS = len(POOL_WINDOWS)
POOL_GROUP = POOL_WIDTH // N_POOL_GROUPS
SSM_GROUP = 16
N_SSM_GROUPS = SSM_WIDTH // SSM_GROUP
SSM_STATE = 64
D_FF = 5632
CONV_K = 3
DT_MIN, DT_MAX = 1e-3, 1e-1
EPS = 1e-6

kernel_name = "hybrid_pool_s5_prefix_dit_block"


def rms_norm(x, gain):
    xf = x.astype(jnp.float32)
    y = xf * lax.rsqrt(jnp.mean(xf * xf, axis=-1, keepdims=True) + EPS)
    return (y * gain.astype(jnp.float32)).astype(x.dtype)


def modulate(h, shift, scale):
    return h * (1 + scale[:, None, :]) + shift[:, None, :]


def multiscale_pool(u, w_pool, pool_scale):
    bsz, n, _ = u.shape
    uf = u.astype(jnp.float32)
    cs = jnp.concatenate([jnp.zeros((bsz, 1, POOL_WIDTH), jnp.float32), jnp.cumsum(uf, axis=1)], axis=1)
    t = jnp.arange(n)
    parts = []
    for g, w in enumerate(POOL_WINDOWS):
        lo = jnp.maximum(t - w // 2, 0)
        hi = jnp.minimum(t + w // 2, n)
        sl = slice(g * POOL_GROUP, (g + 1) * POOL_GROUP)
        csg = cs[..., sl]
        cnt = (hi - lo).astype(jnp.float32)[None, :, None]
        mean = (jnp.take(csg, hi, axis=1) - jnp.take(csg, lo, axis=1)) / cnt
        parts.append(mean - uf[..., sl])
    p = jnp.stack(parts, axis=2)
    y = jnp.einsum('blgc,gcd->blgd', p, w_pool.astype(jnp.float32))
    return (y.reshape(bsz, n, POOL_WIDTH) * pool_scale.astype(jnp.float32)).astype(u.dtype)


def s5_discretise(a_re, a_im, log_dt, b_re, b_im):
    a_re = a_re.astype(jnp.float32)
    a_im = a_im.astype(jnp.float32)
    dt = jnp.exp(log_dt.astype(jnp.float32))[:, None]
    mag = jnp.exp(a_re * dt)
    lam_re = mag * jnp.cos(a_im * dt)
    lam_im = mag * jnp.sin(a_im * dt)
    denom = a_re * a_re + a_im * a_im
    nr, ni = lam_re - 1.0, lam_im
    f_re = (nr * a_re + ni * a_im) / denom
    f_im = (ni * a_re - nr * a_im) / denom
    b_re = b_re.astype(jnp.float32)
    b_im = b_im.astype(jnp.float32)
    bb_re = f_re[..., None] * b_re - f_im[..., None] * b_im
    bb_im = f_re[..., None] * b_im + f_im[..., None] * b_re
    return lam_re, lam_im, bb_re, bb_im


def _complex_linear_recurrence_op(left, right):
    a1r, a1i, b1r, b1i = left
    a2r, a2i, b2r, b2i = right
    return (a2r * a1r - a2i * a1i,
            a2r * a1i + a2i * a1r,
            a2r * b1r - a2i * b1i + b2r,
            a2r * b1i + a2i * b1r + b2i)


def s5_scan(u_g, lam_re, lam_im, bb_re, bb_im, h0, reverse):
    b_re = jnp.einsum('blgh,gph->blgp', u_g, bb_re)
    b_im = jnp.einsum('blgh,gph->blgp', u_g, bb_im)
    if h0 is not None:
        pos = -1 if reverse else 0
        h0_re, h0_im = h0
        b_re = b_re.at[:, pos].add(lam_re * h0_re - lam_im * h0_im)
        b_im = b_im.at[:, pos].add(lam_re * h0_im + lam_im * h0_re)
    a_re = jnp.broadcast_to(lam_re, b_re.shape)
    a_im = jnp.broadcast_to(lam_im, b_im.shape)
    _, _, h_re, h_im = lax.associative_scan(
        _complex_linear_recurrence_op, (a_re, a_im, b_re, b_im), axis=1, reverse=reverse)
    return h_re, h_im


def s5_readout(h, c_re, c_im):
    h_re, h_im = h
    return (jnp.einsum('blgp,ghp->blgh', h_re, c_re.astype(jnp.float32))
            - jnp.einsum('blgp,ghp->blgh', h_im, c_im.astype(jnp.float32)))


def to_ssm_groups(u_ssm):
    bsz, n, _ = u_ssm.shape
    return u_ssm.astype(jnp.float32).reshape(bsz, n, N_SSM_GROUPS, SSM_GROUP)


def s5_head_output(u_ssm, y, ssm_d, w_glu):
    bsz, n, _ = u_ssm.shape
    yf = y.reshape(bsz, n, SSM_WIDTH) + ssm_d.astype(jnp.float32) * u_ssm.astype(jnp.float32)
    yf = jax.nn.gelu(yf)
    return (yf * jax.nn.sigmoid(yf @ w_glu.astype(jnp.float32))).astype(u_ssm.dtype)


def mix_project(u, ssm_y, w_pool, pool_scale, ssm_d, w_glu, w_out):
    pool_out = multiscale_pool(u[..., :POOL_WIDTH], w_pool, pool_scale)
    ssm_out = s5_head_output(u[..., POOL_WIDTH:], ssm_y, ssm_d, w_glu)
    return jnp.concatenate([pool_out, ssm_out], axis=-1) @ w_out


def conv_glu_ffn(h, w_up, w_conv, w_down, rows):
    bsz, n, _ = h.shape
    z = h @ w_up
    if rows is None:
        grid = z[:, None]
        k = w_conv[1:2]
    else:
        grid = z.reshape(bsz, rows, GRID_W, 2 * D_FF)
        k = w_conv
    grid = lax.conv_general_dilated(grid, k[:, :, None, :], (1, 1), 'SAME',
                                    dimension_numbers=('NHWC', 'HWIO', 'NHWC'),
                                    feature_group_count=2 * D_FF)
    val, gate = jnp.split(grid.reshape(bsz, n, 2 * D_FF), 2, axis=-1)
    return (val * jax.nn.silu(gate)) @ w_down


def setup_inputs(seed: int = 0) -> dict:
    key = jax.random.key(seed)
    ks = jax.random.split(key, 26)
    f32 = jnp.float32
    nrm = lambda k, shape, s: jax.random.normal(k, shape, f32) * s
    G, P, H = N_SSM_GROUPS, SSM_STATE, SSM_GROUP
    a_im_base = jnp.pi * jnp.arange(P, dtype=f32)
    return {
        "x": nrm(ks[0], (BATCH, SEQ, D_MODEL), 1.0),
        "c": nrm(ks[1], (BATCH, D_MODEL), 1.0),
        "ctx": nrm(ks[2], (BATCH, CTX_LEN, D_MODEL), 1.0),
        "c_ctx": nrm(ks[3], (D_MODEL,), 1.0),
        "w_ada": nrm(ks[4], (DEPTH, D_MODEL, 6 * D_MODEL), 0.5 * D_MODEL ** -0.5),
        "b_ada": nrm(ks[5], (DEPTH, 6 * D_MODEL), 0.02),
        "w_in": nrm(ks[6], (DEPTH, D_MODEL, MIX_WIDTH), D_MODEL ** -0.5),
        "w_pool": nrm(ks[7], (DEPTH, N_POOL_GROUPS, POOL_GROUP, POOL_GROUP), POOL_GROUP ** -0.5),
        "pool_scale": 1.0 + nrm(ks[8], (DEPTH, POOL_WIDTH), 0.02),
        "ssm_a_re": -0.5 + nrm(ks[9], (DEPTH, 2, G, P), 0.01),
        "ssm_a_im": a_im_base + nrm(ks[10], (DEPTH, 2, G, P), 0.01),
        "ssm_log_dt": jax.random.uniform(ks[11], (DEPTH, 2, G), f32, math.log(DT_MIN), math.log(DT_MAX)),
        "ssm_b_re": nrm(ks[12], (DEPTH, 2, G, P, H), (2 * H) ** -0.5),
        "ssm_b_im": nrm(ks[13], (DEPTH, 2, G, P, H), (2 * H) ** -0.5),
        "ssm_c_re": nrm(ks[14], (DEPTH, 2, G, H, P), P ** -0.5),
        "ssm_c_im": nrm(ks[15], (DEPTH, 2, G, H, P), P ** -0.5),
        "ssm_d": nrm(ks[16], (DEPTH, SSM_WIDTH), 1.0),
        "w_glu": nrm(ks[17], (DEPTH, SSM_WIDTH, SSM_WIDTH), SSM_WIDTH ** -0.5),
        "w_out": nrm(ks[18], (DEPTH, MIX_WIDTH, D_MODEL), MIX_WIDTH ** -0.5),
        "g_pre_mix": 1.0 + nrm(ks[19], (DEPTH, D_MODEL), 0.02),
        "g_post_mix": 1.0 + nrm(ks[20], (DEPTH, D_MODEL), 0.02),
        "g_pre_ffn": 1.0 + nrm(ks[21], (DEPTH, D_MODEL), 0.02),
        "g_post_ffn": 1.0 + nrm(ks[22], (DEPTH, D_MODEL), 0.02),
        "w_up": nrm(ks[23], (DEPTH, D_MODEL, 2 * D_FF), D_MODEL ** -0.5),
        "w_conv": nrm(ks[24], (DEPTH, CONV_K, CONV_K, 2 * D_FF), 1.0 / CONV_K),
        "w_down": nrm(ks[25], (DEPTH, D_FF, D_MODEL), D_FF ** -0.5),
    }


def reference(x, c, ctx, c_ctx, w_ada, b_ada, w_in, w_pool, pool_scale, ssm_a_re, ssm_a_im,
              ssm_log_dt, ssm_b_re, ssm_b_im, ssm_c_re, ssm_c_im, ssm_d, w_glu, w_out,
              g_pre_mix, g_post_mix, g_pre_ffn, g_post_ffn, w_up, w_conv, w_down):
    n_lat = x.shape[1]
    rows = n_lat // GRID_W
    s_c = jax.nn.silu(c)
    s_ctx = jax.nn.silu(c_ctx)[None, :]
    for l in range(DEPTH):
        last = l == DEPTH - 1
        mx = jnp.split(s_c @ w_ada[l] + b_ada[l], 6, axis=-1)
        mc = jnp.split(s_ctx @ w_ada[l] + b_ada[l], 6, axis=-1)
        disc = [s5_discretise(ssm_a_re[l, d], ssm_a_im[l, d], ssm_log_dt[l, d],
                              ssm_b_re[l, d], ssm_b_im[l, d]) for d in range(2)]

        h_ctx = modulate(rms_norm(ctx, g_pre_mix[l]), mc[0], mc[1])
        h_lat = modulate(rms_norm(x, g_pre_mix[l]), mx[0], mx[1])
        u_lat = h_lat @ w_in[l]
        if last:
            u_ctx_ssm = h_ctx @ w_in[l][:, POOL_WIDTH:]
        else:
            u_ctx = h_ctx @ w_in[l]
            u_ctx_ssm = u_ctx[..., POOL_WIDTH:]
        ctx_g = to_ssm_groups(u_ctx_ssm)
        lat_g = to_ssm_groups(u_lat[..., POOL_WIDTH:])
        lat_dirs, ctx_dirs = [], []
        for d in range(2):
            rev = d == 1
            lam_re, lam_im, bb_re, bb_im = disc[d]
            hc = s5_scan(ctx_g, lam_re, lam_im, bb_re, bb_im, None, rev)
            fin = 0 if rev else -1
            h0 = (hc[0][:, fin], hc[1][:, fin])
            hl = s5_scan(lat_g, lam_re, lam_im, bb_re, bb_im, h0, rev)
            lat_dirs.append(s5_readout(hl, ssm_c_re[l, d], ssm_c_im[l, d]))
            if not last:
                ctx_dirs.append(s5_readout(hc, ssm_c_re[l, d], ssm_c_im[l, d]))
        mix_lat = mix_project(u_lat, lat_dirs[0] + lat_dirs[1], w_pool[l], pool_scale[l],
                              ssm_d[l], w_glu[l], w_out[l])
        x = x + mx[2][:, None, :] * rms_norm(mix_lat, g_post_mix[l])

        f_lat = conv_glu_ffn(modulate(rms_norm(x, g_pre_ffn[l]), mx[3], mx[4]),
                             w_up[l], w_conv[l], w_down[l], rows)
        x = x + mx[5][:, None, :] * rms_norm(f_lat, g_post_ffn[l])

        if not last:
            mix_ctx = mix_project(u_ctx, ctx_dirs[0] + ctx_dirs[1], w_pool[l], pool_scale[l],
                                  ssm_d[l], w_glu[l], w_out[l])
            ctx = ctx + mc[2][:, None, :] * rms_norm(mix_ctx, g_post_mix[l])
            f_ctx = conv_glu_ffn(modulate(rms_norm(ctx, g_pre_ffn[l]), mc[3], mc[4]),
                                 w_up[l], w_conv[l], w_down[l], None)
            ctx = ctx + mc[5][:, None, :] * rms_norm(f_ctx, g_post_ffn[l])
    return x
```

```python
import contextlib
import numpy as np
import concourse.bass as bass
import concourse.mybir as mybir

F32 = mybir.dt.float32
BF16 = mybir.dt.bfloat16
AF = mybir.ActivationFunctionType
ALU = mybir.AluOpType

RING = 10


class Prog:
    def __init__(self, nc):
        self.nc = nc
        self.ops = []

    def add(self, eng, fn, R=(), W=(), dma=False):
        self.ops.append(dict(eng=eng, fn=fn, R=tuple(R), W=tuple(W), dma=dma, barrier=False))

    def barrier(self):
        self.ops.append(dict(eng=None, fn=None, R=(), W=(), dma=False, barrier=True))

    def dma(self, q, out, in_, R=(), W=(), **kw):
        self.add(q, lambda e: e.dma_start(out=out, in_=in_, **kw), R, W, dma=True)

    def mm(self, mms, R=(), W=()):
        def fn(e):
            ins = None
            for (o, l, r, st, sp) in mms:
                ins = e.matmul(o, l, r, start=st, stop=sp)
            return ins
        self.add('pe', fn, R, W)

    def act(self, out, in_, func, R=(), W=(), **kw):
        self.add('act', lambda e: e.activation(out, in_, func, **kw), R, W)

    def tt(self, eng, out, in0, in1, op, R=(), W=()):
        self.add(eng, lambda e: e.tensor_tensor(out, in0, in1, op), R, W)

    def ts(self, eng, out, in0, s1, s2, op0, op1=None, R=(), W=()):
        if op1 is None:
            self.add(eng, lambda e: e.tensor_scalar(out, in0, s1, None, op0), R, W)
        else:
            self.add(eng, lambda e: e.tensor_scalar(out, in0, s1, s2, op0, op1), R, W)

    def stt(self, out, in0, scalar, in1, op0, op1, R=(), W=()):
        self.add('dve', lambda e: e.scalar_tensor_tensor(out, in0, scalar, in1, op0, op1), R, W)

    def copy(self, eng, out, in_, R=(), W=()):
        if eng == 'act':
            self.add('act', lambda e: e.activation(out, in_, AF.Copy), R, W)
        else:
            self.add(eng, lambda e: e.tensor_copy(out, in_), R, W)

    def memset(self, eng, ap, val, W=()):
        self.add(eng, lambda e: e.memset(ap, val), (), W)

    def emit(self, stack):
        nc = self.nc
        ops = self.ops
        n = len(ops)
        last_w = {}
        readers = {}
        deps = [None] * n
        engs = ['pe', 'act', 'dve', 'pool', 'sp']
        last_op = {e: None for e in engs}
        last_dmas = {e: [] for e in engs}
        need_bar = {e: set() for e in engs}
        for i, op in enumerate(ops):
            if op['barrier']:
                bd = set()
                for e in engs:
                    if last_op[e] is not None:
                        bd.add(last_op[e])
                    bd.update(last_dmas[e])
                for e in engs:
                    need_bar[e] |= bd
                deps[i] = set()
                continue
            d = set(need_bar[op['eng']])
            need_bar[op['eng']] = set()
            last_op[op['eng']] = i
            if op['dma']:
                last_dmas[op['eng']] = (last_dmas[op['eng']] + [i])[-RING:]
            for k in op['R']:
                if k in last_w:
                    d.add(last_w[k])
            for k in op['W']:
                if k in last_w:
                    d.add(last_w[k])
                for r in readers.get(k, ()):
                    d.add(r)
            d.discard(i)
            for k in op['W']:
                last_w[k] = i
                readers[k] = []
            for k in op['R']:
                readers.setdefault(k, []).append(i)
            deps[i] = d
        signal = [False] * n
        for i, op in enumerate(ops):
            if op['barrier']:
                continue
            nd = set()
            for j in deps[i]:
                if ops[j]['eng'] == 'pe' and op['eng'] == 'pe' and not ops[j]['dma'] and not op['dma']:
                    continue
                nd.add(j)
            deps[i] = nd
            for j in nd:
                signal[j] = True
        SEG = 30000
        cnt = {e: 0 for e in engs}
        dcnt = {e: 0 for e in engs}
        tok = [None] * n
        sems = {}

        def getsem(key):
            if key not in sems:
                sems[key] = stack.enter_context(nc.semaphore(name="s_%s" % "_".join(str(x) for x in key)))
            return sems[key]

        for i, op in enumerate(ops):
            e = op['eng']
            if op['barrier']:
                continue
            if op['dma']:
                k = dcnt[e]
                dcnt[e] += 1
                op['dslot'] = k
                tok[i] = (('d', e, k % RING), 16 * (k // RING + 1))
            elif signal[i]:
                c = cnt[e]
                cnt[e] += 1
                tok[i] = (('c', e, c // SEG), c % SEG + 1)
        for i in range(n):
            if tok[i] is not None:
                getsem(tok[i][0])
        by_eng = {e: [] for e in engs}
        for i, op in enumerate(ops):
            if not op['barrier']:
                by_eng[op['eng']].append(i)
        engobj = {}
        block = stack.enter_context(nc.Block())

        def run(e, eng):
            waited = {}
            def wait(t):
                key, val = t
                if waited.get(key, 0) >= val:
                    return
                waited[key] = val
                eng.wait_ge(getsem(key), val)
            for i in by_eng[e]:
                op = ops[i]
                for j in sorted(deps[i]):
                    wait(tok[j])
                if op['dma']:
                    k = op['dslot']
                    if k >= RING:
                        wait((('d', e, k % RING), 16 * (k // RING)))
                ins = op['fn'](eng)
                if tok[i] is not None:
                    key, val = tok[i]
                    assert ins is not None, "op returned no instruction"
                    ins.then_inc(getsem(key), 16 if op['dma'] else 1)
            k = dcnt[e]
            for s in range(RING):
                m = (k - s + RING - 1) // RING if k > s else 0
                if m > 0:
                    wait((('d', e, s), 16 * m))

        @block.tensor
        def _(eng):
            run('pe', eng)

        @block.scalar
        def _(eng):
            run('act', eng)

        @block.vector
        def _(eng):
            run('dve', eng)

        @block.gpsimd
        def _(eng):
            run('pool', eng)

        @block.sync
        def _(eng):
            run('sp', eng)
        return dict(n_ops=n, cnt=cnt, dcnt=dcnt, nsems=len(sems))
from concourse.bass_utils import run_bass_kernel_spmd

import math


class Arena:
    def __init__(self, ap32, nwords):
        self.ap = ap32
        self.n = nwords
        self.off = 0
        self.peak = 0

    def mark(self):
        return self.off

    def release(self, m):
        self.off = m

    def _take(self, nw):
        o = self.off
        self.off += nw
        self.peak = max(self.peak, self.off)
        assert self.off <= self.n, "arena overflow %d > %d" % (self.off, self.n)
        return o

    def f32(self, n):
        o = self._take(n)
        return self.ap[:, o:o + n]

    def bf16(self, n):
        nw = (n + 1) // 2
        o = self._take(nw)
        return self.ap[:, o:o + nw].bitcast(BF16)[:, 0:n]


def blocks(c0, c1, step):
    return [(a, min(step, c1 - a)) for a in range(c0, c1, step)]


def build(cfg, dbg=False):
    D, L, CTX, DFF, DEPTH, GW = cfg
    nc = bass.Bass("TRN2", target_bir_lowering=False)
    KT = D // 128
    W = CTX + L
    NQ = DFF // 128
    NJ = 2 * NQ
    SSMW = D // 4
    POOLW = D - SSMW
    UT = SSMW // 128
    PG = POOLW // 4
    PGT = PG // 128
    NPAIR = SSMW // 32
    NG = SSMW // 16
    CCH = CTX // 8
    NC1 = W // 8
    NCH = NC1 + CCH
    ROWS = L // GW
    RG = min(RG_MAX[0], ROWS)
    NGRP = ROWS // RG
    EPS = 1e-6
    NB = 2 if NPAIR >= 2 else 1
    LV = 0
    while LV < 5 and NC1 % (2 << LV) == 0:
        LV += 1
    PB = NPAIR // NB

    def din(name, shape, dt=F32):
        return nc.dram_tensor(name, list(shape), dt, kind="ExternalInput").ap()

    def dscr(name, shape, dt):
        return nc.dram_tensor(name, list(shape), dt, kind=("ExternalOutput" if dbg else "Internal")).ap()

    xT = din("xT", [D, L]); ctxT = din("ctxT", [D, CTX]); cc = din("cc", [128, KT * 2])
    w_ada = din("w_ada", [DEPTH, D, 6 * D]); b_adaT = din("b_adaT", [DEPTH, 128, 6 * KT])
    w_in = din("w_in", [DEPTH, D, D]); w_pool = din("w_pool", [DEPTH, 4, PG, PG]); pscT = din("pscT", [DEPTH, 128, POOLW // 128])
    a_reT = din("a_reT", [DEPTH, 128, 2 * NPAIR]); a_imT = din("a_imT", [DEPTH, 128, 2 * NPAIR]); ldtT = din("ldtT", [DEPTH, 128, 2 * NPAIR])
    bT_re = din("bT_re", [DEPTH, 128, 2 * NPAIR * 16]); bT_im = din("bT_im", [DEPTH, 128, 2 * NPAIR * 16])
    cT_re = din("cT_re", [DEPTH, 128, 2 * NPAIR * 16]); cT_im = din("cT_im", [DEPTH, 128, 2 * NPAIR * 16])
    ssm_dT = din("ssm_dT", [DEPTH, 128, UT]); w_glu = din("w_glu", [DEPTH, SSMW, SSMW]); w_out = din("w_out", [DEPTH, D, D])
    gT = din("gT", [128, DEPTH * 4 * KT])
    w_up = din("w_up", [DEPTH, D, 2 * DFF]); w_convT = din("w_convT", [DEPTH, 128, NJ * 9]); w_down = din("w_down", [DEPTH, DFF, D])
    ident_d = din("ident", [128, 128]); esel_d = din("esel", [128, 8 * 8 * 128]); etsel_d = din("etsel", [128, 8 * 8 * 128])
    maskf_d = din("maskf", [128, 128]); maskb_d = din("maskb", [128, 128]); ietab_d = din("ietab", [128, 4 * 16])
    out = nc.dram_tensor("out", [D, L], F32, kind="ExternalOutput").ap()

    Rr = dscr("Rr", [D, W], F32)
    U = dscr("U", [D, W], F32)
    MIXIN = dscr("MIXIN", [D, W], BF16)
    H2 = dscr("H2", [D, W], BF16)
    ACTS = dscr("ACTS", [DFF, W], BF16)
    WUs = nc.dram_tensor("WUs", [NQ, 128, 2 * KT * 128], BF16, kind="Internal").ap()
    WDs = nc.dram_tensor("WDs", [KT, 128, NQ * 128], BF16, kind="Internal").ap()
    YD = nc.dram_tensor("YD", [128, NG * NC1], BF16, kind="Internal").ap()
    MODd = dscr("MODd", [128, DEPTH * 6 * KT * 2], F32)

    st = contextlib.ExitStack()
    with st:
        def T(name, shape, dt):
            return st.enter_context(nc.sbuf_tensor(name, list(shape), dt))
        AW = AW_OVR[0]
        arena_t = T("arena", [128, AW], F32)
        ar = Arena(arena_t[:], AW)
        mod = T("mod", [128, DEPTH * 6 * KT * 2], F32)
        modv = mod[:].rearrange("p (l n s) -> p l n s", l=DEPTH, s=2)
        A1 = T("A1", [128, DEPTH * KT * 2], F32); G1 = T("G1", [128, DEPTH * KT * 2], F32)
        A2 = T("A2", [128, DEPTH * KT * 2], F32); G2 = T("G2", [128, DEPTH * KT * 2], F32)
        v4 = lambda t: t[:].rearrange("p (l k s) -> p l k s", l=DEPTH, s=2)
        A1v, G1v, A2v, G2v = v4(A1), v4(G1), v4(A2), v4(G2)
        gsb = T("gsb", [128, DEPTH * 4 * KT], F32)
        gv = gsb[:].rearrange("p (l w k) -> p l w k", l=DEPTH, w=4)
        ones_bf = T("ones_bf", [128, 128], BF16)
        ident_f = T("ident_f", [128, 128], F32)
        ident_b = T("ident_b", [128, 128], BF16)
        wconv = T("wconv", [128, NJ * 9], F32)
        wconvv = wconv[:].rearrange("p (j t) -> p j t", t=9)
        ZPW = (RG + 2) * (GW + 2)
        zp = [[T("zp%d_%d" % (g, v), [128, ZPW], BF16) for v in range(2)] for g in range(NGRP)]
        zpc = [T("zpc%d" % v, [128, CTX + 2], BF16) for v in range(2)]
        NPS = 7
        psb = [st.enter_context(nc.psum_tensor("ps%d" % i, [128, 512], F32)) for i in range(NPS)]
        pss = st.enter_context(nc.psum_tensor("pss", [128, 512], F32))
        P = Prog(nc)
        psi = [0]

        def nps():
            i = psi[0] % NPS
            psi[0] += 1
            return psb[i], ('ps', i)

        def cdma(out_, in_, R=(), W=()):
            P.dma('pool', out_, in_, R=R, W=W, max_dma_last_dim=4096)

        tb512 = [(a, n, 1) for a, n in blocks(0, CTX, 512)] + [(a, n, 0) for a, n in blocks(CTX, W, 512)]
        tb256 = [(a, n, 1) for a, n in blocks(0, CTX, 256)] + [(a, n, 0) for a, n in blocks(CTX, W, 256)]

        P.memset('dve', ones_bf[:], 1.0, W=['ones'])
        P.dma('sp', ident_f[:], ident_d, W=['ident_f'])
        cdma(ident_b[:], ident_d, W=['ident_b'])
        P.dma('sp', gsb[:], gT, W=['gsb'])
        for g in range(NGRP):
            for v in range(2):
                P.memset('pool', zp[g][v][:], 0.0, W=[('zp', g, v)])
        for v in range(2):
            P.memset('pool', zpc[v][:], 0.0, W=[('zpc', v)])
        P.dma('sp', Rr[:, 0:CTX], ctxT, W=['Rr'])
        for (a, n) in blocks(0, L, 1024):
            P.dma('sp', Rr[:, CTX + a:CTX + a + n], xT[:, a:a + n], W=['Rr'])

        def phase0():
            P.barrier()
            ar.release(0)
            ccs = ar.f32(KT * 2)
            scb = ar.bf16(KT * 2)
            scbv = scb.rearrange("p (k s) -> p k s", s=2)
            P.dma('sp', ccs, cc, W=['ccs'])
            P.act(scb, ccs, AF.Silu, R=['ccs'], W=['scb'])
            bada = ar.f32(6 * KT)
            wa = [ar.bf16(KT * 512) for _ in range(2)]
            NT6 = 6 * KT
            cnt = 0
            for l in range(DEPTH):
                P.dma('sp', bada, b_adaT[l], W=['bada'])
                pm, pmk = nps()
                for nb in range(NT6 // 4):
                    wt = wa[cnt % 2]
                    wk = ('wa', cnt % 2)
                    cnt += 1
                    wtv = wt.rearrange("p (k n) -> p k n", k=KT)
                    cdma(wtv, w_ada[l][:, nb * 512:(nb + 1) * 512].rearrange("(k p) n -> p k n", p=128), W=[wk])
                    for nt in range(4):
                        n_ = nb * 4 + nt
                        P.mm([(pm[:, 2 * n_:2 * n_ + 2], wtv[:, kt, nt * 128:(nt + 1) * 128], scbv[:, kt, :], kt == 0, kt == KT - 1)
                              for kt in range(KT)], R=[wk, 'scb'], W=[pmk])
                P.tt('dve', modv[:, l], pm[:, 0:2 * NT6].rearrange("p (n s) -> p n s", s=2),
                     bada.unsqueeze(2).to_broadcast([128, NT6, 2]), ALU.add, R=[pmk, 'bada'], W=['mod'])
                gb = lambda w_: gv[:, l, w_, :].unsqueeze(2).to_broadcast([128, KT, 2])
                P.stt(A1v[:, l], modv[:, l, KT:2 * KT, :], 1.0, gb(0), ALU.add, ALU.mult, R=['mod', 'gsb'], W=['A1'])
                P.tt('dve', G1v[:, l], modv[:, l, 2 * KT:3 * KT, :], gb(1), ALU.mult, R=['mod', 'gsb'], W=['G1'])
                P.stt(A2v[:, l], modv[:, l, 4 * KT:5 * KT, :], 1.0, gb(2), ALU.add, ALU.mult, R=['mod', 'gsb'], W=['A2'])
                P.tt('dve', G2v[:, l], modv[:, l, 5 * KT:6 * KT, :], gb(3), ALU.mult, R=['mod', 'gsb'], W=['G2'])
            if dbg:
                P.dma('sp', MODd, mod[:], R=['mod'], W=['MODd'])

        def rms_stats(src3, n, sq, rstd, Rk, Wk, sqk='sq'):
            sqv = sq[:, 0:KT * n].rearrange("p (k n) -> p k n", k=KT)
            P.act(sqv, src3, AF.Square, R=[Rk], W=[sqk])
            ps, pk = nps()
            P.mm([(ps[:, 0:n], ones_bf[:], sqv[:, kt, :], kt == 0, kt == KT - 1) for kt in range(KT)], R=[sqk, 'ones'], W=[pk])
            P.act(rstd[:, 0:n], ps[:, 0:n], AF.Sqrt, bias=EPS, scale=1.0 / D, R=[pk], W=[Wk])
            P.add('dve', lambda e: e.reciprocal(rstd[:, 0:n], rstd[:, 0:n]), R=[Wk], W=[Wk])

        def interleave(a_th, b_th):
            na, nb = len(a_th), len(b_th)
            bi_ = 0
            for k_, th in enumerate(a_th):
                th()
                upto = ((k_ + 1) * nb) // max(na, 1)
                while bi_ < upto:
                    b_th[bi_]()
                    bi_ += 1
            while bi_ < nb:
                b_th[bi_]()
                bi_ += 1

        def phase1(l):
            P.barrier()
            ar.release(0)
            win = ar.bf16(KT * D)
            winv = win.rearrange("p (k n) -> p k n", k=KT)
            for (a, n) in blocks(0, D, 512):
                cdma(winv[:, :, a:a + n], w_in[l][:, a:a + n].rearrange("(k p) n -> p k n", p=128), W=['win'])
            NE = 256
            sets = [(ar.f32(KT * NE), ar.bf16(KT * NE), ar.bf16(KT * NE), ar.f32(NE)) for _ in range(2)]
            def ctx1(bi):
                (c0, n, s_) = tb256[bi]
                b = bi % 2
                xs, sq, hb, rstd = sets[b]
                kx, kq, kh, kr = ('xs', b), ('sq', b), ('hb', b), ('rstd', b)
                xsv = xs[:, 0:KT * n].rearrange("p (k n) -> p k n", k=KT)
                hbv = hb[:, 0:KT * n].rearrange("p (k n) -> p k n", k=KT)
                return c0, n, s_, xs, sq, hb, rstd, kx, kq, kh, kr, xsv, hbv

            def prologue(bi):
                c0, n, s_, xs, sq, hb, rstd, kx, kq, kh, kr, xsv, hbv = ctx1(bi)
                th = []
                th.append(lambda: P.dma('sp', xsv, Rr[:, c0:c0 + n].rearrange("(k p) n -> p k n", p=128), R=['Rr'], W=[kx]))
                th.append(lambda: rms_stats(xsv, n, sq, rstd, kx, kr, kq))
                th.append(lambda: P.tt('dve', xsv, xsv, rstd[:, 0:n].unsqueeze(1).to_broadcast([128, KT, n]), ALU.mult, R=[kx, kr], W=[kx]))
                for kt in range(KT):
                    th.append(lambda kt=kt: P.act(hbv[:, kt, :], xsv[:, kt, :], AF.Identity, scale=A1v[:, l, kt, s_:s_ + 1],
                                                  bias=modv[:, l, kt, s_:s_ + 1], R=[kx, 'A1', 'mod'], W=[kh]))
                return th

            def main(bi):
                c0, n, s_, xs, sq, hb, rstd, kx, kq, kh, kr, xsv, hbv = ctx1(bi)
                th = []
                for mt in range(KT):
                    def tm(mt=mt):
                        ps, pk = nps()
                        P.mm([(ps[:, 0:n], winv[:, kt, mt * 128:(mt + 1) * 128], hbv[:, kt, :], kt == 0, kt == KT - 1) for kt in range(KT)],
                             R=['win', kh], W=[pk])
                        P.copy('act' if mt % 2 else 'dve', xsv[:, mt, :], ps[:, 0:n], R=[pk], W=[kx])
                    th.append(tm)
                th.append(lambda: P.dma('pool', U[:, c0:c0 + n].rearrange("(k p) n -> p k n", p=128), xsv, R=[kx], W=['U']))
                return th
            for th in prologue(0):
                th()
            for bi in range(len(tb256)):
                interleave(main(bi), prologue(bi + 1) if bi + 1 < len(tb256) else [])

        def phase2(l):
            P.barrier()
            ar.release(0)
            PAD = 16
            oA = PAD
            oB = PAD + CTX + 2 * PAD
            WP = CTX + L + 4 * PAD
            segs = [(oA, 0, CTX), (oB, CTX, L)]
            wp = ar.bf16(PGT * PG); wpv = wp.rearrange("p (k n) -> p k n", k=PGT)
            pb = ar.bf16(PGT * W); pbv = pb.rearrange("p (k n) -> p k n", k=PGT)
            psets = [(ar.f32(WP), ar.f32(WP), ar.f32(WP)) for _ in range(2)]
            pob = ar.bf16(W)
            iet = ar.f32(64); ietv = iet.rearrange("p (w e) -> p w e", w=4)
            psc = ar.f32(POOLW // 128)
            t8 = ar.f32(8)
            P.dma('sp', iet, ietab_d, W=['iet'])
            P.dma('sp', psc, pscT[l], W=['psc'])
            for b_ in range(2):
                P.memset('dve', psets[b_][0], 0.0, W=[('up', b_)])
            for g in range(4):
                w = (2, 4, 8, 16)[g]
                cdma(wpv, w_pool[l][g].rearrange("(k p) n -> p k n", p=128), W=['wp'])
                for k3 in range(PGT):
                    ct = g * PGT + k3
                    up, ta, tb_ = psets[ct % 2]
                    ku, ka, kb = ('up', ct % 2), ('ta', ct % 2), ('tb', ct % 2)
                    for (o, c0, n) in segs:
                        P.dma('sp', up[:, o:o + n], U[ct * 128:(ct + 1) * 128, c0:c0 + n], R=['U'], W=[ku])
                    eng = 'dve' if ct % 2 == 0 else 'pool'
                    P.tt(eng, ta[:, 1:WP], up[:, 1:WP], up[:, 0:WP - 1], ALU.add, R=[ku], W=[ka])
                    if w == 2:
                        S = ta; Sk = ka
                    elif w == 4:
                        P.tt(eng, tb_[:, 2:WP - 1], ta[:, 1:WP - 2], ta[:, 3:WP], ALU.add, R=[ka], W=[kb])
                        S = tb_; Sk = kb
                    elif w == 8:
                        P.tt(eng, tb_[:, 3:WP], ta[:, 3:WP], ta[:, 1:WP - 2], ALU.add, R=[ka], W=[kb])
                        P.tt(eng, ta[:, 4:WP - 3], tb_[:, 3:WP - 4], tb_[:, 7:WP], ALU.add, R=[kb], W=[ka])
                        S = ta; Sk = ka
                    else:
                        P.tt(eng, tb_[:, 3:WP], ta[:, 3:WP], ta[:, 1:WP - 2], ALU.add, R=[ka], W=[kb])
                        P.tt(eng, ta[:, 7:WP], tb_[:, 7:WP], tb_[:, 3:WP - 4], ALU.add, R=[kb], W=[ka])
                        P.tt(eng, tb_[:, 8:WP - 7], ta[:, 7:WP - 8], ta[:, 15:WP], ALU.add, R=[ka], W=[kb])
                        S = tb_; Sk = kb
                    for (o, c0, n) in segs:
                        P.stt(pbv[:, k3, c0:c0 + n], S[:, o:o + n], 1.0 / w, up[:, o:o + n], ALU.mult, ALU.subtract,
                              R=[Sk, ku], W=['pb'])
                        for (eo, to) in ((0, 0), (n - 8, 8)):
                            P.tt('dve', t8, S[:, o + eo:o + eo + 8], ietv[:, g, to:to + 8], ALU.mult, R=[Sk, 'iet'], W=['t8'])
                            P.tt('dve', pbv[:, k3, c0 + eo:c0 + eo + 8], t8, up[:, o + eo:o + eo + 8], ALU.subtract,
                                 R=['t8', ku], W=['pb'])
                for m3 in range(PGT):
                    for (c0, n, s) in tb512:
                        ps, pk = nps()
                        P.mm([(ps[:, 0:n], wpv[:, k3, m3 * 128:(m3 + 1) * 128], pbv[:, k3, c0:c0 + n], k3 == 0, k3 == PGT - 1)
                              for k3 in range(PGT)], R=['wp', 'pb'], W=[pk])
                        P.act(pob[:, c0:c0 + n], ps[:, 0:n], AF.Identity, scale=psc[:, g * PGT + m3:g * PGT + m3 + 1],
                              R=[pk, 'psc'], W=['pob'])
                    r0 = (g * PGT + m3) * 128
                    P.dma('sp', MIXIN[r0:r0 + 128, :], pob, R=['pob'], W=['MIXIN'])

        def phase3(l):
            P.barrier()
            ar.release(0)
            N2 = 2 * NPAIR
            CZ = ar.bf16(2 * 2 * NG * 128); CZv = CZ.rearrange("p (d r g m) -> p d r g m", d=2, r=2, g=NG)
            BJ = ar.bf16(2 * 2 * NG * 64); BJv = BJ.rearrange("p (d r g m) -> p d r g m", d=2, r=2, g=NG)
            Mg = ar.bf16(NG * 128); Mgv = Mg.rearrange("p (g m) -> p g m", g=NG)
            L1 = [[ar.f32(2 * NPAIR) for _ in range(LV + 1)] for _ in range(2)]; L2 = [[ar.f32(2 * NPAIR) for _ in range(LV + 1)] for _ in range(2)]
            pm = ar.f32(2)
            mk0 = ar.mark()

            def sm(n=N2):
                return ar.f32(n)
            are = sm(); aim = sm(); ldt = sm(); dt = sm(); th = sm(); mag = sm()
            c_ = sm(); s_ = sm(); t1 = sm(); t2 = sm(); t3 = sm(); lre = sm(); lim = sm(); fre = sm(); fim = sm()
            P.dma('sp', are, a_reT[l], W=['are']); P.dma('sp', aim, a_imT[l], W=['aim']); P.dma('sp', ldt, ldtT[l], W=['ldt'])
            K = 'gen'

            def tt(o, a, b, op, eng='dve'):
                P.tt(eng, o, a, b, op, R=[K], W=[K])
            P.act(dt, ldt, AF.Exp, R=['ldt'], W=[K])
            P.tt('dve', th, aim, dt, ALU.mult, R=['aim', K], W=[K])
            P.tt('dve', t1, are, dt, ALU.mult, R=['are', K], W=[K])
            P.act(mag, t1, AF.Exp, R=[K], W=[K])
            hp = ar.f32(1)
            P.memset('dve', hp, math.pi / 2, W=[K])
            P.act(c_, th, AF.Sin, scale=1.0 / 16, bias=hp[:, 0:1], R=[K], W=[K])
            P.act(s_, th, AF.Sin, scale=1.0 / 16, R=[K], W=[K])
            for _ in range(4):
                tt(t1, c_, c_, ALU.mult); tt(t2, s_, s_, ALU.mult); tt(t3, c_, s_, ALU.mult)
                tt(c_, t1, t2, ALU.subtract)
                P.ts('dve', s_, t3, 2.0, None, ALU.mult, R=[K], W=[K])
            tt(lre, mag, c_, ALU.mult); tt(lim, mag, s_, ALU.mult)
            if sub[0] <= 0.1:
                return
            nr = sm(); den = sm()
            P.ts('dve', nr, lre, -1.0, None, ALU.add, R=[K], W=[K])
            tt(t1, are, are, ALU.mult); tt(t2, aim, aim, ALU.mult); tt(den, t1, t2, ALU.add)
            P.add('dve', lambda e: e.reciprocal(den, den), R=[K], W=[K])
            tt(t1, nr, are, ALU.mult); tt(t2, lim, aim, ALU.mult); tt(t1, t1, t2, ALU.add); tt(fre, t1, den, ALU.mult)
            tt(t1, lim, are, ALU.mult); tt(t2, nr, aim, ALU.mult); tt(t1, t1, t2, ALU.subtract); tt(fim, t1, den, ALU.mult)

            def cmul(ore, oim, ar_, ai_, br_, bi_):
                tt(t1, ar_, br_, ALU.mult); tt(t2, ai_, bi_, ALU.mult); tt(ore, t1, t2, ALU.subtract)
                tt(t1, ar_, bi_, ALU.mult); tt(t2, ai_, br_, ALU.mult); tt(oim, t1, t2, ALU.add)
            Zr = [sm() for _ in range(9)]; Zi = [sm() for _ in range(9)]
            ZNr = [sm() for _ in range(8)]; ZNi = [sm() for _ in range(8)]
            P.memset('dve', Zr[0], 1.0, W=[K]); P.memset('dve', Zi[0], 0.0, W=[K])
            P.memset('dve', ZNr[0], 1.0, W=[K]); P.memset('dve', ZNi[0], 0.0, W=[K])
            for e_ in range(1, 9):
                cmul(Zr[e_], Zi[e_], Zr[e_ - 1], Zi[e_ - 1], lre, lim)
            lir = sm(); lii = sm()
            tt(t1, lre, lre, ALU.mult); tt(t2, lim, lim, ALU.mult); tt(t3, t1, t2, ALU.add)
            P.add('dve', lambda e: e.reciprocal(t3, t3), R=[K], W=[K])
            tt(lir, lre, t3, ALU.mult)
            P.stt(lii, lim, -1.0, t3, ALU.mult, ALU.mult, R=[K], W=[K])
            for e_ in range(1, 8):
                cmul(ZNr[e_], ZNi[e_], ZNr[e_ - 1], ZNi[e_ - 1], lir, lii)
            ZFr = [sm() for _ in range(8)]; ZFi = [sm() for _ in range(8)]
            ZNFr = [sm() for _ in range(8)]; ZNFi = [sm() for _ in range(8)]
            for e_ in range(8):
                cmul(ZFr[e_], ZFi[e_], Zr[e_], Zi[e_], fre, fim)
                cmul(ZNFr[e_], ZNFi[e_], ZNr[e_], ZNi[e_], fre, fim)
            LPr = [Zr[8]] + [sm() for _ in range(LV)]; LPi = [Zi[8]] + [sm() for _ in range(LV)]
            for k_ in range(1, LV + 1):
                cmul(LPr[k_], LPi[k_], LPr[k_ - 1], LPi[k_ - 1], LPr[k_ - 1], LPi[k_ - 1])
            for d in range(2):
                sl = slice(d * NPAIR, (d + 1) * NPAIR)
                for k_ in range(LV + 1):
                    P.copy('dve', L1[d][k_][:, 0:NPAIR], LPr[k_][:, sl], R=[K], W=[K])
                    P.copy('dve', L1[d][k_][:, NPAIR:], LPr[k_][:, sl], R=[K], W=[K])
                    P.ts('dve', L2[d][k_][:, 0:NPAIR], LPi[k_][:, sl], -1.0, None, ALU.mult, R=[K], W=[K])
                    P.copy('dve', L2[d][k_][:, NPAIR:], LPi[k_][:, sl], R=[K], W=[K])
            if sub[0] <= 0.2:
                return
            Bre = ar.f32(N2 * 16); Bim = ar.f32(N2 * 16); Cre = ar.f32(N2 * 16); Cim = ar.f32(N2 * 16)
            P.dma('sp', Bre, bT_re[l], W=[K]); P.dma('sp', Bim, bT_im[l], W=[K])
            P.dma('sp', Cre, cT_re[l], W=[K]); P.dma('sp', Cim, cT_im[l], W=[K])
            b3 = lambda t_, d: t_.rearrange("p (d k h) -> p d k h", d=2, h=16)[:, d]
            MS = NPAIR * 128

            def mat():
                return ar.bf16(MS)
            mv = lambda m_: m_.rearrange("p (k j h) -> p k j h", k=NPAIR, j=8)
            o1 = ar.f32(NPAIR * 16); o2 = ar.f32(NPAIR * 16)
            o1v = o1.rearrange("p (k h) -> p k h", h=16); o2v = o2.rearrange("p (k h) -> p k h", h=16)

            def outer(dst_re, dst_im, d, j, fr, fi, Xre, Xim, neg_im):
                sl = slice(d * NPAIR, (d + 1) * NPAIR)
                frb = fr[:, sl].unsqueeze(2).to_broadcast([128, NPAIR, 16])
                fib = fi[:, sl].unsqueeze(2).to_broadcast([128, NPAIR, 16])
                xr = b3(Xre, d); xi = b3(Xim, d)
                tt(o1v, xr, frb, ALU.mult); tt(o2v, xi, fib, ALU.mult)
                tt(mv(dst_re)[:, :, j, :], o1v, o2v, ALU.subtract)
                tt(o1v, xi, frb, ALU.mult); tt(o2v, xr, fib, ALU.mult)
                if neg_im:
                    P.stt(mv(dst_im)[:, :, j, :], o1v, -1.0, o2v, ALU.mult, ALU.subtract, R=[K], W=[K])
                else:
                    tt(mv(dst_im)[:, :, j, :], o1v, o2v, ALU.add)
            CJr = [mat() for _ in range(2)]; CJn = [mat() for _ in range(2)]
            P.memset('dve', pm, 0.0, W=[K])
            P.memset('dve', pm[0:64, 0:1], 1.0, W=[K])
            P.memset('dve', pm[64:128, 1:2], 1.0, W=[K])
            BTr = [mat() for _ in range(2)]; BTi = [mat() for _ in range(2)]
            BCr = mat(); BCi = mat()
            CCr = [mat() for _ in range(2)]; CCn = [mat() for _ in range(2)]
            mkf = ar.f32(128); mkb = ar.f32(128); tm1 = ar.f32(128); tm2 = ar.f32(128)
            P.dma('sp', mkf, maskf_d, W=['mkf']); P.dma('sp', mkb, maskb_d, W=['mkb'])
            for j in range(8):
                outer(BTr[0], BTi[0], 0, j, ZFr[7 - j], ZFi[7 - j], Bre, Bim, False)
                outer(BTr[1], BTi[1], 1, j, ZFr[j], ZFi[j], Bre, Bim, False)
                outer(BCr, BCi, 0, j, ZNFr[j], ZNFi[j], Bre, Bim, False)
                outer(CJr[0], CJn[0], 0, j, Zr[j + 1], Zi[j + 1], Cre, Cim, True)
                outer(CJr[1], CJn[1], 1, j, Zr[8 - j], Zi[8 - j], Cre, Cim, True)
                outer(CCr[0], CCn[0], 0, j, Zr[j], Zi[j], Cre, Cim, True)
                outer(CCr[1], CCn[1], 1, j, ZNr[j], ZNi[j], Cre, Cim, True)
            m3 = lambda m_: m_.rearrange("p (k x) -> p k x", k=NPAIR)
            for d in range(2):
                for r_, src in ((0, CJr[d]), (1, CJn[d])):
                    for g2 in range(2):
                        dst = CZv[:, d, r_].rearrange("p (k t) m -> p k t m", t=2)[:, :, g2, :]
                        P.ts('dve', dst, m3(src), pm[:, g2:g2 + 1], None, ALU.mult, R=[K], W=[K])
            if sub[0] <= 0.3:
                return
            for d in range(2):
                for r_, src in ((0, BTr[d]), (1, BTi[d])):
                    for g2 in range(2):
                        rs = slice(g2 * 64, g2 * 64 + 64)
                        for (k0, nk) in blocks(0, NPAIR, 8):
                            ps, pk = nps()
                            P.mm([(ps[:, ki * 64:(ki + 1) * 64], m3(src)[rs, k0 + ki, :], ident_b[rs, rs], True, True) for ki in range(nk)],
                                 R=[K, 'ident_b'], W=[pk])
                            dst = BJv[:, d, r_].rearrange("p (k t) m -> p k t m", t=2)[:, k0:k0 + nk, g2, :]
                            P.copy('act', dst, ps[:, 0:nk * 64].rearrange("p (a m) -> p a m", m=64), R=[pk], W=['BJ'])
            if sub[0] <= 0.4:
                return
            BCL = [(BCr, BCi), (BTr[1], BTi[1])]
            for g in range(NG):
                k, g2 = g // 2, g % 2
                rs = slice(g2 * 64, g2 * 64 + 64)
                pss = []
                for d in range(2):
                    ps, pk = nps()
                    P.mm([(ps[:, 0:128], m3(BCL[d][0])[rs, k, :], m3(CCr[d])[rs, k, :], True, False),
                          (ps[:, 0:128], m3(BCL[d][1])[rs, k, :], m3(CCn[d])[rs, k, :], False, True)], R=[K], W=[pk])
                    pss.append((ps, pk))
                P.tt('dve', tm1, pss[0][0][:, 0:128], mkf, ALU.mult, R=[pss[0][1], 'mkf'], W=['tm1'])
                P.tt('dve', tm2, pss[1][0][:, 0:128], mkb, ALU.mult, R=[pss[1][1], 'mkb'], W=['tm2'])
                P.tt('dve', Mgv[:, g, :], tm1, tm2, ALU.add, R=['tm1', 'tm2'], W=['Mg'])
            if sub[0] <= 1:
                return
            P.barrier()
            ar.release(mk0)
            U8 = ar.bf16(NG * NCH); U8v = U8.rearrange("p (g c) -> p g c", g=NG)
            mk1 = ar.mark()
            es = ar.bf16(64 * 128); esv = es.rearrange("p (g j m) -> p g j m", g=8, j=8)
            cdma(es, esel_d, W=['es'])
            usb = ar.bf16(W + CTX)
            usv = usb.rearrange("p (c j) -> p c j", j=8)
            for ut in range(UT):
                r0 = POOLW + ut * 128
                cdma(usb[:, 0:W], U[r0:r0 + 128, :], R=['U'], W=['usb'])
                cdma(usb[:, W:W + CTX], U[r0:r0 + 128, 0:CTX], R=['U'], W=['usb'])
                for g8 in range(8):
                    g = ut * 8 + g8
                    for (cb, n) in blocks(0, NCH, 512):
                        ps, pk = nps()
                        P.mm([(ps[:, 0:n], esv[:, g8, j, :], usv[:, cb:cb + n, j], j == 0, j == 7) for j in range(8)],
                             R=['es', 'usb'], W=[pk])
                        P.copy('act' if g8 % 2 else 'dve', U8v[:, g, cb:cb + n], ps[:, 0:n], R=[pk], W=['U8'])
            if sub[0] <= 2:
                return
            P.barrier()
            ar.release(mk1)
            YDv = YD.rearrange("p (g c) -> p g c", g=NG)
            yst = [ar.bf16(NC1) for _ in range(2)]
            yc = [0]

            def ystage():
                i = yc[0] % 2
                yc[0] += 1
                return yst[i], ('yst', i)
            if sub[0] <= 3:
                return
            S = ar.f32(2 * PB * NC1); Sv = S.rearrange("p (r k c) -> p r k c", r=2, k=PB)
            Hb = [ar.bf16(2 * PB * (NC1 + 1)) for _ in range(2)]
            Hv = [h_.rearrange("p (r k c) -> p r k c", r=2, k=PB) for h_ in Hb]
            CM = 68
            tA = ar.f32(2 * PB * CM); tB = ar.f32(2 * PB * CM)

            def tv(t_, cnt):
                return t_[:, 0:2 * PB * cnt].rearrange("p (r k c) -> p r k c", r=2, k=PB)

            def cma(d, dst_pos, src_pos, cnt, step, Lk, bt):
                L1v = L1[d][Lk].rearrange("p (r k) -> p r k", r=2)[:, :, bt * PB:(bt + 1) * PB]
                L2v = L2[d][Lk].rearrange("p (r k) -> p r k", r=2)[:, :, bt * PB:(bt + 1) * PB]
                for m0 in range(0, cnt, CM):
                    c_ = min(CM, cnt - m0)

                    def sl(p0):
                        a = p0 + m0 * step
                        if d == 0:
                            return slice(a, a + (c_ - 1) * step + 1, step)
                        a = NC1 - 1 - a
                        e = a - (c_ - 1) * step - 1
                        return slice(a, e if e >= 0 else None, -step)
                    src = Sv[:, :, :, sl(src_pos)]
                    dst = Sv[:, :, :, sl(dst_pos)]
                    b1 = L1v.unsqueeze(3).to_broadcast([128, 2, PB, c_])
                    b2 = L2v.unsqueeze(3).to_broadcast([128, 2, PB, c_])
                    P.tt('dve', tv(tA, c_), src, b1, ALU.mult, R=['S', K], W=['tA'])
                    P.tt('pool' if c_ > 8 else 'dve', tv(tB, c_), src[:, ::-1], b2, ALU.mult, R=['S', K], W=['tB'])
                    P.tt('dve', tv(tA, c_), tv(tA, c_), tv(tB, c_), ALU.add, R=['tA', 'tB'], W=['tA'])
                    P.tt('dve', dst, dst, tv(tA, c_), ALU.add, R=['S', 'tA'], W=['S'])
            for bt in range(NB):
                for d in range(2):
                    cs = 0 if d == 0 else CCH
                    for kl in range(PB):
                        k = bt * PB + kl
                        for r_ in range(2):
                            for (cb, n) in blocks(0, NC1, 512):
                                ps, pk = nps()
                                P.mm([(ps[g2 * 64:(g2 + 1) * 64, 0:n], BJv[:, d, r_, 2 * k + g2, :], U8v[:, 2 * k + g2, cs + cb:cs + cb + n], True, True)
                                      for g2 in range(2)], R=['BJ', 'U8'], W=[pk])
                                P.copy('act' if r_ else 'dve', Sv[:, r_, kl, cb:cb + n], ps[:, 0:n], R=[pk], W=['S'])
                    for lv in range(LV):
                        s_ = 1 << lv
                        cnt = NC1 // (2 * s_)
                        cma(d, 2 * s_ - 1, s_ - 1, cnt, 2 * s_, lv, bt)
                    TT = 1 << LV
                    for m in range(1, NC1 // TT):
                        cma(d, TT * m + TT - 1, TT * m - 1, 1, TT, LV, bt)
                    for lv in range(LV - 1, -1, -1):
                        s_ = 1 << lv
                        cnt = NC1 // (2 * s_) - 1
                        if cnt > 0:
                            cma(d, 3 * s_ - 1, 2 * s_ - 1, cnt, 2 * s_, lv, bt)
                    if d == 0:
                        P.memset('pool', Hv[0][:, :, :, 0:1], 0.0, W=[('Hb', 0)])
                        P.copy('pool', Hv[0][:, :, :, 1:NC1 + 1], Sv, R=['S'], W=[('Hb', 0)])
                    else:
                        P.memset('pool', Hv[1][:, :, :, NC1:NC1 + 1], 0.0, W=[('Hb', 1)])
                        P.copy('pool', Hv[1][:, :, :, 0:NC1], Sv, R=['S'], W=[('Hb', 1)])
                for kl in range(PB):
                    k = bt * PB + kl
                    for g2 in range(2):
                        g = 2 * k + g2
                        ys, yk = ystage()
                        oblks = [(0, CCH)] + [(a, n) for a, n in blocks(CCH, NC1, 512)]
                        for (cb, n) in oblks:
                            hb0 = NC1 - CCH + 1 if cb < CCH else cb + 1 - CCH
                            ps, pk = nps()
                            P.mm([(ps[:, 0:n], Mgv[:, g, :], U8v[:, g, cb:cb + n], True, False),
                                  (ps[:, 0:n], CZv[:, 0, 0, g, :], Hv[0][:, 0, kl, cb:cb + n], False, False),
                                  (ps[:, 0:n], CZv[:, 0, 1, g, :], Hv[0][:, 1, kl, cb:cb + n], False, False),
                                  (ps[:, 0:n], CZv[:, 1, 0, g, :], Hv[1][:, 0, kl, hb0:hb0 + n], False, False),
                                  (ps[:, 0:n], CZv[:, 1, 1, g, :], Hv[1][:, 1, kl, hb0:hb0 + n], False, True)],
                                 R=[K, 'Mg', 'U8', ('Hb', 0), ('Hb', 1)], W=[pk])
                            P.copy('act', ys[:, cb:cb + n], ps[:, 0:n], R=[pk], W=[yk])
                        P.dma('sp', YDv[:, g, :], ys, R=[yk], W=['YD'])
            if sub[0] <= 4:
                return
            P.barrier()
            ar.release(0)
            Y8 = ar.bf16(NG * NC1); Y8v = Y8.rearrange("p (g c) -> p g c", g=NG)
            P.dma('sp', Y8v, YDv, R=['YD'], W=['Y8'])
            et = ar.bf16(64 * 128); etv = et.rearrange("p (g j m) -> p g j m", g=8, j=8)
            cdma(et, etsel_d, W=['et'])
            wgl = ar.bf16(UT * SSMW); wglv = wgl.rearrange("p (k n) -> p k n", k=UT)
            cdma(wglv, w_glu[l].rearrange("(k p) n -> p k n", p=128), W=['wgl'])
            sd = ar.f32(UT)
            P.dma('sp', sd, ssm_dT[l], W=['sd'])
            NE = 512
            u32 = ar.f32(UT * NE); yf = ar.f32(UT * NE); wv_ = ar.f32(UT * NE); sgm = ar.f32(UT * NE)
            geb = ar.bf16(UT * NE); sg2 = ar.f32(NE); sob = ar.bf16(UT * NE)
            for (c0, n, s) in tb512:
                r3 = lambda t_: t_[:, 0:UT * n].rearrange("p (k n) -> p k n", k=UT)
                cb0, ncq = c0 // 8, n // 8
                P.dma('sp', r3(u32), U[POOLW:D, c0:c0 + n].rearrange("(k p) n -> p k n", p=128), R=['U'], W=['u32'])
                for ut in range(UT):
                    ps, pk = nps()
                    psv = ps[:, 0:n].rearrange("p (c j) -> p c j", j=8)
                    mms = []
                    for j in range(8):
                        for g8 in range(8):
                            mms.append((psv[:, :, j], etv[:, g8, j, :], Y8v[:, ut * 8 + g8, cb0:cb0 + ncq], g8 == 0, g8 == 7))
                    P.mm(mms, R=['et', 'Y8'], W=[pk])
                    P.stt(r3(yf)[:, ut, :], r3(u32)[:, ut, :], sd[:, ut:ut + 1], ps[:, 0:n], ALU.mult, ALU.add,
                          R=['u32', 'sd', pk], W=['yf'])
                P.act(r3(wv_), r3(yf), AF.Square, R=['yf'], W=['wv'])
                P.ts('dve', r3(wv_), r3(wv_), 0.044715, 1.0, ALU.mult, ALU.add, R=['wv'], W=['wv'])
                P.tt('dve', r3(wv_), r3(wv_), r3(yf), ALU.mult, R=['wv', 'yf'], W=['wv'])
                P.act(r3(sgm), r3(wv_), AF.Sigmoid, scale=1.5957691216, R=['wv'], W=['sgm'])
                P.tt('dve', r3(yf), r3(yf), r3(sgm), ALU.mult, R=['yf', 'sgm'], W=['yf'])
                P.copy('pool', r3(geb), r3(yf), R=['yf'], W=['geb'])
                for uo in range(UT):
                    ps, pk = nps()
                    P.mm([(ps[:, 0:n], wglv[:, ui, uo * 128:(uo + 1) * 128], r3(geb)[:, ui, :], ui == 0, ui == UT - 1) for ui in range(UT)],
                         R=['wgl', 'geb'], W=[pk])
                    P.act(sg2[:, 0:n], ps[:, 0:n], AF.Sigmoid, R=[pk], W=['sg2'])
                    P.tt('dve', r3(sob)[:, uo, :], r3(yf)[:, uo, :], sg2[:, 0:n], ALU.mult, R=['yf', 'sg2'], W=['sob'])
                P.dma('sp', MIXIN[POOLW:D, c0:c0 + n].rearrange("(k p) n -> p k n", p=128), r3(sob), R=['sob'], W=['MIXIN'])

        def phase4(l):
            P.barrier()
            ar.release(0)
            wo = ar.bf16(KT * D); wov = wo.rearrange("p (k n) -> p k n", k=KT)
            for (a, n) in blocks(0, D, 512):
                cdma(wov[:, :, a:a + n], w_out[l][:, a:a + n].rearrange("(k p) n -> p k n", p=128), W=['wo'])
            NE = 256
            sets = [(ar.bf16(KT * NE), ar.f32(KT * NE), ar.f32(KT * NE)) for _ in range(2)]
            sq = ar.bf16(KT * NE); h2b = ar.bf16(KT * NE)
            rstd = ar.f32(NE)
            def ctx4(bi):
                (c0, n, s_) = tb256[bi]
                b = bi % 2
                mi, mix, xs = sets[b]
                r3 = lambda t_: t_[:, 0:KT * n].rearrange("p (k n) -> p k n", k=KT)
                return c0, n, s_, r3(mi), r3(mix), r3(xs), ('mi', b), ('mix', b), ('xs', b)

            def stageA(bi):
                c0, n, s_, miv, mixv, xsv, kmi, kmx, kxs = ctx4(bi)
                th = []

                def t0():
                    P.dma('sp', miv, MIXIN[:, c0:c0 + n].rearrange("(k p) n -> p k n", p=128), R=['MIXIN'], W=[kmi])
                    P.dma('sp', xsv, Rr[:, c0:c0 + n].rearrange("(k p) n -> p k n", p=128), R=['Rr'], W=[kxs])
                th.append(t0)
                for mt in range(KT):
                    def tm(mt=mt):
                        ps, pk = nps()
                        P.mm([(ps[:, 0:n], wov[:, kt, mt * 128:(mt + 1) * 128], miv[:, kt, :], kt == 0, kt == KT - 1) for kt in range(KT)],
                             R=['wo', kmi], W=[pk])
                        P.copy('act' if mt % 2 else 'dve', mixv[:, mt, :], ps[:, 0:n], R=[pk], W=[kmx])
                    th.append(tm)
                return th

            def stageB(bi):
                c0, n, s_, miv, mixv, xsv, kmi, kmx, kxs = ctx4(bi)
                r3 = lambda t_: t_[:, 0:KT * n].rearrange("p (k n) -> p k n", k=KT)
                rb = rstd[:, 0:n].unsqueeze(1).to_broadcast([128, KT, n])
                th = []
                th.append(lambda: rms_stats(mixv, n, sq, rstd, kmx, 'rstd'))
                th.append(lambda: P.tt('dve', mixv, mixv, rb, ALU.mult, R=[kmx, 'rstd'], W=[kmx]))
                for mt in range(KT):
                    th.append(lambda mt=mt: P.stt(xsv[:, mt, :], mixv[:, mt, :], G1v[:, l, mt, s_:s_ + 1], xsv[:, mt, :], ALU.mult, ALU.add,
                                                  R=[kmx, 'G1', kxs], W=[kxs]))
                th.append(lambda: P.dma('pool', Rr[:, c0:c0 + n].rearrange("(k p) n -> p k n", p=128), xsv, R=[kxs], W=['Rr']))
                th.append(lambda: rms_stats(xsv, n, sq, rstd, kxs, 'rstd'))
                th.append(lambda: P.tt('dve', mixv, xsv, rb, ALU.mult, R=[kxs, 'rstd'], W=[kmx]))
                for kt in range(KT):
                    th.append(lambda kt=kt: P.act(r3(h2b)[:, kt, :], mixv[:, kt, :], AF.Identity, scale=A2v[:, l, kt, s_:s_ + 1],
                                                  bias=modv[:, l, 3 * KT + kt, s_:s_ + 1], R=[kmx, 'A2', 'mod'], W=['h2b']))
                th.append(lambda: P.dma('pool', H2[:, c0:c0 + n].rearrange("(k p) n -> p k n", p=128), r3(h2b), R=['h2b'], W=['H2']))
                return th
            nb4 = len(tb256)
            pend = []
            for bi in range(nb4 + 1):
                interleave(stageA(bi) if bi < nb4 else [], pend)
                pend = stageB(bi) if bi < nb4 else []

        def phase5(l):
            P.barrier()
            ar.release(0)
            P.dma('sp', wconv[:], w_convT[l], W=['wconv'])
            ZT = (RG + 2) * GW
            h2g = ar.bf16(KT * ZT)
            wvg = [ar.bf16(2 * KT * 128) for _ in range(2)]
            dg = [ar.bf16(2 * 9 * 128) for _ in range(2)]
            acts = [ar.bf16(max(RG * GW, CTX)) for _ in range(2)]
            sgb = [ar.f32(512) for _ in range(2)]
            groups = [('ctx', 0, 0)] + [('lat', gi, gi * RG) for gi in range(NGRP)]
            qc = [0]
            pend_conv = [None]
            zpq = [[[ar.bf16(ZPW) for v in range(2)] for g in range(NGRP)] for _ in range(2)]
            for b_ in range(2):
                for g in range(NGRP):
                    for v in range(2):
                        P.memset('pool', zpq[b_][g][v], 0.0, W=[('zpq', b_, g, v)])
            for (kind, gi, r0) in groups:
                if kind == 'ctx':
                    ntz = CTX
                    zc0 = 0
                else:
                    zr0 = max(r0 - 1, 0); zr1 = min(r0 + RG + 1, ROWS)
                    ntz = (zr1 - zr0) * GW
                    zc0 = CTX + zr0 * GW
                h2v = h2g[:, 0:KT * ntz].rearrange("p (k n) -> p k n", k=KT)
                P.dma('sp', h2v, H2[:, zc0:zc0 + ntz].rearrange("(k p) n -> p k n", p=128), R=['H2'], W=['h2g'])
                first_grp = (kind == 'ctx')
                for q in range(NQ):
                    b = qc[0] % 2
                    qc[0] += 1
                    wt = wvg[b]; wk = ('wvg', b)
                    wtv = wt.rearrange("p (v k m) -> p v k m", v=2, k=KT)
                    if first_grp:
                        for v in range(2):
                            cdma(wtv[:, v], w_up[l][:, v * DFF + q * 128:v * DFF + (q + 1) * 128].rearrange("(k p) m -> p k m", p=128), W=[wk])
                        P.dma('sp', WUs[q], wt, R=[wk], W=[('WUs', q)])
                    else:
                        P.dma('sp', wt, WUs[q], R=[('WUs', q)], W=[wk])
                    dgt = dg[b]; dk = ('dg', b)
                    dgv = dgt.rearrange("p (v t m) -> p v t m", v=2, t=9)
                    for v in range(2):
                        for t_ in range(9):
                            if kind == 'ctx' and t_ // 3 != 1:
                                continue
                            P.ts('dve', dgv[:, v, t_, :], ident_f[:], wconvv[:, v * NQ + q, t_:t_ + 1], None, ALU.mult,
                                 R=['ident_f', 'wconv'], W=[dk])
                    at = acts[b]; ak = ('acts', b)
                    if kind == 'ctx':
                        for v in range(2):
                            ps, pk = nps()
                            P.mm([(ps[:, 0:CTX], wtv[:, v, kt, :], h2v[:, kt, :], kt == 0, kt == KT - 1) for kt in range(KT)],
                                 R=[wk, 'h2g'], W=[pk])
                            P.copy('act', zpc[v][:, 1:CTX + 1], ps[:, 0:CTX], R=[pk], W=[('zpc', v)])
                        pcs = []
                        for v in range(2):
                            ps, pk = nps()
                            P.mm([(ps[:, 0:CTX], dgv[:, v, 3 + dx, :], zpc[v][:, dx:dx + CTX], dx == 0, dx == 2) for dx in range(3)],
                                 R=[dk, ('zpc', v)], W=[pk])
                            pcs.append((ps, pk))
                        sb_ = sgb[0]
                        P.act(sb_[:, 0:CTX], pcs[1][0][:, 0:CTX], AF.Silu, R=[pcs[1][1]], W=[('sgb', 0)])
                        P.tt('dve', at[:, 0:CTX], pcs[0][0][:, 0:CTX], sb_[:, 0:CTX], ALU.mult, R=[pcs[0][1], ('sgb', 0)], W=[ak])
                        P.dma('pool', ACTS[q * 128:(q + 1) * 128, 0:CTX], at[:, 0:CTX], R=[ak], W=['ACTS'])
                    else:
                        zpv = [zpq[b][gi][v].rearrange("p (r c) -> p r c", c=GW + 2) for v in range(2)]
                        zk = [('zpq', b, gi, v) for v in range(2)]

                        def up_part(v, wtv=wtv, wk=wk, zpv=zpv, zk=zk):
                            for (ra, nr_) in blocks(zr0, zr1, 8):
                                n = nr_ * GW
                                ta_ = (ra - zr0) * GW
                                ps, pk = nps()
                                P.mm([(ps[:, 0:n], wtv[:, v, kt, :], h2v[:, kt, ta_:ta_ + n], kt == 0, kt == KT - 1) for kt in range(KT)],
                                     R=[wk, 'h2g'], W=[pk])
                                sl0 = ra - (r0 - 1)
                                P.copy('act', zpv[v][:, sl0:sl0 + nr_, 1:GW + 1], ps[:, 0:n].rearrange("p (r c) -> p r c", c=GW),
                                       R=[pk], W=[zk[v]])

                        def conv_part(q=q, dgv=dgv, dk=dk, zpv=zpv, zk=zk, at=at, ak=ak):
                            for oi, (ra, nr_) in enumerate(blocks(r0, r0 + RG, 8)):
                                n = nr_ * GW
                                pcs = []
                                for v in range(2):
                                    ps, pk = nps()
                                    mms = []
                                    for t_ in range(9):
                                        dy, dx = t_ // 3, t_ % 3
                                        sl0 = ra - (r0 - 1) + dy - 1
                                        mms.append((ps[:, 0:n].rearrange("p (r c) -> p r c", c=GW), dgv[:, v, t_, :],
                                                    zpv[v][:, sl0:sl0 + nr_, dx:dx + GW], t_ == 0, t_ == 8))
                                    P.mm(mms, R=[dk, zk[v]], W=[pk])
                                    pcs.append((ps, pk))
                                sb_ = sgb[oi % 2]; sk = ('sgb', oi % 2)
                                P.act(sb_[:, 0:n], pcs[1][0][:, 0:n], AF.Silu, R=[pcs[1][1]], W=[sk])
                                to = (ra - r0) * GW
                                P.tt('dve', at[:, to:to + n], pcs[0][0][:, 0:n], sb_[:, 0:n], ALU.mult, R=[pcs[0][1], sk], W=[ak])
                            c0 = CTX + r0 * GW
                            P.dma('pool', ACTS[q * 128:(q + 1) * 128, c0:c0 + RG * GW], at[:, 0:RG * GW], R=[ak], W=['ACTS'])
                        up_part(0)
                        if pend_conv[0] is not None:
                            pend_conv[0]()
                        up_part(1)
                        pend_conv[0] = conv_part
                if pend_conv[0] is not None:
                    pend_conv[0]()
                    pend_conv[0] = None

        def phase6(l):
            P.barrier()
            ar.release(0)
            last = (l == DEPTH - 1)
            ab = [ar.bf16(NQ * 512) for _ in range(2)]
            wd = [ar.bf16(NQ * 128) for _ in range(2)]
            f = ar.f32(KT * 512); xs = ar.f32(KT * 512); rstd = ar.f32(512)
            sqs = [ar.bf16(512) for _ in range(2)]
            wc = [0]
            for bi, (c0, n, s) in enumerate(tb512):
                pb_ = bi % 2
                abv = ab[pb_][:, 0:NQ * n].rearrange("p (q n) -> p q n", q=NQ)
                r3 = lambda t_: t_[:, 0:KT * n].rearrange("p (k n) -> p k n", k=KT)
                qparts = blocks(0, NQ, (NQ + 3) // 4)
                abk = [('ab', pb_, qi) for qi in range(len(qparts))]

                def load_ab(bj):
                    (cc0, nn, _s) = tb512[bj]
                    pj = bj % 2
                    av = ab[pj][:, 0:NQ * nn].rearrange("p (q n) -> p q n", q=NQ)
                    for qi, (q0, nq) in enumerate(qparts):
                        P.dma('sp', av[:, q0:q0 + nq, :], ACTS[q0 * 128:(q0 + nq) * 128, cc0:cc0 + nn].rearrange("(q p) n -> p q n", p=128),
                              R=['ACTS'], W=[('ab', pj, qi)])
                if bi == 0:
                    load_ab(0)
                if bi + 1 < len(tb512):
                    load_ab(bi + 1)
                fk = [('f', m_) for m_ in range(KT)]
                pend = None
                for mt in range(KT):
                    b = wc[0] % 2
                    wc[0] += 1
                    wt = wd[b]; wk = ('wd', b)
                    wtv = wt.rearrange("p (q m) -> p q m", q=NQ)
                    if bi == 0:
                        cdma(wtv, w_down[l][:, mt * 128:(mt + 1) * 128].rearrange("(q p) m -> p q m", p=128), W=[wk])
                        P.dma('sp', WDs[mt], wt, R=[wk], W=[('WDs', mt)])
                    else:
                        P.dma('sp', wt, WDs[mt], R=[('WDs', mt)], W=[wk])
                    ps, pk = nps()
                    P.mm([(ps[:, 0:n], wtv[:, q, :], abv[:, q, :], q == 0, q == NQ - 1) for q in range(NQ)], R=[wk] + abk, W=[pk])
                    P.copy('act' if mt % 2 else 'dve', r3(f)[:, mt, :], ps[:, 0:n], R=[pk], W=[fk[mt]])
                    if pend is not None:
                        pend()

                    def pend(mt=mt):
                        sq_ = sqs[mt % 2]; sk_ = ('sqs', mt % 2)
                        P.act(sq_[:, 0:n], r3(f)[:, mt, :], AF.Square, R=[fk[mt]], W=[sk_])
                        P.mm([(pss[:, 0:n], ones_bf[:], sq_[:, 0:n], mt == 0, mt == KT - 1)], R=[sk_, 'ones'], W=['pss'])
                pend()
                P.dma('pool', r3(xs), Rr[:, c0:c0 + n].rearrange("(k p) n -> p k n", p=128), R=['Rr'], W=['xs'])
                P.act(rstd[:, 0:n], pss[:, 0:n], AF.Sqrt, bias=EPS, scale=1.0 / D, R=['pss'], W=['rstd'])
                P.add('dve', lambda e, n=n: e.reciprocal(rstd[:, 0:n], rstd[:, 0:n]), R=['rstd'], W=['rstd'])
                P.tt('dve', r3(f), r3(f), rstd[:, 0:n].unsqueeze(1).to_broadcast([128, KT, n]), ALU.mult, R=fk + ['rstd'], W=fk)
                for mt in range(KT):
                    P.stt(r3(xs)[:, mt, :], r3(f)[:, mt, :], G2v[:, l, mt, s:s + 1], r3(xs)[:, mt, :], ALU.mult, ALU.add,
                          R=[fk[mt], 'G2', 'xs'], W=['xs'])
                if last and s == 0:
                    P.dma('pool', out[:, c0 - CTX:c0 - CTX + n].rearrange("(k p) n -> p k n", p=128), r3(xs), R=['xs'], W=['out'])
                else:
                    P.dma('pool', Rr[:, c0:c0 + n].rearrange("(k p) n -> p k n", p=128), r3(xs), R=['xs'], W=['Rr'])

        stop = cfg_stop[0]
        phase0()
        for l in range(DEPTH):
            for ph, fn in enumerate((phase1, phase2, phase3, phase4, phase5, phase6)):
                if stop is not None and (l, ph + 1) > stop:
                    break
                fn(l)
        info = P.emit(st)
        info['arena_peak'] = ar.peak
        print("build info", info, flush=True)
    return nc


cfg_stop = [None]
RG_MAX = [32]
AW_OVR = [45600]
sub = [9]


def _consts():
    ident = np.eye(128, dtype=np.float32)
    esel = np.zeros((128, 8, 8, 128), np.float32)
    etsel = np.zeros((128, 8, 8, 128), np.float32)
    for g8 in range(8):
        for j in range(8):
            for h in range(16):
                esel[g8 * 16 + h, g8, j, j * 16 + h] = 1.0
                etsel[j * 16 + h, g8, j, g8 * 16 + h] = 1.0
    jj = np.arange(128) // 16
    maskf = (jj[None, :] >= jj[:, None]).astype(np.float32)
    maskb = (jj[:, None] >= jj[None, :]).astype(np.float32)
    iet = np.zeros((4, 16), np.float32)
    for wi, w in enumerate((2, 4, 8, 16)):
        for t in range(8):
            iet[wi, t] = 1.0 / (t + w // 2 if t < w // 2 else w)
            e = 7 - t
            iet[wi, 8 + t] = 1.0 / (e + 1 + w // 2 if e <= w // 2 - 1 else w)
    ietab = np.broadcast_to(iet.reshape(1, 64), (128, 64)).copy()
    return dict(ident=ident, esel=esel.reshape(128, -1), etsel=etsel.reshape(128, -1), maskf=maskf, maskb=maskb, ietab=ietab)


def _prep_shared(cfg, inp):
    D, L, CTX, DFF, DEPTH, GW = cfg
    KT = D // 128
    f = lambda a: np.ascontiguousarray(np.asarray(a, dtype=np.float32))
    NG = (D // 4) // 16
    NPAIR = NG // 2
    sh = {}
    for k in ("w_ada", "w_in", "w_pool", "w_glu", "w_out", "w_up", "w_down"):
        sh[k] = f(inp[k])
    sh["b_adaT"] = f(np.asarray(inp["b_ada"]).reshape(DEPTH, 6 * KT, 128).transpose(0, 2, 1))
    sh["pscT"] = f(np.asarray(inp["pool_scale"]).reshape(DEPTH, -1, 128).transpose(0, 2, 1))

    def statelay(a):
        a = np.asarray(a).reshape(DEPTH, 2, NPAIR, 2, 64)
        return f(a.transpose(0, 3, 4, 1, 2).reshape(DEPTH, 128, 2 * NPAIR))
    sh["a_reT"] = statelay(inp["ssm_a_re"])
    sh["a_imT"] = statelay(inp["ssm_a_im"])
    ldt = np.broadcast_to(np.asarray(inp["ssm_log_dt"])[..., None], (DEPTH, 2, NG, 64))
    sh["ldtT"] = statelay(ldt)

    def blay(a):
        a = np.asarray(a).reshape(DEPTH, 2, NPAIR, 2, 64, 16)
        return f(a.transpose(0, 3, 4, 1, 2, 5).reshape(DEPTH, 128, 2 * NPAIR * 16))

    def clay(a):
        a = np.asarray(a).reshape(DEPTH, 2, NPAIR, 2, 16, 64)
        return f(a.transpose(0, 3, 5, 1, 2, 4).reshape(DEPTH, 128, 2 * NPAIR * 16))
    sh["bT_re"] = blay(inp["ssm_b_re"]); sh["bT_im"] = blay(inp["ssm_b_im"])
    sh["cT_re"] = clay(inp["ssm_c_re"]); sh["cT_im"] = clay(inp["ssm_c_im"])
    sh["ssm_dT"] = f(np.asarray(inp["ssm_d"]).reshape(DEPTH, -1, 128).transpose(0, 2, 1))
    g = np.stack([np.asarray(inp[k]) for k in ("g_pre_mix", "g_post_mix", "g_pre_ffn", "g_post_ffn")], 1)
    sh["gT"] = f(g.reshape(DEPTH, 4, KT, 128).transpose(3, 0, 1, 2).reshape(128, DEPTH * 4 * KT))
    wc = np.asarray(inp["w_conv"]).reshape(DEPTH, 9, -1, 128)
    sh["w_convT"] = f(wc.transpose(0, 3, 2, 1).reshape(DEPTH, 128, -1))
    sh.update(_consts())
    return sh


def run_cfg(cfg, inp, ncores, dbg=False, stop=None):
    D, L, CTX, DFF, DEPTH, GW = cfg
    KT = D // 128
    cfg_stop[0] = stop
    nc = build(cfg, dbg=dbg)
    sh = _prep_shared(cfg, inp)
    x = np.asarray(inp["x"], dtype=np.float32); c = np.asarray(inp["c"], dtype=np.float32)
    ctx = np.asarray(inp["ctx"], dtype=np.float32); c_ctx = np.asarray(inp["c_ctx"], dtype=np.float32)
    in_maps = []
    for b in range(ncores):
        m = dict(sh)
        m["xT"] = np.ascontiguousarray(x[b].T)
        m["ctxT"] = np.ascontiguousarray(ctx[b].T)
        ccv = np.stack([c[b].reshape(KT, 128), c_ctx.reshape(KT, 128)], -1)
        m["cc"] = np.ascontiguousarray(ccv.transpose(1, 0, 2).reshape(128, KT * 2))
        in_maps.append(m)
    res = run_bass_kernel_spmd(nc, in_maps, core_ids=list(range(ncores)))
    return res


def kernel(**inputs):
    cfg = (2048, 4096, 256, 5632, 4, 64)
    res = run_cfg(cfg, inputs, 8)
    outs = [np.ascontiguousarray(res.results[b]["out"].T) for b in range(8)]
    return np.stack(outs, 0).astype(np.float32)
```

```python
import contextlib
import numpy as np
import concourse.bass as bass
import concourse.mybir as mybir

F32 = mybir.dt.float32
BF16 = mybir.dt.bfloat16
AF = mybir.ActivationFunctionType
ALU = mybir.AluOpType

RING = 10


class Prog:
    def __init__(self, nc):
        self.nc = nc
        self.ops = []

    def add(self, eng, fn, R=(), W=(), dma=False):
        self.ops.append(dict(eng=eng, fn=fn, R=tuple(R), W=tuple(W), dma=dma, barrier=False))

    def barrier(self):
        self.ops.append(dict(eng=None, fn=None, R=(), W=(), dma=False, barrier=True))

    def dma(self, q, out, in_, R=(), W=(), **kw):
        self.add(q, lambda e: e.dma_start(out=out, in_=in_, **kw), R, W, dma=True)

    def mm(self, mms, R=(), W=()):
        def fn(e):
            ins = None
            for (o, l, r, st, sp) in mms:
                ins = e.matmul(o, l, r, start=st, stop=sp)
            return ins
        self.add('pe', fn, R, W)

    def act(self, out, in_, func, R=(), W=(), **kw):
        self.add('act', lambda e: e.activation(out, in_, func, **kw), R, W)

    def tt(self, eng, out, in0, in1, op, R=(), W=()):
        self.add(eng, lambda e: e.tensor_tensor(out, in0, in1, op), R, W)

    def ts(self, eng, out, in0, s1, s2, op0, op1=None, R=(), W=()):
        if op1 is None:
            self.add(eng, lambda e: e.tensor_scalar(out, in0, s1, None, op0), R, W)
        else:
            self.add(eng, lambda e: e.tensor_scalar(out, in0, s1, s2, op0, op1), R, W)

    def stt(self, out, in0, scalar, in1, op0, op1, R=(), W=()):
        self.add('dve', lambda e: e.scalar_tensor_tensor(out, in0, scalar, in1, op0, op1), R, W)

    def copy(self, eng, out, in_, R=(), W=()):
        if eng == 'act':
            self.add('act', lambda e: e.activation(out, in_, AF.Copy), R, W)
        else:
            self.add(eng, lambda e: e.tensor_copy(out, in_), R, W)

    def memset(self, eng, ap, val, W=()):
        self.add(eng, lambda e: e.memset(ap, val), (), W)

    def emit(self, stack):
        nc = self.nc
        ops = self.ops
        n = len(ops)
        last_w = {}
        readers = {}
        deps = [None] * n
        engs = ['pe', 'act', 'dve', 'pool', 'sp']
        last_op = {e: None for e in engs}
        last_dmas = {e: [] for e in engs}
        need_bar = {e: set() for e in engs}
        for i, op in enumerate(ops):
            if op['barrier']:
                bd = set()
                for e in engs:
                    if last_op[e] is not None:
                        bd.add(last_op[e])
                    bd.update(last_dmas[e])
                for e in engs:
                    need_bar[e] |= bd
                deps[i] = set()
                continue
            d = set(need_bar[op['eng']])
            need_bar[op['eng']] = set()
            last_op[op['eng']] = i
            if op['dma']:
                last_dmas[op['eng']] = (last_dmas[op['eng']] + [i])[-RING:]
            for k in op['R']:
                if k in last_w:
                    d.add(last_w[k])
            for k in op['W']:
                if k in last_w:
                    d.add(last_w[k])
                for r in readers.get(k, ()):
                    d.add(r)
            d.discard(i)
            for k in op['W']:
                last_w[k] = i
                readers[k] = []
            for k in op['R']:
                readers.setdefault(k, []).append(i)
            deps[i] = d
        signal = [False] * n
        for i, op in enumerate(ops):
            if op['barrier']:
                continue
            nd = set()
            for j in deps[i]:
                if ops[j]['eng'] == 'pe' and op['eng'] == 'pe' and not ops[j]['dma'] and not op['dma']:
                    continue
                nd.add(j)
            deps[i] = nd
            for j in nd:
                signal[j] = True
        SEG = 30000
        cnt = {e: 0 for e in engs}
        dcnt = {e: 0 for e in engs}
        tok = [None] * n
        sems = {}

        def getsem(key):
            if key not in sems:
                sems[key] = stack.enter_context(nc.semaphore(name="s_%s" % "_".join(str(x) for x in key)))
            return sems[key]

        for i, op in enumerate(ops):
            e = op['eng']
            if op['barrier']:
                continue
            if op['dma']:
                k = dcnt[e]
                dcnt[e] += 1
                op['dslot'] = k
                tok[i] = (('d', e, k % RING), 16 * (k // RING + 1))
            elif signal[i]:
                c = cnt[e]
                cnt[e] += 1
                tok[i] = (('c', e, c // SEG), c % SEG + 1)
        for i in range(n):
            if tok[i] is not None:
                getsem(tok[i][0])
        by_eng = {e: [] for e in engs}
        for i, op in enumerate(ops):
            if not op['barrier']:
                by_eng[op['eng']].append(i)
        engobj = {}
        block = stack.enter_context(nc.Block())

        def run(e, eng):
            waited = {}
            def wait(t):
                key, val = t
                if waited.get(key, 0) >= val:
                    return
                waited[key] = val
                eng.wait_ge(getsem(key), val)
            for i in by_eng[e]:
                op = ops[i]
                for j in sorted(deps[i]):
                    wait(tok[j])
                if op['dma']:
                    k = op['dslot']
                    if k >= RING:
                        wait((('d', e, k % RING), 16 * (k // RING)))
                ins = op['fn'](eng)
                if tok[i] is not None:
                    key, val = tok[i]
                    assert ins is not None, "op returned no instruction"
                    ins.then_inc(getsem(key), 16 if op['dma'] else 1)
            k = dcnt[e]
            for s in range(RING):
                m = (k - s + RING - 1) // RING if k > s else 0
                if m > 0:
                    wait((('d', e, s), 16 * m))

        @block.tensor
        def _(eng):
            run('pe', eng)

        @block.scalar
        def _(eng):
            run('act', eng)

        @block.vector
        def _(eng):
            run('dve', eng)

        @block.gpsimd
        def _(eng):
            run('pool', eng)

        @block.sync
        def _(eng):
            run('sp', eng)
        return dict(n_ops=n, cnt=cnt, dcnt=dcnt, nsems=len(sems))
from concourse.bass_utils import run_bass_kernel_spmd

import math


class Arena:
    def __init__(self, ap32, nwords):
        self.ap = ap32
        self.n = nwords
        self.off = 0
        self.peak = 0

    def mark(self):
        return self.off

    def release(self, m):
        self.off = m

    def _take(self, nw):
        o = self.off
        self.off += nw
        self.peak = max(self.peak, self.off)
        assert self.off <= self.n, "arena overflow %d > %d" % (self.off, self.n)
        return o

    def f32(self, n):
        o = self._take(n)
        return self.ap[:, o:o + n]

    def bf16(self, n):
        nw = (n + 1) // 2
        o = self._take(nw)
        return self.ap[:, o:o + nw].bitcast(BF16)[:, 0:n]


def blocks(c0, c1, step):
    return [(a, min(step, c1 - a)) for a in range(c0, c1, step)]


def build(cfg, dbg=False):
    D, L, CTX, DFF, DEPTH, GW = cfg
    nc = bass.Bass("TRN2", target_bir_lowering=False)
    KT = D // 128
    W = CTX + L
    NQ = DFF // 128
    NJ = 2 * NQ
    SSMW = D // 4
    POOLW = D - SSMW
    UT = SSMW // 128
    PG = POOLW // 4
    PGT = PG // 128
    NPAIR = SSMW // 32
    NG = SSMW // 16
    CCH = CTX // 8
    NC1 = W // 8
    NCH = NC1 + CCH
    ROWS = L // GW
    RG = min(RG_MAX[0], ROWS)
    NGRP = ROWS // RG
    EPS = 1e-6
    NB = 2 if NPAIR >= 2 else 1
    LV = 0
    while LV < 5 and NC1 % (2 << LV) == 0:
        LV += 1
    PB = NPAIR // NB

    def din(name, shape, dt=F32):
        return nc.dram_tensor(name, list(shape), dt, kind="ExternalInput").ap()

    def dscr(name, shape, dt):
        return nc.dram_tensor(name, list(shape), dt, kind=("ExternalOutput" if dbg else "Internal")).ap()

    xT = din("xT", [D, L]); ctxT = din("ctxT", [D, CTX]); cc = din("cc", [128, KT * 2])
    w_ada = din("w_ada", [DEPTH, D, 6 * D]); b_adaT = din("b_adaT", [DEPTH, 128, 6 * KT])
    w_in = din("w_in", [DEPTH, D, D]); w_pool = din("w_pool", [DEPTH, 4, PG, PG]); pscT = din("pscT", [DEPTH, 128, POOLW // 128])
    a_reT = din("a_reT", [DEPTH, 128, 2 * NPAIR]); a_imT = din("a_imT", [DEPTH, 128, 2 * NPAIR]); ldtT = din("ldtT", [DEPTH, 128, 2 * NPAIR])
    bT_re = din("bT_re", [DEPTH, 128, 2 * NPAIR * 16]); bT_im = din("bT_im", [DEPTH, 128, 2 * NPAIR * 16])
    cT_re = din("cT_re", [DEPTH, 128, 2 * NPAIR * 16]); cT_im = din("cT_im", [DEPTH, 128, 2 * NPAIR * 16])
    ssm_dT = din("ssm_dT", [DEPTH, 128, UT]); w_glu = din("w_glu", [DEPTH, SSMW, SSMW]); w_out = din("w_out", [DEPTH, D, D])
    gT = din("gT", [128, DEPTH * 4 * KT])
    w_up = din("w_up", [DEPTH, D, 2 * DFF]); w_convT = din("w_convT", [DEPTH, 128, NJ * 9]); w_down = din("w_down", [DEPTH, DFF, D])
    ident_d = din("ident", [128, 128]); esel_d = din("esel", [128, 8 * 8 * 128]); etsel_d = din("etsel", [128, 8 * 8 * 128])
    maskf_d = din("maskf", [128, 128]); maskb_d = din("maskb", [128, 128]); ietab_d = din("ietab", [128, 4 * 16])
    out = nc.dram_tensor("out", [D, L], F32, kind="ExternalOutput").ap()

    Rr = dscr("Rr", [D, W], F32)
    U = dscr("U", [D, W], F32)
    MIXIN = dscr("MIXIN", [D, W], BF16)
    H2 = dscr("H2", [D, W], BF16)
    ACTS = dscr("ACTS", [DFF, W], BF16)
    WUs = nc.dram_tensor("WUs", [NQ, 128, 2 * KT * 128], BF16, kind="Internal").ap()
    WDs = nc.dram_tensor("WDs", [KT, 128, NQ * 128], BF16, kind="Internal").ap()
    YD = nc.dram_tensor("YD", [128, NG * NC1], BF16, kind="Internal").ap()
    MODd = dscr("MODd", [128, DEPTH * 6 * KT * 2], F32)

    st = contextlib.ExitStack()
    with st:
        def T(name, shape, dt):
            return st.enter_context(nc.sbuf_tensor(name, list(shape), dt))
        AW = AW_OVR[0]
        arena_t = T("arena", [128, AW], F32)
        ar = Arena(arena_t[:], AW)
        mod = T("mod", [128, DEPTH * 6 * KT * 2], F32)
        modv = mod[:].rearrange("p (l n s) -> p l n s", l=DEPTH, s=2)
        A1 = T("A1", [128, DEPTH * KT * 2], F32); G1 = T("G1", [128, DEPTH * KT * 2], F32)
        A2 = T("A2", [128, DEPTH * KT * 2], F32); G2 = T("G2", [128, DEPTH * KT * 2], F32)
        v4 = lambda t: t[:].rearrange("p (l k s) -> p l k s", l=DEPTH, s=2)
        A1v, G1v, A2v, G2v = v4(A1), v4(G1), v4(A2), v4(G2)
        gsb = T("gsb", [128, DEPTH * 4 * KT], F32)
        gv = gsb[:].rearrange("p (l w k) -> p l w k", l=DEPTH, w=4)
        ones_bf = T("ones_bf", [128, 128], BF16)
        ident_f = T("ident_f", [128, 128], F32)
        ident_b = T("ident_b", [128, 128], BF16)
        wconv = T("wconv", [128, NJ * 9], F32)
        wconvv = wconv[:].rearrange("p (j t) -> p j t", t=9)
        ZPW = (RG + 2) * (GW + 2)
        zp = [[T("zp%d_%d" % (g, v), [128, ZPW], BF16) for v in range(2)] for g in range(NGRP)]
        zpc = [T("zpc%d" % v, [128, CTX + 2], BF16) for v in range(2)]
        NPS = 7
        psb = [st.enter_context(nc.psum_tensor("ps%d" % i, [128, 512], F32)) for i in range(NPS)]
        pss = st.enter_context(nc.psum_tensor("pss", [128, 512], F32))
        P = Prog(nc)
        psi = [0]

        def nps():
            i = psi[0] % NPS
            psi[0] += 1
            return psb[i], ('ps', i)

        def cdma(out_, in_, R=(), W=()):
            P.dma('pool', out_, in_, R=R, W=W, max_dma_last_dim=4096)

        tb512 = [(a, n, 1) for a, n in blocks(0, CTX, 512)] + [(a, n, 0) for a, n in blocks(CTX, W, 512)]
        tb256 = [(a, n, 1) for a, n in blocks(0, CTX, 256)] + [(a, n, 0) for a, n in blocks(CTX, W, 256)]

        P.memset('dve', ones_bf[:], 1.0, W=['ones'])
        P.dma('sp', ident_f[:], ident_d, W=['ident_f'])
        cdma(ident_b[:], ident_d, W=['ident_b'])
        P.dma('sp', gsb[:], gT, W=['gsb'])
        for g in range(NGRP):
            for v in range(2):
                P.memset('pool', zp[g][v][:], 0.0, W=[('zp', g, v)])
        for v in range(2):
            P.memset('pool', zpc[v][:], 0.0, W=[('zpc', v)])
        P.dma('sp', Rr[:, 0:CTX], ctxT, W=['Rr'])
        for (a, n) in blocks(0, L, 1024):
            P.dma('sp', Rr[:, CTX + a:CTX + a + n], xT[:, a:a + n], W=['Rr'])

        def phase0():
            P.barrier()
            ar.release(0)
            ccs = ar.f32(KT * 2)
            scb = ar.bf16(KT * 2)
            scbv = scb.rearrange("p (k s) -> p k s", s=2)
            P.dma('sp', ccs, cc, W=['ccs'])
            P.act(scb, ccs, AF.Silu, R=['ccs'], W=['scb'])
            bada = ar.f32(6 * KT)
            wa = [ar.bf16(KT * 512) for _ in range(2)]
            NT6 = 6 * KT
            cnt = 0
            for l in range(DEPTH):
                P.dma('sp', bada, b_adaT[l], W=['bada'])
                pm, pmk = nps()
                for nb in range(NT6 // 4):
                    wt = wa[cnt % 2]
                    wk = ('wa', cnt % 2)
                    cnt += 1
                    wtv = wt.rearrange("p (k n) -> p k n", k=KT)
                    cdma(wtv, w_ada[l][:, nb * 512:(nb + 1) * 512].rearrange("(k p) n -> p k n", p=128), W=[wk])
                    for nt in range(4):
                        n_ = nb * 4 + nt
                        P.mm([(pm[:, 2 * n_:2 * n_ + 2], wtv[:, kt, nt * 128:(nt + 1) * 128], scbv[:, kt, :], kt == 0, kt == KT - 1)
                              for kt in range(KT)], R=[wk, 'scb'], W=[pmk])
                P.tt('dve', modv[:, l], pm[:, 0:2 * NT6].rearrange("p (n s) -> p n s", s=2),
                     bada.unsqueeze(2).to_broadcast([128, NT6, 2]), ALU.add, R=[pmk, 'bada'], W=['mod'])
                gb = lambda w_: gv[:, l, w_, :].unsqueeze(2).to_broadcast([128, KT, 2])
                P.stt(A1v[:, l], modv[:, l, KT:2 * KT, :], 1.0, gb(0), ALU.add, ALU.mult, R=['mod', 'gsb'], W=['A1'])
                P.tt('dve', G1v[:, l], modv[:, l, 2 * KT:3 * KT, :], gb(1), ALU.mult, R=['mod', 'gsb'], W=['G1'])
                P.stt(A2v[:, l], modv[:, l, 4 * KT:5 * KT, :], 1.0, gb(2), ALU.add, ALU.mult, R=['mod', 'gsb'], W=['A2'])
                P.tt('dve', G2v[:, l], modv[:, l, 5 * KT:6 * KT, :], gb(3), ALU.mult, R=['mod', 'gsb'], W=['G2'])
            if dbg:
                P.dma('sp', MODd, mod[:], R=['mod'], W=['MODd'])

        def rms_stats(src3, n, sq, rstd, Rk, Wk, sqk='sq'):
            sqv = sq[:, 0:KT * n].rearrange("p (k n) -> p k n", k=KT)
            P.act(sqv, src3, AF.Square, R=[Rk], W=[sqk])
            ps, pk = nps()
            P.mm([(ps[:, 0:n], ones_bf[:], sqv[:, kt, :], kt == 0, kt == KT - 1) for kt in range(KT)], R=[sqk, 'ones'], W=[pk])
            P.act(rstd[:, 0:n], ps[:, 0:n], AF.Sqrt, bias=EPS, scale=1.0 / D, R=[pk], W=[Wk])
            P.add('dve', lambda e: e.reciprocal(rstd[:, 0:n], rstd[:, 0:n]), R=[Wk], W=[Wk])

        def interleave(a_th, b_th):
            na, nb = len(a_th), len(b_th)
            bi_ = 0
            for k_, th in enumerate(a_th):
                th()
                upto = ((k_ + 1) * nb) // max(na, 1)
                while bi_ < upto:
                    b_th[bi_]()
                    bi_ += 1
            while bi_ < nb:
                b_th[bi_]()
                bi_ += 1

        def phase1(l):
            P.barrier()
            ar.release(0)
            win = ar.bf16(KT * D)
            winv = win.rearrange("p (k n) -> p k n", k=KT)
            for (a, n) in blocks(0, D, 512):
                cdma(winv[:, :, a:a + n], w_in[l][:, a:a + n].rearrange("(k p) n -> p k n", p=128), W=['win'])
            NE = 256
            sets = [(ar.f32(KT * NE), ar.bf16(KT * NE), ar.bf16(KT * NE), ar.f32(NE)) for _ in range(2)]
            def ctx1(bi):
                (c0, n, s_) = tb256[bi]
                b = bi % 2
                xs, sq, hb, rstd = sets[b]
                kx, kq, kh, kr = ('xs', b), ('sq', b), ('hb', b), ('rstd', b)
                xsv = xs[:, 0:KT * n].rearrange("p (k n) -> p k n", k=KT)
                hbv = hb[:, 0:KT * n].rearrange("p (k n) -> p k n", k=KT)
                return c0, n, s_, xs, sq, hb, rstd, kx, kq, kh, kr, xsv, hbv

            def prologue(bi):
                c0, n, s_, xs, sq, hb, rstd, kx, kq, kh, kr, xsv, hbv = ctx1(bi)
                th = []
                th.append(lambda: P.dma('sp', xsv, Rr[:, c0:c0 + n].rearrange("(k p) n -> p k n", p=128), R=['Rr'], W=[kx]))
                th.append(lambda: rms_stats(xsv, n, sq, rstd, kx, kr, kq))
                th.append(lambda: P.tt('dve', xsv, xsv, rstd[:, 0:n].unsqueeze(1).to_broadcast([128, KT, n]), ALU.mult, R=[kx, kr], W=[kx]))
                for kt in range(KT):
                    th.append(lambda kt=kt: P.act(hbv[:, kt, :], xsv[:, kt, :], AF.Identity, scale=A1v[:, l, kt, s_:s_ + 1],
                                                  bias=modv[:, l, kt, s_:s_ + 1], R=[kx, 'A1', 'mod'], W=[kh]))
                return th

            def main(bi):
                c0, n, s_, xs, sq, hb, rstd, kx, kq, kh, kr, xsv, hbv = ctx1(bi)
                th = []
                for mt in range(KT):
                    def tm(mt=mt):
                        ps, pk = nps()
                        P.mm([(ps[:, 0:n], winv[:, kt, mt * 128:(mt + 1) * 128], hbv[:, kt, :], kt == 0, kt == KT - 1) for kt in range(KT)],
                             R=['win', kh], W=[pk])
                        P.copy('act' if mt % 2 else 'dve', xsv[:, mt, :], ps[:, 0:n], R=[pk], W=[kx])
                    th.append(tm)
                th.append(lambda: P.dma('sp', U[:, c0:c0 + n].rearrange("(k p) n -> p k n", p=128), xsv, R=[kx], W=['U']))
                return th
            for th in prologue(0):
                th()
            for bi in range(len(tb256)):
                interleave(main(bi), prologue(bi + 1) if bi + 1 < len(tb256) else [])

        def phase2(l):
            P.barrier()
            ar.release(0)
            PAD = 16
            oA = PAD
            oB = PAD + CTX + 2 * PAD
            WP = CTX + L + 4 * PAD
            segs = [(oA, 0, CTX), (oB, CTX, L)]
            wp = ar.bf16(PGT * PG); wpv = wp.rearrange("p (k n) -> p k n", k=PGT)
            pb = ar.bf16(PGT * W); pbv = pb.rearrange("p (k n) -> p k n", k=PGT)
            psets = [(ar.f32(WP), ar.f32(WP), ar.f32(WP)) for _ in range(2)]
            pob = ar.bf16(W)
            iet = ar.f32(64); ietv = iet.rearrange("p (w e) -> p w e", w=4)
            psc = ar.f32(POOLW // 128)
            t8 = ar.f32(8)
            P.dma('sp', iet, ietab_d, W=['iet'])
            P.dma('sp', psc, pscT[l], W=['psc'])
            for b_ in range(2):
                P.memset('dve', psets[b_][0], 0.0, W=[('up', b_)])
            for g in range(4):
                w = (2, 4, 8, 16)[g]
                cdma(wpv, w_pool[l][g].rearrange("(k p) n -> p k n", p=128), W=['wp'])
                for k3 in range(PGT):
                    ct = g * PGT + k3
                    up, ta, tb_ = psets[ct % 2]
                    ku, ka, kb = ('up', ct % 2), ('ta', ct % 2), ('tb', ct % 2)
                    for (o, c0, n) in segs:
                        P.dma('sp', up[:, o:o + n], U[ct * 128:(ct + 1) * 128, c0:c0 + n], R=['U'], W=[ku])
                    eng = 'dve' if ct % 2 == 0 else 'pool'
                    P.tt(eng, ta[:, 1:WP], up[:, 1:WP], up[:, 0:WP - 1], ALU.add, R=[ku], W=[ka])
                    if w == 2:
                        S = ta; Sk = ka
                    elif w == 4:
                        P.tt(eng, tb_[:, 2:WP - 1], ta[:, 1:WP - 2], ta[:, 3:WP], ALU.add, R=[ka], W=[kb])
                        S = tb_; Sk = kb
                    elif w == 8:
                        P.tt(eng, tb_[:, 3:WP], ta[:, 3:WP], ta[:, 1:WP - 2], ALU.add, R=[ka], W=[kb])
                        P.tt(eng, ta[:, 4:WP - 3], tb_[:, 3:WP - 4], tb_[:, 7:WP], ALU.add, R=[kb], W=[ka])
                        S = ta; Sk = ka
                    else:
                        P.tt(eng, tb_[:, 3:WP], ta[:, 3:WP], ta[:, 1:WP - 2], ALU.add, R=[ka], W=[kb])
                        P.tt(eng, ta[:, 7:WP], tb_[:, 7:WP], tb_[:, 3:WP - 4], ALU.add, R=[kb], W=[ka])
                        P.tt(eng, tb_[:, 8:WP - 7], ta[:, 7:WP - 8], ta[:, 15:WP], ALU.add, R=[ka], W=[kb])
                        S = tb_; Sk = kb
                    for (o, c0, n) in segs:
                        P.stt(pbv[:, k3, c0:c0 + n], S[:, o:o + n], 1.0 / w, up[:, o:o + n], ALU.mult, ALU.subtract,
                              R=[Sk, ku], W=['pb'])
                        for (eo, to) in ((0, 0), (n - 8, 8)):
                            P.tt('dve', t8, S[:, o + eo:o + eo + 8], ietv[:, g, to:to + 8], ALU.mult, R=[Sk, 'iet'], W=['t8'])
                            P.tt('dve', pbv[:, k3, c0 + eo:c0 + eo + 8], t8, up[:, o + eo:o + eo + 8], ALU.subtract,
                                 R=['t8', ku], W=['pb'])
                for m3 in range(PGT):
                    for (c0, n, s) in tb512:
                        ps, pk = nps()
                        P.mm([(ps[:, 0:n], wpv[:, k3, m3 * 128:(m3 + 1) * 128], pbv[:, k3, c0:c0 + n], k3 == 0, k3 == PGT - 1)
                              for k3 in range(PGT)], R=['wp', 'pb'], W=[pk])
                        P.act(pob[:, c0:c0 + n], ps[:, 0:n], AF.Identity, scale=psc[:, g * PGT + m3:g * PGT + m3 + 1],
                              R=[pk, 'psc'], W=['pob'])
                    r0 = (g * PGT + m3) * 128
                    P.dma('sp', MIXIN[r0:r0 + 128, :], pob, R=['pob'], W=['MIXIN'])

        def phase3(l):
            P.barrier()
            ar.release(0)
            N2 = 2 * NPAIR
            CZ = ar.bf16(2 * 2 * NG * 128); CZv = CZ.rearrange("p (d r g m) -> p d r g m", d=2, r=2, g=NG)
            BJ = ar.bf16(2 * 2 * NG * 64); BJv = BJ.rearrange("p (d r g m) -> p d r g m", d=2, r=2, g=NG)
            Mg = ar.bf16(NG * 128); Mgv = Mg.rearrange("p (g m) -> p g m", g=NG)
            L1 = [[ar.f32(2 * NPAIR) for _ in range(LV + 1)] for _ in range(2)]; L2 = [[ar.f32(2 * NPAIR) for _ in range(LV + 1)] for _ in range(2)]
            pm = ar.f32(2)
            mk0 = ar.mark()

            def sm(n=N2):
                return ar.f32(n)
            are = sm(); aim = sm(); ldt = sm(); dt = sm(); th = sm(); mag = sm()
            c_ = sm(); s_ = sm(); t1 = sm(); t2 = sm(); t3 = sm(); lre = sm(); lim = sm(); fre = sm(); fim = sm()
            P.dma('sp', are, a_reT[l], W=['are']); P.dma('sp', aim, a_imT[l], W=['aim']); P.dma('sp', ldt, ldtT[l], W=['ldt'])
            K = 'gen'

            def tt(o, a, b, op, eng='dve'):
                P.tt(eng, o, a, b, op, R=[K], W=[K])
            P.act(dt, ldt, AF.Exp, R=['ldt'], W=[K])
            P.tt('dve', th, aim, dt, ALU.mult, R=['aim', K], W=[K])
            P.tt('dve', t1, are, dt, ALU.mult, R=['are', K], W=[K])
            P.act(mag, t1, AF.Exp, R=[K], W=[K])
            hp = ar.f32(1)
            P.memset('dve', hp, math.pi / 2, W=[K])
            P.act(c_, th, AF.Sin, scale=1.0 / 16, bias=hp[:, 0:1], R=[K], W=[K])
            P.act(s_, th, AF.Sin, scale=1.0 / 16, R=[K], W=[K])
            for _ in range(4):
                tt(t1, c_, c_, ALU.mult); tt(t2, s_, s_, ALU.mult); tt(t3, c_, s_, ALU.mult)
                tt(c_, t1, t2, ALU.subtract)
                P.ts('dve', s_, t3, 2.0, None, ALU.mult, R=[K], W=[K])
            tt(lre, mag, c_, ALU.mult); tt(lim, mag, s_, ALU.mult)
            if sub[0] <= 0.1:
                return
            nr = sm(); den = sm()
            P.ts('dve', nr, lre, -1.0, None, ALU.add, R=[K], W=[K])
            tt(t1, are, are, ALU.mult); tt(t2, aim, aim, ALU.mult); tt(den, t1, t2, ALU.add)
            P.add('dve', lambda e: e.reciprocal(den, den), R=[K], W=[K])
            tt(t1, nr, are, ALU.mult); tt(t2, lim, aim, ALU.mult); tt(t1, t1, t2, ALU.add); tt(fre, t1, den, ALU.mult)
            tt(t1, lim, are, ALU.mult); tt(t2, nr, aim, ALU.mult); tt(t1, t1, t2, ALU.subtract); tt(fim, t1, den, ALU.mult)

            def cmul(ore, oim, ar_, ai_, br_, bi_):
                tt(t1, ar_, br_, ALU.mult); tt(t2, ai_, bi_, ALU.mult); tt(ore, t1, t2, ALU.subtract)
                tt(t1, ar_, bi_, ALU.mult); tt(t2, ai_, br_, ALU.mult); tt(oim, t1, t2, ALU.add)
            Zr = [sm() for _ in range(9)]; Zi = [sm() for _ in range(9)]
            ZNr = [sm() for _ in range(8)]; ZNi = [sm() for _ in range(8)]
            P.memset('dve', Zr[0], 1.0, W=[K]); P.memset('dve', Zi[0], 0.0, W=[K])
            P.memset('dve', ZNr[0], 1.0, W=[K]); P.memset('dve', ZNi[0], 0.0, W=[K])
            for e_ in range(1, 9):
                cmul(Zr[e_], Zi[e_], Zr[e_ - 1], Zi[e_ - 1], lre, lim)
            lir = sm(); lii = sm()
            tt(t1, lre, lre, ALU.mult); tt(t2, lim, lim, ALU.mult); tt(t3, t1, t2, ALU.add)
            P.add('dve', lambda e: e.reciprocal(t3, t3), R=[K], W=[K])
            tt(lir, lre, t3, ALU.mult)
            P.stt(lii, lim, -1.0, t3, ALU.mult, ALU.mult, R=[K], W=[K])
            for e_ in range(1, 8):
                cmul(ZNr[e_], ZNi[e_], ZNr[e_ - 1], ZNi[e_ - 1], lir, lii)
            ZFr = [sm() for _ in range(8)]; ZFi = [sm() for _ in range(8)]
            ZNFr = [sm() for _ in range(8)]; ZNFi = [sm() for _ in range(8)]
            for e_ in range(8):
                cmul(ZFr[e_], ZFi[e_], Zr[e_], Zi[e_], fre, fim)
                cmul(ZNFr[e_], ZNFi[e_], ZNr[e_], ZNi[e_], fre, fim)
            LPr = [Zr[8]] + [sm() for _ in range(LV)]; LPi = [Zi[8]] + [sm() for _ in range(LV)]
            for k_ in range(1, LV + 1):
                cmul(LPr[k_], LPi[k_], LPr[k_ - 1], LPi[k_ - 1], LPr[k_ - 1], LPi[k_ - 1])
            for d in range(2):
                sl = slice(d * NPAIR, (d + 1) * NPAIR)
                for k_ in range(LV + 1):
                    P.copy('dve', L1[d][k_][:, 0:NPAIR], LPr[k_][:, sl], R=[K], W=[K])
                    P.copy('dve', L1[d][k_][:, NPAIR:], LPr[k_][:, sl], R=[K], W=[K])
                    P.ts('dve', L2[d][k_][:, 0:NPAIR], LPi[k_][:, sl], -1.0, None, ALU.mult, R=[K], W=[K])
                    P.copy('dve', L2[d][k_][:, NPAIR:], LPi[k_][:, sl], R=[K], W=[K])
            if sub[0] <= 0.2:
                return
            Bre = ar.f32(N2 * 16); Bim = ar.f32(N2 * 16); Cre = ar.f32(N2 * 16); Cim = ar.f32(N2 * 16)
            P.dma('sp', Bre, bT_re[l], W=[K]); P.dma('sp', Bim, bT_im[l], W=[K])
            P.dma('sp', Cre, cT_re[l], W=[K]); P.dma('sp', Cim, cT_im[l], W=[K])
            b3 = lambda t_, d: t_.rearrange("p (d k h) -> p d k h", d=2, h=16)[:, d]
            MS = NPAIR * 128

            def mat():
                return ar.bf16(MS)
            mv = lambda m_: m_.rearrange("p (k j h) -> p k j h", k=NPAIR, j=8)
            o1 = ar.f32(NPAIR * 16); o2 = ar.f32(NPAIR * 16)
            o1v = o1.rearrange("p (k h) -> p k h", h=16); o2v = o2.rearrange("p (k h) -> p k h", h=16)

            def outer(dst_re, dst_im, d, j, fr, fi, Xre, Xim, neg_im):
                sl = slice(d * NPAIR, (d + 1) * NPAIR)
                frb = fr[:, sl].unsqueeze(2).to_broadcast([128, NPAIR, 16])
                fib = fi[:, sl].unsqueeze(2).to_broadcast([128, NPAIR, 16])
                xr = b3(Xre, d); xi = b3(Xim, d)
                tt(o1v, xr, frb, ALU.mult); tt(o2v, xi, fib, ALU.mult)
                tt(mv(dst_re)[:, :, j, :], o1v, o2v, ALU.subtract)
                tt(o1v, xi, frb, ALU.mult); tt(o2v, xr, fib, ALU.mult)
                if neg_im:
                    P.stt(mv(dst_im)[:, :, j, :], o1v, -1.0, o2v, ALU.mult, ALU.subtract, R=[K], W=[K])
                else:
                    tt(mv(dst_im)[:, :, j, :], o1v, o2v, ALU.add)
            CJr = [mat() for _ in range(2)]; CJn = [mat() for _ in range(2)]
            P.memset('dve', pm, 0.0, W=[K])
            P.memset('dve', pm[0:64, 0:1], 1.0, W=[K])
            P.memset('dve', pm[64:128, 1:2], 1.0, W=[K])
            BTr = [mat() for _ in range(2)]; BTi = [mat() for _ in range(2)]
            BCr = mat(); BCi = mat()
            CCr = [mat() for _ in range(2)]; CCn = [mat() for _ in range(2)]
            mkf = ar.f32(128); mkb = ar.f32(128); tm1 = ar.f32(128); tm2 = ar.f32(128)
            P.dma('sp', mkf, maskf_d, W=['mkf']); P.dma('sp', mkb, maskb_d, W=['mkb'])
            for j in range(8):
                outer(BTr[0], BTi[0], 0, j, ZFr[7 - j], ZFi[7 - j], Bre, Bim, False)
                outer(BTr[1], BTi[1], 1, j, ZFr[j], ZFi[j], Bre, Bim, False)
                outer(BCr, BCi, 0, j, ZNFr[j], ZNFi[j], Bre, Bim, False)
                outer(CJr[0], CJn[0], 0, j, Zr[j + 1], Zi[j + 1], Cre, Cim, True)
                outer(CJr[1], CJn[1], 1, j, Zr[8 - j], Zi[8 - j], Cre, Cim, True)
                outer(CCr[0], CCn[0], 0, j, Zr[j], Zi[j], Cre, Cim, True)
                outer(CCr[1], CCn[1], 1, j, ZNr[j], ZNi[j], Cre, Cim, True)
            m3 = lambda m_: m_.rearrange("p (k x) -> p k x", k=NPAIR)
            for d in range(2):
                for r_, src in ((0, CJr[d]), (1, CJn[d])):
                    for g2 in range(2):
                        dst = CZv[:, d, r_].rearrange("p (k t) m -> p k t m", t=2)[:, :, g2, :]
                        P.ts('dve', dst, m3(src), pm[:, g2:g2 + 1], None, ALU.mult, R=[K], W=[K])
            if sub[0] <= 0.3:
                return
            for d in range(2):
                for r_, src in ((0, BTr[d]), (1, BTi[d])):
                    for g2 in range(2):
                        rs = slice(g2 * 64, g2 * 64 + 64)
                        for (k0, nk) in blocks(0, NPAIR, 8):
                            ps, pk = nps()
                            P.mm([(ps[:, ki * 64:(ki + 1) * 64], m3(src)[rs, k0 + ki, :], ident_b[rs, rs], True, True) for ki in range(nk)],
                                 R=[K, 'ident_b'], W=[pk])
                            dst = BJv[:, d, r_].rearrange("p (k t) m -> p k t m", t=2)[:, k0:k0 + nk, g2, :]
                            P.copy('act', dst, ps[:, 0:nk * 64].rearrange("p (a m) -> p a m", m=64), R=[pk], W=['BJ'])
            if sub[0] <= 0.4:
                return
            BCL = [(BCr, BCi), (BTr[1], BTi[1])]
            for g in range(NG):
                k, g2 = g // 2, g % 2
                rs = slice(g2 * 64, g2 * 64 + 64)
                pss = []
                for d in range(2):
                    ps, pk = nps()
                    P.mm([(ps[:, 0:128], m3(BCL[d][0])[rs, k, :], m3(CCr[d])[rs, k, :], True, False),
                          (ps[:, 0:128], m3(BCL[d][1])[rs, k, :], m3(CCn[d])[rs, k, :], False, True)], R=[K], W=[pk])
                    pss.append((ps, pk))
                P.tt('dve', tm1, pss[0][0][:, 0:128], mkf, ALU.mult, R=[pss[0][1], 'mkf'], W=['tm1'])
                P.tt('dve', tm2, pss[1][0][:, 0:128], mkb, ALU.mult, R=[pss[1][1], 'mkb'], W=['tm2'])
                P.tt('dve', Mgv[:, g, :], tm1, tm2, ALU.add, R=['tm1', 'tm2'], W=['Mg'])
            if sub[0] <= 1:
                return
            P.barrier()
            ar.release(mk0)
            U8 = ar.bf16(NG * NCH); U8v = U8.rearrange("p (g c) -> p g c", g=NG)
            mk1 = ar.mark()
            es = ar.bf16(64 * 128); esv = es.rearrange("p (g j m) -> p g j m", g=8, j=8)
            cdma(es, esel_d, W=['es'])
            usb = ar.bf16(W + CTX)
            usv = usb.rearrange("p (c j) -> p c j", j=8)
            for ut in range(UT):
                r0 = POOLW + ut * 128
                cdma(usb[:, 0:W], U[r0:r0 + 128, :], R=['U'], W=['usb'])
                cdma(usb[:, W:W + CTX], U[r0:r0 + 128, 0:CTX], R=['U'], W=['usb'])
                for g8 in range(8):
                    g = ut * 8 + g8
                    for (cb, n) in blocks(0, NCH, 512):
                        ps, pk = nps()
                        P.mm([(ps[:, 0:n], esv[:, g8, j, :], usv[:, cb:cb + n, j], j == 0, j == 7) for j in range(8)],
                             R=['es', 'usb'], W=[pk])
                        P.copy('act' if g8 % 2 else 'dve', U8v[:, g, cb:cb + n], ps[:, 0:n], R=[pk], W=['U8'])
            if sub[0] <= 2:
                return
            P.barrier()
            ar.release(mk1)
            YDv = YD.rearrange("p (g c) -> p g c", g=NG)
            yst = [ar.bf16(NC1) for _ in range(2)]
            yc = [0]

            def ystage():
                i = yc[0] % 2
                yc[0] += 1
                return yst[i], ('yst', i)
            if sub[0] <= 3:
                return
            S = ar.f32(2 * PB * NC1); Sv = S.rearrange("p (r k c) -> p r k c", r=2, k=PB)
            Hb = [ar.bf16(2 * PB * (NC1 + 1)) for _ in range(2)]
            Hv = [h_.rearrange("p (r k c) -> p r k c", r=2, k=PB) for h_ in Hb]
            CM = 68
            tA = ar.f32(2 * PB * CM); tB = ar.f32(2 * PB * CM)

            def tv(t_, cnt):
                return t_[:, 0:2 * PB * cnt].rearrange("p (r k c) -> p r k c", r=2, k=PB)

            def cma(d, dst_pos, src_pos, cnt, step, Lk, bt):
                L1v = L1[d][Lk].rearrange("p (r k) -> p r k", r=2)[:, :, bt * PB:(bt + 1) * PB]
                L2v = L2[d][Lk].rearrange("p (r k) -> p r k", r=2)[:, :, bt * PB:(bt + 1) * PB]
                for m0 in range(0, cnt, CM):
                    c_ = min(CM, cnt - m0)

                    def sl(p0):
                        a = p0 + m0 * step
                        if d == 0:
                            return slice(a, a + (c_ - 1) * step + 1, step)
                        a = NC1 - 1 - a
                        e = a - (c_ - 1) * step - 1
                        return slice(a, e if e >= 0 else None, -step)
                    src = Sv[:, :, :, sl(src_pos)]
                    dst = Sv[:, :, :, sl(dst_pos)]
                    b1 = L1v.unsqueeze(3).to_broadcast([128, 2, PB, c_])
                    b2 = L2v.unsqueeze(3).to_broadcast([128, 2, PB, c_])
                    P.tt('dve', tv(tA, c_), src, b1, ALU.mult, R=['S', K], W=['tA'])
                    P.tt('pool' if c_ > 8 else 'dve', tv(tB, c_), src[:, ::-1], b2, ALU.mult, R=['S', K], W=['tB'])
                    P.tt('dve', tv(tA, c_), tv(tA, c_), tv(tB, c_), ALU.add, R=['tA', 'tB'], W=['tA'])
                    P.tt('dve', dst, dst, tv(tA, c_), ALU.add, R=['S', 'tA'], W=['S'])
            for bt in range(NB):
                for d in range(2):
                    cs = 0 if d == 0 else CCH
                    for kl in range(PB):
                        k = bt * PB + kl
                        for r_ in range(2):
                            for (cb, n) in blocks(0, NC1, 512):
                                ps, pk = nps()
                                P.mm([(ps[g2 * 64:(g2 + 1) * 64, 0:n], BJv[:, d, r_, 2 * k + g2, :], U8v[:, 2 * k + g2, cs + cb:cs + cb + n], True, True)
                                      for g2 in range(2)], R=['BJ', 'U8'], W=[pk])
                                P.copy('act' if r_ else 'dve', Sv[:, r_, kl, cb:cb + n], ps[:, 0:n], R=[pk], W=['S'])
                    for lv in range(LV):
                        s_ = 1 << lv
                        cnt = NC1 // (2 * s_)
                        cma(d, 2 * s_ - 1, s_ - 1, cnt, 2 * s_, lv, bt)
                    TT = 1 << LV
                    for m in range(1, NC1 // TT):
                        cma(d, TT * m + TT - 1, TT * m - 1, 1, TT, LV, bt)
                    for lv in range(LV - 1, -1, -1):
                        s_ = 1 << lv
                        cnt = NC1 // (2 * s_) - 1
                        if cnt > 0:
                            cma(d, 3 * s_ - 1, 2 * s_ - 1, cnt, 2 * s_, lv, bt)
                    if d == 0:
                        P.memset('pool', Hv[0][:, :, :, 0:1], 0.0, W=[('Hb', 0)])
                        P.copy('pool', Hv[0][:, :, :, 1:NC1 + 1], Sv, R=['S'], W=[('Hb', 0)])
                    else:
                        P.memset('pool', Hv[1][:, :, :, NC1:NC1 + 1], 0.0, W=[('Hb', 1)])
                        P.copy('pool', Hv[1][:, :, :, 0:NC1], Sv, R=['S'], W=[('Hb', 1)])
                for kl in range(PB):
                    k = bt * PB + kl
                    for g2 in range(2):
                        g = 2 * k + g2
                        ys, yk = ystage()
                        oblks = [(0, CCH)] + [(a, n) for a, n in blocks(CCH, NC1, 512)]
                        for (cb, n) in oblks:
                            hb0 = NC1 - CCH + 1 if cb < CCH else cb + 1 - CCH
                            ps, pk = nps()
                            P.mm([(ps[:, 0:n], Mgv[:, g, :], U8v[:, g, cb:cb + n], True, False),
                                  (ps[:, 0:n], CZv[:, 0, 0, g, :], Hv[0][:, 0, kl, cb:cb + n], False, False),
                                  (ps[:, 0:n], CZv[:, 0, 1, g, :], Hv[0][:, 1, kl, cb:cb + n], False, False),
                                  (ps[:, 0:n], CZv[:, 1, 0, g, :], Hv[1][:, 0, kl, hb0:hb0 + n], False, False),
                                  (ps[:, 0:n], CZv[:, 1, 1, g, :], Hv[1][:, 1, kl, hb0:hb0 + n], False, True)],
                                 R=[K, 'Mg', 'U8', ('Hb', 0), ('Hb', 1)], W=[pk])
                            P.copy('act', ys[:, cb:cb + n], ps[:, 0:n], R=[pk], W=[yk])
                        P.dma('sp', YDv[:, g, :], ys, R=[yk], W=['YD'])
            if sub[0] <= 4:
                return
            P.barrier()
            ar.release(0)
            Y8 = ar.bf16(NG * NC1); Y8v = Y8.rearrange("p (g c) -> p g c", g=NG)
            P.dma('sp', Y8v, YDv, R=['YD'], W=['Y8'])
            et = ar.bf16(64 * 128); etv = et.rearrange("p (g j m) -> p g j m", g=8, j=8)
            cdma(et, etsel_d, W=['et'])
            wgl = ar.bf16(UT * SSMW); wglv = wgl.rearrange("p (k n) -> p k n", k=UT)
            cdma(wglv, w_glu[l].rearrange("(k p) n -> p k n", p=128), W=['wgl'])
            sd = ar.f32(UT)
            P.dma('sp', sd, ssm_dT[l], W=['sd'])
            NE = 512
            u32 = ar.f32(UT * NE); yf = ar.f32(UT * NE); wv_ = ar.f32(UT * NE); sgm = ar.f32(UT * NE)
            geb = ar.bf16(UT * NE); sg2 = ar.f32(NE); sob = ar.bf16(UT * NE)
            for (c0, n, s) in tb512:
                r3 = lambda t_: t_[:, 0:UT * n].rearrange("p (k n) -> p k n", k=UT)
                cb0, ncq = c0 // 8, n // 8
                P.dma('sp', r3(u32), U[POOLW:D, c0:c0 + n].rearrange("(k p) n -> p k n", p=128), R=['U'], W=['u32'])
                for ut in range(UT):
                    ps, pk = nps()
                    psv = ps[:, 0:n].rearrange("p (c j) -> p c j", j=8)
                    mms = []
                    for j in range(8):
                        for g8 in range(8):
                            mms.append((psv[:, :, j], etv[:, g8, j, :], Y8v[:, ut * 8 + g8, cb0:cb0 + ncq], g8 == 0, g8 == 7))
                    P.mm(mms, R=['et', 'Y8'], W=[pk])
                    P.stt(r3(yf)[:, ut, :], r3(u32)[:, ut, :], sd[:, ut:ut + 1], ps[:, 0:n], ALU.mult, ALU.add,
                          R=['u32', 'sd', pk], W=['yf'])
                P.act(r3(wv_), r3(yf), AF.Square, R=['yf'], W=['wv'])
                P.ts('dve', r3(wv_), r3(wv_), 0.044715, 1.0, ALU.mult, ALU.add, R=['wv'], W=['wv'])
                P.tt('dve', r3(wv_), r3(wv_), r3(yf), ALU.mult, R=['wv', 'yf'], W=['wv'])
                P.act(r3(sgm), r3(wv_), AF.Sigmoid, scale=1.5957691216, R=['wv'], W=['sgm'])
                P.tt('dve', r3(yf), r3(yf), r3(sgm), ALU.mult, R=['yf', 'sgm'], W=['yf'])
                P.copy('pool', r3(geb), r3(yf), R=['yf'], W=['geb'])
                for uo in range(UT):
                    ps, pk = nps()
                    P.mm([(ps[:, 0:n], wglv[:, ui, uo * 128:(uo + 1) * 128], r3(geb)[:, ui, :], ui == 0, ui == UT - 1) for ui in range(UT)],
                         R=['wgl', 'geb'], W=[pk])
                    P.act(sg2[:, 0:n], ps[:, 0:n], AF.Sigmoid, R=[pk], W=['sg2'])
                    P.tt('dve', r3(sob)[:, uo, :], r3(yf)[:, uo, :], sg2[:, 0:n], ALU.mult, R=['yf', 'sg2'], W=['sob'])
                P.dma('sp', MIXIN[POOLW:D, c0:c0 + n].rearrange("(k p) n -> p k n", p=128), r3(sob), R=['sob'], W=['MIXIN'])

        def phase4(l):
            P.barrier()
            ar.release(0)
            wo = ar.bf16(KT * D); wov = wo.rearrange("p (k n) -> p k n", k=KT)
            for (a, n) in blocks(0, D, 512):
                cdma(wov[:, :, a:a + n], w_out[l][:, a:a + n].rearrange("(k p) n -> p k n", p=128), W=['wo'])
            NE = 256
            sets = [(ar.bf16(KT * NE), ar.f32(KT * NE), ar.f32(KT * NE)) for _ in range(2)]
            sq = ar.bf16(KT * NE); h2b = ar.bf16(KT * NE)
            rstd = ar.f32(NE)
            def ctx4(bi):
                (c0, n, s_) = tb256[bi]
                b = bi % 2
                mi, mix, xs = sets[b]
                r3 = lambda t_: t_[:, 0:KT * n].rearrange("p (k n) -> p k n", k=KT)
                return c0, n, s_, r3(mi), r3(mix), r3(xs), ('mi', b), ('mix', b), ('xs', b)

            def stageA(bi):
                c0, n, s_, miv, mixv, xsv, kmi, kmx, kxs = ctx4(bi)
                th = []

                def t0():
                    P.dma('sp', miv, MIXIN[:, c0:c0 + n].rearrange("(k p) n -> p k n", p=128), R=['MIXIN'], W=[kmi])
                    P.dma('sp', xsv, Rr[:, c0:c0 + n].rearrange("(k p) n -> p k n", p=128), R=['Rr'], W=[kxs])
                th.append(t0)
                for mt in range(KT):
                    def tm(mt=mt):
                        ps, pk = nps()
                        P.mm([(ps[:, 0:n], wov[:, kt, mt * 128:(mt + 1) * 128], miv[:, kt, :], kt == 0, kt == KT - 1) for kt in range(KT)],
                             R=['wo', kmi], W=[pk])
                        P.copy('act' if mt % 2 else 'dve', mixv[:, mt, :], ps[:, 0:n], R=[pk], W=[kmx])
                    th.append(tm)
                return th

            def stageB(bi):
                c0, n, s_, miv, mixv, xsv, kmi, kmx, kxs = ctx4(bi)
                r3 = lambda t_: t_[:, 0:KT * n].rearrange("p (k n) -> p k n", k=KT)
                rb = rstd[:, 0:n].unsqueeze(1).to_broadcast([128, KT, n])
                th = []
                th.append(lambda: rms_stats(mixv, n, sq, rstd, kmx, 'rstd'))
                th.append(lambda: P.tt('dve', mixv, mixv, rb, ALU.mult, R=[kmx, 'rstd'], W=[kmx]))
                for mt in range(KT):
                    th.append(lambda mt=mt: P.stt(xsv[:, mt, :], mixv[:, mt, :], G1v[:, l, mt, s_:s_ + 1], xsv[:, mt, :], ALU.mult, ALU.add,
                                                  R=[kmx, 'G1', kxs], W=[kxs]))
                th.append(lambda: P.dma('sp', Rr[:, c0:c0 + n].rearrange("(k p) n -> p k n", p=128), xsv, R=[kxs], W=['Rr']))
                th.append(lambda: rms_stats(xsv, n, sq, rstd, kxs, 'rstd'))
                th.append(lambda: P.tt('dve', mixv, xsv, rb, ALU.mult, R=[kxs, 'rstd'], W=[kmx]))
                for kt in range(KT):
                    th.append(lambda kt=kt: P.act(r3(h2b)[:, kt, :], mixv[:, kt, :], AF.Identity, scale=A2v[:, l, kt, s_:s_ + 1],
                                                  bias=modv[:, l, 3 * KT + kt, s_:s_ + 1], R=[kmx, 'A2', 'mod'], W=['h2b']))
                th.append(lambda: P.dma('sp', H2[:, c0:c0 + n].rearrange("(k p) n -> p k n", p=128), r3(h2b), R=['h2b'], W=['H2']))
                return th
            nb4 = len(tb256)
            pend = []
            for bi in range(nb4 + 1):
                interleave(stageA(bi) if bi < nb4 else [], pend)
                pend = stageB(bi) if bi < nb4 else []

        def phase5(l):
            P.barrier()
            ar.release(0)
            P.dma('sp', wconv[:], w_convT[l], W=['wconv'])
            ZT = (RG + 2) * GW
            h2g = ar.bf16(KT * ZT)
            wvg = [ar.bf16(2 * KT * 128) for _ in range(2)]
            dg = [ar.bf16(2 * 9 * 128) for _ in range(2)]
            acts = [ar.bf16(max(RG * GW, CTX)) for _ in range(2)]
            sgb = [ar.f32(512) for _ in range(2)]
            groups = [('ctx', 0, 0)] + [('lat', gi, gi * RG) for gi in range(NGRP)]
            qc = [0]
            pend_conv = [None]
            zpq = [[[ar.bf16(ZPW) for v in range(2)] for g in range(NGRP)] for _ in range(2)]
            for b_ in range(2):
                for g in range(NGRP):
                    for v in range(2):
                        P.memset('pool', zpq[b_][g][v], 0.0, W=[('zpq', b_, g, v)])
            for (kind, gi, r0) in groups:
                if kind == 'ctx':
                    ntz = CTX
                    zc0 = 0
                else:
                    zr0 = max(r0 - 1, 0); zr1 = min(r0 + RG + 1, ROWS)
                    ntz = (zr1 - zr0) * GW
                    zc0 = CTX + zr0 * GW
                h2v = h2g[:, 0:KT * ntz].rearrange("p (k n) -> p k n", k=KT)
                P.dma('sp', h2v, H2[:, zc0:zc0 + ntz].rearrange("(k p) n -> p k n", p=128), R=['H2'], W=['h2g'])
                first_grp = (kind == 'ctx')
                for q in range(NQ):
                    b = qc[0] % 2
                    qc[0] += 1
                    wt = wvg[b]; wk = ('wvg', b)
                    wtv = wt.rearrange("p (v k m) -> p v k m", v=2, k=KT)
                    if first_grp:
                        for v in range(2):
                            cdma(wtv[:, v], w_up[l][:, v * DFF + q * 128:v * DFF + (q + 1) * 128].rearrange("(k p) m -> p k m", p=128), W=[wk])
                        P.dma('sp', WUs[q], wt, R=[wk], W=[('WUs', q)])
                    else:
                        P.dma('sp', wt, WUs[q], R=[('WUs', q)], W=[wk])
                    dgt = dg[b]; dk = ('dg', b)
                    dgv = dgt.rearrange("p (v t m) -> p v t m", v=2, t=9)
                    for v in range(2):
                        for t_ in range(9):
                            if kind == 'ctx' and t_ // 3 != 1:
                                continue
                            P.ts('dve', dgv[:, v, t_, :], ident_f[:], wconvv[:, v * NQ + q, t_:t_ + 1], None, ALU.mult,
                                 R=['ident_f', 'wconv'], W=[dk])
                    at = acts[b]; ak = ('acts', b)
                    if kind == 'ctx':
                        for v in range(2):
                            ps, pk = nps()
                            P.mm([(ps[:, 0:CTX], wtv[:, v, kt, :], h2v[:, kt, :], kt == 0, kt == KT - 1) for kt in range(KT)],
                                 R=[wk, 'h2g'], W=[pk])
                            P.copy('act', zpc[v][:, 1:CTX + 1], ps[:, 0:CTX], R=[pk], W=[('zpc', v)])
                        pcs = []
                        for v in range(2):
                            ps, pk = nps()
                            P.mm([(ps[:, 0:CTX], dgv[:, v, 3 + dx, :], zpc[v][:, dx:dx + CTX], dx == 0, dx == 2) for dx in range(3)],
                                 R=[dk, ('zpc', v)], W=[pk])
                            pcs.append((ps, pk))
                        sb_ = sgb[0]
                        P.act(sb_[:, 0:CTX], pcs[1][0][:, 0:CTX], AF.Silu, R=[pcs[1][1]], W=[('sgb', 0)])
                        P.tt('dve', at[:, 0:CTX], pcs[0][0][:, 0:CTX], sb_[:, 0:CTX], ALU.mult, R=[pcs[0][1], ('sgb', 0)], W=[ak])
                        P.dma('pool', ACTS[q * 128:(q + 1) * 128, 0:CTX], at[:, 0:CTX], R=[ak], W=['ACTS'])
                    else:
                        zpv = [zpq[b][gi][v].rearrange("p (r c) -> p r c", c=GW + 2) for v in range(2)]
                        zk = [('zpq', b, gi, v) for v in range(2)]

                        def up_part(v, wtv=wtv, wk=wk, zpv=zpv, zk=zk):
                            for (ra, nr_) in blocks(zr0, zr1, 8):
                                n = nr_ * GW
                                ta_ = (ra - zr0) * GW
                                ps, pk = nps()
                                P.mm([(ps[:, 0:n], wtv[:, v, kt, :], h2v[:, kt, ta_:ta_ + n], kt == 0, kt == KT - 1) for kt in range(KT)],
                                     R=[wk, 'h2g'], W=[pk])
                                sl0 = ra - (r0 - 1)
                                P.copy('act', zpv[v][:, sl0:sl0 + nr_, 1:GW + 1], ps[:, 0:n].rearrange("p (r c) -> p r c", c=GW),
                                       R=[pk], W=[zk[v]])

                        def conv_part(q=q, dgv=dgv, dk=dk, zpv=zpv, zk=zk, at=at, ak=ak):
                            for oi, (ra, nr_) in enumerate(blocks(r0, r0 + RG, 8)):
                                n = nr_ * GW
                                pcs = []
                                for v in range(2):
                                    ps, pk = nps()
                                    mms = []
                                    for t_ in range(9):
                                        dy, dx = t_ // 3, t_ % 3
                                        sl0 = ra - (r0 - 1) + dy - 1
                                        mms.append((ps[:, 0:n].rearrange("p (r c) -> p r c", c=GW), dgv[:, v, t_, :],
                                                    zpv[v][:, sl0:sl0 + nr_, dx:dx + GW], t_ == 0, t_ == 8))
                                    P.mm(mms, R=[dk, zk[v]], W=[pk])
                                    pcs.append((ps, pk))
                                sb_ = sgb[oi % 2]; sk = ('sgb', oi % 2)
                                P.act(sb_[:, 0:n], pcs[1][0][:, 0:n], AF.Silu, R=[pcs[1][1]], W=[sk])
                                to = (ra - r0) * GW
                                P.tt('dve', at[:, to:to + n], pcs[0][0][:, 0:n], sb_[:, 0:n], ALU.mult, R=[pcs[0][1], sk], W=[ak])
                            c0 = CTX + r0 * GW
                            P.dma('pool', ACTS[q * 128:(q + 1) * 128, c0:c0 + RG * GW], at[:, 0:RG * GW], R=[ak], W=['ACTS'])
                        up_part(0)
                        if pend_conv[0] is not None:
                            pend_conv[0]()
                        up_part(1)
                        pend_conv[0] = conv_part
                if pend_conv[0] is not None:
                    pend_conv[0]()
                    pend_conv[0] = None

        def phase6(l):
            P.barrier()
            ar.release(0)
            last = (l == DEPTH - 1)
            ab = [ar.bf16(NQ * 512) for _ in range(2)]
            wd = [ar.bf16(NQ * 128) for _ in range(2)]
            f = ar.f32(KT * 512); xs = ar.f32(KT * 512); rstd = ar.f32(512)
            sqs = [ar.bf16(512) for _ in range(2)]
            wc = [0]
            for bi, (c0, n, s) in enumerate(tb512):
                pb_ = bi % 2
                abv = ab[pb_][:, 0:NQ * n].rearrange("p (q n) -> p q n", q=NQ)
                r3 = lambda t_: t_[:, 0:KT * n].rearrange("p (k n) -> p k n", k=KT)
                qparts = blocks(0, NQ, (NQ + 3) // 4)
                abk = [('ab', pb_, qi) for qi in range(len(qparts))]

                def load_ab(bj):
                    (cc0, nn, _s) = tb512[bj]
                    pj = bj % 2
                    av = ab[pj][:, 0:NQ * nn].rearrange("p (q n) -> p q n", q=NQ)
                    for qi, (q0, nq) in enumerate(qparts):
                        P.dma('sp', av[:, q0:q0 + nq, :], ACTS[q0 * 128:(q0 + nq) * 128, cc0:cc0 + nn].rearrange("(q p) n -> p q n", p=128),
                              R=['ACTS'], W=[('ab', pj, qi)])
                if bi == 0:
                    load_ab(0)
                if bi + 1 < len(tb512):
                    load_ab(bi + 1)
                fk = [('f', m_) for m_ in range(KT)]
                pend = None
                for mt in range(KT):
                    b = wc[0] % 2
                    wc[0] += 1
                    wt = wd[b]; wk = ('wd', b)
                    wtv = wt.rearrange("p (q m) -> p q m", q=NQ)
                    if bi == 0:
                        cdma(wtv, w_down[l][:, mt * 128:(mt + 1) * 128].rearrange("(q p) m -> p q m", p=128), W=[wk])
                        P.dma('sp', WDs[mt], wt, R=[wk], W=[('WDs', mt)])
                    else:
                        P.dma('sp', wt, WDs[mt], R=[('WDs', mt)], W=[wk])
                    ps, pk = nps()
                    P.mm([(ps[:, 0:n], wtv[:, q, :], abv[:, q, :], q == 0, q == NQ - 1) for q in range(NQ)], R=[wk] + abk, W=[pk])
                    P.copy('act' if mt % 2 else 'dve', r3(f)[:, mt, :], ps[:, 0:n], R=[pk], W=[fk[mt]])
                    if pend is not None:
                        pend()

                    def pend(mt=mt):
                        sq_ = sqs[mt % 2]; sk_ = ('sqs', mt % 2)
                        P.act(sq_[:, 0:n], r3(f)[:, mt, :], AF.Square, R=[fk[mt]], W=[sk_])
                        P.mm([(pss[:, 0:n], ones_bf[:], sq_[:, 0:n], mt == 0, mt == KT - 1)], R=[sk_, 'ones'], W=['pss'])
                pend()
                P.dma('pool', r3(xs), Rr[:, c0:c0 + n].rearrange("(k p) n -> p k n", p=128), R=['Rr'], W=['xs'])
                P.act(rstd[:, 0:n], pss[:, 0:n], AF.Sqrt, bias=EPS, scale=1.0 / D, R=['pss'], W=['rstd'])
                P.add('dve', lambda e, n=n: e.reciprocal(rstd[:, 0:n], rstd[:, 0:n]), R=['rstd'], W=['rstd'])
                P.tt('dve', r3(f), r3(f), rstd[:, 0:n].unsqueeze(1).to_broadcast([128, KT, n]), ALU.mult, R=fk + ['rstd'], W=fk)
                for mt in range(KT):
                    P.stt(r3(xs)[:, mt, :], r3(f)[:, mt, :], G2v[:, l, mt, s:s + 1], r3(xs)[:, mt, :], ALU.mult, ALU.add,
                          R=[fk[mt], 'G2', 'xs'], W=['xs'])
                if last and s == 0:
                    P.dma('pool', out[:, c0 - CTX:c0 - CTX + n].rearrange("(k p) n -> p k n", p=128), r3(xs), R=['xs'], W=['out'])
                else:
                    P.dma('pool', Rr[:, c0:c0 + n].rearrange("(k p) n -> p k n", p=128), r3(xs), R=['xs'], W=['Rr'])

        stop = cfg_stop[0]
        phase0()
        for l in range(DEPTH):
            for ph, fn in enumerate((phase1, phase2, phase3, phase4, phase5, phase6)):
                if stop is not None and (l, ph + 1) > stop:
                    break
                fn(l)
        info = P.emit(st)
        info['arena_peak'] = ar.peak
        print("build info", info, flush=True)
    return nc


cfg_stop = [None]
RG_MAX = [32]
AW_OVR = [45600]
sub = [9]


def _consts():
    ident = np.eye(128, dtype=np.float32)
    esel = np.zeros((128, 8, 8, 128), np.float32)
    etsel = np.zeros((128, 8, 8, 128), np.float32)
    for g8 in range(8):
        for j in range(8):
            for h in range(16):
                esel[g8 * 16 + h, g8, j, j * 16 + h] = 1.0
                etsel[j * 16 + h, g8, j, g8 * 16 + h] = 1.0
    jj = np.arange(128) // 16
    maskf = (jj[None, :] >= jj[:, None]).astype(np.float32)
    maskb = (jj[:, None] >= jj[None, :]).astype(np.float32)
    iet = np.zeros((4, 16), np.float32)
    for wi, w in enumerate((2, 4, 8, 16)):
        for t in range(8):
            iet[wi, t] = 1.0 / (t + w // 2 if t < w // 2 else w)
            e = 7 - t
            iet[wi, 8 + t] = 1.0 / (e + 1 + w // 2 if e <= w // 2 - 1 else w)
    ietab = np.broadcast_to(iet.reshape(1, 64), (128, 64)).copy()
    return dict(ident=ident, esel=esel.reshape(128, -1), etsel=etsel.reshape(128, -1), maskf=maskf, maskb=maskb, ietab=ietab)


def _prep_shared(cfg, inp):
    D, L, CTX, DFF, DEPTH, GW = cfg
    KT = D // 128
    f = lambda a: np.ascontiguousarray(np.asarray(a, dtype=np.float32))
    NG = (D // 4) // 16
    NPAIR = NG // 2
    sh = {}
    for k in ("w_ada", "w_in", "w_pool", "w_glu", "w_out", "w_up", "w_down"):
        sh[k] = f(inp[k])
    sh["b_adaT"] = f(np.asarray(inp["b_ada"]).reshape(DEPTH, 6 * KT, 128).transpose(0, 2, 1))
    sh["pscT"] = f(np.asarray(inp["pool_scale"]).reshape(DEPTH, -1, 128).transpose(0, 2, 1))

    def statelay(a):
        a = np.asarray(a).reshape(DEPTH, 2, NPAIR, 2, 64)
        return f(a.transpose(0, 3, 4, 1, 2).reshape(DEPTH, 128, 2 * NPAIR))
    sh["a_reT"] = statelay(inp["ssm_a_re"])
    sh["a_imT"] = statelay(inp["ssm_a_im"])
    ldt = np.broadcast_to(np.asarray(inp["ssm_log_dt"])[..., None], (DEPTH, 2, NG, 64))
    sh["ldtT"] = statelay(ldt)

    def blay(a):
        a = np.asarray(a).reshape(DEPTH, 2, NPAIR, 2, 64, 16)
        return f(a.transpose(0, 3, 4, 1, 2, 5).reshape(DEPTH, 128, 2 * NPAIR * 16))

    def clay(a):
        a = np.asarray(a).reshape(DEPTH, 2, NPAIR, 2, 16, 64)
        return f(a.transpose(0, 3, 5, 1, 2, 4).reshape(DEPTH, 128, 2 * NPAIR * 16))
    sh["bT_re"] = blay(inp["ssm_b_re"]); sh["bT_im"] = blay(inp["ssm_b_im"])
    sh["cT_re"] = clay(inp["ssm_c_re"]); sh["cT_im"] = clay(inp["ssm_c_im"])
    sh["ssm_dT"] = f(np.asarray(inp["ssm_d"]).reshape(DEPTH, -1, 128).transpose(0, 2, 1))
    g = np.stack([np.asarray(inp[k]) for k in ("g_pre_mix", "g_post_mix", "g_pre_ffn", "g_post_ffn")], 1)
    sh["gT"] = f(g.reshape(DEPTH, 4, KT, 128).transpose(3, 0, 1, 2).reshape(128, DEPTH * 4 * KT))
    wc = np.asarray(inp["w_conv"]).reshape(DEPTH, 9, -1, 128)
    sh["w_convT"] = f(wc.transpose(0, 3, 2, 1).reshape(DEPTH, 128, -1))
    sh.update(_consts())
    return sh


def run_cfg(cfg, inp, ncores, dbg=False, stop=None):
    D, L, CTX, DFF, DEPTH, GW = cfg
    KT = D // 128
    cfg_stop[0] = stop
    nc = build(cfg, dbg=dbg)
    sh = _prep_shared(cfg, inp)
    x = np.asarray(inp["x"], dtype=np.float32); c = np.asarray(inp["c"], dtype=np.float32)
    ctx = np.asarray(inp["ctx"], dtype=np.float32); c_ctx = np.asarray(inp["c_ctx"], dtype=np.float32)
    in_maps = []
    for b in range(ncores):
        m = dict(sh)
        m["xT"] = np.ascontiguousarray(x[b].T)
        m["ctxT"] = np.ascontiguousarray(ctx[b].T)
        ccv = np.stack([c[b].reshape(KT, 128), c_ctx.reshape(KT, 128)], -1)
        m["cc"] = np.ascontiguousarray(ccv.transpose(1, 0, 2).reshape(128, KT * 2))
        in_maps.append(m)
    res = run_bass_kernel_spmd(nc, in_maps, core_ids=list(range(ncores)))
    return res


def kernel(**inputs):
    cfg = (2048, 4096, 256, 5632, 4, 64)
    res = run_cfg(cfg, inputs, 8)
    outs = [np.ascontiguousarray(res.results[b]["out"].T) for b in range(8)]
    return np.stack(outs, 0).astype(np.float32)
```

```python
import contextlib
import numpy as np
import concourse.bass as bass
import concourse.mybir as mybir

F32 = mybir.dt.float32
BF16 = mybir.dt.bfloat16
AF = mybir.ActivationFunctionType
ALU = mybir.AluOpType

RING = 16


class Prog:
    def __init__(self, nc):
        self.nc = nc
        self.ops = []

    def add(self, eng, fn, R=(), W=(), dma=False):
        self.ops.append(dict(eng=eng, fn=fn, R=tuple(R), W=tuple(W), dma=dma, barrier=False))

    def barrier(self):
        self.ops.append(dict(eng=None, fn=None, R=(), W=(), dma=False, barrier=True))

    def dma(self, q, out, in_, R=(), W=(), **kw):
        self.add(q, lambda e: e.dma_start(out=out, in_=in_, **kw), R, W, dma=True)

    def mm(self, mms, R=(), W=()):
        def fn(e):
            ins = None
            for (o, l, r, st, sp) in mms:
                ins = e.matmul(o, l, r, start=st, stop=sp)
            return ins
        self.add('pe', fn, R, W)

    def act(self, out, in_, func, R=(), W=(), **kw):
        self.add('act', lambda e: e.activation(out, in_, func, **kw), R, W)

    def tt(self, eng, out, in0, in1, op, R=(), W=()):
        self.add(eng, lambda e: e.tensor_tensor(out, in0, in1, op), R, W)

    def ts(self, eng, out, in0, s1, s2, op0, op1=None, R=(), W=()):
        if op1 is None:
            self.add(eng, lambda e: e.tensor_scalar(out, in0, s1, None, op0), R, W)
        else:
            self.add(eng, lambda e: e.tensor_scalar(out, in0, s1, s2, op0, op1), R, W)

    def stt(self, out, in0, scalar, in1, op0, op1, R=(), W=()):
        self.add('dve', lambda e: e.scalar_tensor_tensor(out, in0, scalar, in1, op0, op1), R, W)

    def copy(self, eng, out, in_, R=(), W=()):
        if eng == 'act':
            self.add('act', lambda e: e.activation(out, in_, AF.Copy), R, W)
        else:
            self.add(eng, lambda e: e.tensor_copy(out, in_), R, W)

    def memset(self, eng, ap, val, W=()):
        self.add(eng, lambda e: e.memset(ap, val), (), W)

    def emit(self, stack):
        nc = self.nc
        ops = self.ops
        n = len(ops)
        last_w = {}
        readers = {}
        deps = [None] * n
        engs = ['pe', 'act', 'dve', 'pool', 'sp']
        last_op = {e: None for e in engs}
        last_dmas = {e: [] for e in engs}
        need_bar = {e: set() for e in engs}
        for i, op in enumerate(ops):
            if op['barrier']:
                bd = set()
                for e in engs:
                    if last_op[e] is not None:
                        bd.add(last_op[e])
                    bd.update(last_dmas[e])
                for e in engs:
                    need_bar[e] |= bd
                deps[i] = set()
                continue
            d = set(need_bar[op['eng']])
            need_bar[op['eng']] = set()
            last_op[op['eng']] = i
            if op['dma']:
                last_dmas[op['eng']] = (last_dmas[op['eng']] + [i])[-RING:]
            for k in op['R']:
                if k in last_w:
                    d.add(last_w[k])
            for k in op['W']:
                if k in last_w:
                    d.add(last_w[k])
                for r in readers.get(k, ()):
                    d.add(r)
            d.discard(i)
            for k in op['W']:
                last_w[k] = i
                readers[k] = []
            for k in op['R']:
                readers.setdefault(k, []).append(i)
            deps[i] = d
        signal = [False] * n
        for i, op in enumerate(ops):
            if op['barrier']:
                continue
            nd = set()
            for j in deps[i]:
                if ops[j]['eng'] == 'pe' and op['eng'] == 'pe' and not ops[j]['dma'] and not op['dma']:
                    continue
                nd.add(j)
            deps[i] = nd
            for j in nd:
                signal[j] = True
        SEG = 30000
        cnt = {e: 0 for e in engs}
        dcnt = {e: 0 for e in engs}
        tok = [None] * n
        sems = {}

        def getsem(key):
            if key not in sems:
                sems[key] = stack.enter_context(nc.semaphore(name="s_%s" % "_".join(str(x) for x in key)))
            return sems[key]

        for i, op in enumerate(ops):
            e = op['eng']
            if op['barrier']:
                continue
            if op['dma']:
                k = dcnt[e]
                dcnt[e] += 1
                op['dslot'] = k
                tok[i] = (('d', e, k % RING), 16 * (k // RING + 1))
            elif signal[i]:
                c = cnt[e]
                cnt[e] += 1
                tok[i] = (('c', e, c // SEG), c % SEG + 1)
        for i in range(n):
            if tok[i] is not None:
                getsem(tok[i][0])
        by_eng = {e: [] for e in engs}
        for i, op in enumerate(ops):
            if not op['barrier']:
                by_eng[op['eng']].append(i)
        engobj = {}
        block = stack.enter_context(nc.Block())

        def run(e, eng):
            waited = {}
            def wait(t):
                key, val = t
                if waited.get(key, 0) >= val:
                    return
                waited[key] = val
                eng.wait_ge(getsem(key), val)
            for i in by_eng[e]:
                op = ops[i]
                for j in sorted(deps[i]):
                    wait(tok[j])
                if op['dma']:
                    k = op['dslot']
                    if k >= RING:
                        wait((('d', e, k % RING), 16 * (k // RING)))
                ins = op['fn'](eng)
                if tok[i] is not None:
                    key, val = tok[i]
                    assert ins is not None, "op returned no instruction"
                    ins.then_inc(getsem(key), 16 if op['dma'] else 1)
            k = dcnt[e]
            for s in range(RING):
                m = (k - s + RING - 1) // RING if k > s else 0
                if m > 0:
                    wait((('d', e, s), 16 * m))

        @block.tensor
        def _(eng):
            run('pe', eng)

        @block.scalar
        def _(eng):
            run('act', eng)

        @block.vector
        def _(eng):
            run('dve', eng)

        @block.gpsimd
        def _(eng):
            run('pool', eng)

        @block.sync
        def _(eng):
            run('sp', eng)
        return dict(n_ops=n, cnt=cnt, dcnt=dcnt, nsems=len(sems))
from concourse.bass_utils import run_bass_kernel_spmd

import math


class Arena:
    def __init__(self, ap32, nwords):
        self.ap = ap32
        self.n = nwords
        self.off = 0
        self.peak = 0

    def mark(self):
        return self.off

    def release(self, m):
        self.off = m

    def _take(self, nw):
        o = self.off
        self.off += nw
        self.peak = max(self.peak, self.off)
        assert self.off <= self.n, "arena overflow %d > %d" % (self.off, self.n)
        return o

    def f32(self, n):
        o = self._take(n)
        return self.ap[:, o:o + n]

    def bf16(self, n):
        nw = (n + 1) // 2
        o = self._take(nw)
        return self.ap[:, o:o + nw].bitcast(BF16)[:, 0:n]


def blocks(c0, c1, step):
    return [(a, min(step, c1 - a)) for a in range(c0, c1, step)]


def build(cfg, dbg=False):
    D, L, CTX, DFF, DEPTH, GW = cfg
    nc = bass.Bass("TRN2", target_bir_lowering=False)
    KT = D // 128
    W = CTX + L
    NQ = DFF // 128
    NJ = 2 * NQ
    SSMW = D // 4
    POOLW = D - SSMW
    UT = SSMW // 128
    PG = POOLW // 4
    PGT = PG // 128
    NPAIR = SSMW // 32
    NG = SSMW // 16
    CCH = CTX // 8
    NC1 = W // 8
    NCH = NC1 + CCH
    ROWS = L // GW
    RG = min(RG_MAX[0], ROWS)
    NGRP = ROWS // RG
    EPS = 1e-6
    NB = 2 if NPAIR >= 2 else 1
    LV = 0
    while LV < 5 and NC1 % (2 << LV) == 0:
        LV += 1
    PB = NPAIR // NB

    def din(name, shape, dt=F32):
        return nc.dram_tensor(name, list(shape), dt, kind="ExternalInput").ap()

    def dscr(name, shape, dt):
        return nc.dram_tensor(name, list(shape), dt, kind=("ExternalOutput" if dbg else "Internal")).ap()

    xT = din("xT", [D, L]); ctxT = din("ctxT", [D, CTX]); cc = din("cc", [128, KT * 2])
    w_ada = din("w_ada", [DEPTH, D, 6 * D]); b_adaT = din("b_adaT", [DEPTH, 128, 6 * KT])
    w_in = din("w_in", [DEPTH, D, D]); w_pool = din("w_pool", [DEPTH, 4, PG, PG]); pscT = din("pscT", [DEPTH, 128, POOLW // 128])
    a_reT = din("a_reT", [DEPTH, 128, 2 * NPAIR]); a_imT = din("a_imT", [DEPTH, 128, 2 * NPAIR]); ldtT = din("ldtT", [DEPTH, 128, 2 * NPAIR])
    bT_re = din("bT_re", [DEPTH, 128, 2 * NPAIR * 16]); bT_im = din("bT_im", [DEPTH, 128, 2 * NPAIR * 16])
    cT_re = din("cT_re", [DEPTH, 128, 2 * NPAIR * 16]); cT_im = din("cT_im", [DEPTH, 128, 2 * NPAIR * 16])
    ssm_dT = din("ssm_dT", [DEPTH, 128, UT]); w_glu = din("w_glu", [DEPTH, SSMW, SSMW]); w_out = din("w_out", [DEPTH, D, D])
    gT = din("gT", [128, DEPTH * 4 * KT])
    w_up = din("w_up", [DEPTH, D, 2 * DFF]); w_convT = din("w_convT", [DEPTH, 128, NJ * 9]); w_down = din("w_down", [DEPTH, DFF, D])
    ident_d = din("ident", [128, 128]); esel_d = din("esel", [128, 8 * 8 * 128]); etsel_d = din("etsel", [128, 8 * 8 * 128])
    maskf_d = din("maskf", [128, 128]); maskb_d = din("maskb", [128, 128]); ietab_d = din("ietab", [128, 4 * 16])
    out = nc.dram_tensor("out", [D, L], F32, kind="ExternalOutput").ap()

    Rr = dscr("Rr", [D, W], F32)
    U = dscr("U", [D, W], F32)
    MIXIN = dscr("MIXIN", [D, W], BF16)
    H2 = dscr("H2", [D, W], BF16)
    ACTS = dscr("ACTS", [DFF, W], BF16)
    WUs = nc.dram_tensor("WUs", [NQ, 128, 2 * KT * 128], BF16, kind="Internal").ap()
    WDs = nc.dram_tensor("WDs", [KT, 128, NQ * 128], BF16, kind="Internal").ap()
    YD = nc.dram_tensor("YD", [128, NG * NC1], BF16, kind="Internal").ap()
    MODd = dscr("MODd", [128, DEPTH * 6 * KT * 2], F32)

    st = contextlib.ExitStack()
    with st:
        def T(name, shape, dt):
            return st.enter_context(nc.sbuf_tensor(name, list(shape), dt))
        AW = AW_OVR[0]
        arena_t = T("arena", [128, AW], F32)
        ar = Arena(arena_t[:], AW)
        mod = T("mod", [128, DEPTH * 6 * KT * 2], F32)
        modv = mod[:].rearrange("p (l n s) -> p l n s", l=DEPTH, s=2)
        A1 = T("A1", [128, DEPTH * KT * 2], F32); G1 = T("G1", [128, DEPTH * KT * 2], F32)
        A2 = T("A2", [128, DEPTH * KT * 2], F32); G2 = T("G2", [128, DEPTH * KT * 2], F32)
        v4 = lambda t: t[:].rearrange("p (l k s) -> p l k s", l=DEPTH, s=2)
        A1v, G1v, A2v, G2v = v4(A1), v4(G1), v4(A2), v4(G2)
        gsb = T("gsb", [128, DEPTH * 4 * KT], F32)
        gv = gsb[:].rearrange("p (l w k) -> p l w k", l=DEPTH, w=4)
        ones_bf = T("ones_bf", [128, 128], BF16)
        ident_f = T("ident_f", [128, 128], F32)
        ident_b = T("ident_b", [128, 128], BF16)
        wconv = T("wconv", [128, NJ * 9], F32)
        wconvv = wconv[:].rearrange("p (j t) -> p j t", t=9)
        ZPW = (RG + 2) * (GW + 2)
        zp = [[T("zp%d_%d" % (g, v), [128, ZPW], BF16) for v in range(2)] for g in range(NGRP)]
        zpc = [T("zpc%d" % v, [128, CTX + 2], BF16) for v in range(2)]
        NPS = 7
        psb = [st.enter_context(nc.psum_tensor("ps%d" % i, [128, 512], F32)) for i in range(NPS)]
        pss = st.enter_context(nc.psum_tensor("pss", [128, 512], F32))
        P = Prog(nc)
        psi = [0]

        def nps():
            i = psi[0] % NPS
            psi[0] += 1
            return psb[i], ('ps', i)

        def cdma(out_, in_, R=(), W=()):
            P.dma('pool', out_, in_, R=R, W=W, max_dma_last_dim=4096)

        tb512 = [(a, n, 1) for a, n in blocks(0, CTX, 512)] + [(a, n, 0) for a, n in blocks(CTX, W, 512)]
        tb256 = [(a, n, 1) for a, n in blocks(0, CTX, 256)] + [(a, n, 0) for a, n in blocks(CTX, W, 256)]

        P.memset('dve', ones_bf[:], 1.0, W=['ones'])
        P.dma('sp', ident_f[:], ident_d, W=['ident_f'])
        cdma(ident_b[:], ident_d, W=['ident_b'])
        P.dma('sp', gsb[:], gT, W=['gsb'])
        for g in range(NGRP):
            for v in range(2):
                P.memset('pool', zp[g][v][:], 0.0, W=[('zp', g, v)])
        for v in range(2):
            P.memset('pool', zpc[v][:], 0.0, W=[('zpc', v)])
        P.dma('sp', Rr[:, 0:CTX], ctxT, W=['Rr'])
        for (a, n) in blocks(0, L, 1024):
            P.dma('sp', Rr[:, CTX + a:CTX + a + n], xT[:, a:a + n], W=['Rr'])

        def phase0():
            P.barrier()
            ar.release(0)
            ccs = ar.f32(KT * 2)
            scb = ar.bf16(KT * 2)
            scbv = scb.rearrange("p (k s) -> p k s", s=2)
            P.dma('sp', ccs, cc, W=['ccs'])
            P.act(scb, ccs, AF.Silu, R=['ccs'], W=['scb'])
            bada = ar.f32(6 * KT)
            wa = [ar.bf16(KT * 512) for _ in range(2)]
            NT6 = 6 * KT
            cnt = 0
            for l in range(DEPTH):
                P.dma('sp', bada, b_adaT[l], W=['bada'])
                pm, pmk = nps()
                for nb in range(NT6 // 4):
                    wt = wa[cnt % 2]
                    wk = ('wa', cnt % 2)
                    cnt += 1
                    wtv = wt.rearrange("p (k n) -> p k n", k=KT)
                    cdma(wtv, w_ada[l][:, nb * 512:(nb + 1) * 512].rearrange("(k p) n -> p k n", p=128), W=[wk])
                    for nt in range(4):
                        n_ = nb * 4 + nt
                        P.mm([(pm[:, 2 * n_:2 * n_ + 2], wtv[:, kt, nt * 128:(nt + 1) * 128], scbv[:, kt, :], kt == 0, kt == KT - 1)
                              for kt in range(KT)], R=[wk, 'scb'], W=[pmk])
                P.tt('dve', modv[:, l], pm[:, 0:2 * NT6].rearrange("p (n s) -> p n s", s=2),
                     bada.unsqueeze(2).to_broadcast([128, NT6, 2]), ALU.add, R=[pmk, 'bada'], W=['mod'])
                gb = lambda w_: gv[:, l, w_, :].unsqueeze(2).to_broadcast([128, KT, 2])
                P.stt(A1v[:, l], modv[:, l, KT:2 * KT, :], 1.0, gb(0), ALU.add, ALU.mult, R=['mod', 'gsb'], W=['A1'])
                P.tt('dve', G1v[:, l], modv[:, l, 2 * KT:3 * KT, :], gb(1), ALU.mult, R=['mod', 'gsb'], W=['G1'])
                P.stt(A2v[:, l], modv[:, l, 4 * KT:5 * KT, :], 1.0, gb(2), ALU.add, ALU.mult, R=['mod', 'gsb'], W=['A2'])
                P.tt('dve', G2v[:, l], modv[:, l, 5 * KT:6 * KT, :], gb(3), ALU.mult, R=['mod', 'gsb'], W=['G2'])
            if dbg:
                P.dma('sp', MODd, mod[:], R=['mod'], W=['MODd'])

        def rms_stats(src3, n, sq, rstd, Rk, Wk, sqk='sq'):
            sqv = sq[:, 0:KT * n].rearrange("p (k n) -> p k n", k=KT)
            P.act(sqv, src3, AF.Square, R=[Rk], W=[sqk])
            ps, pk = nps()
            P.mm([(ps[:, 0:n], ones_bf[:], sqv[:, kt, :], kt == 0, kt == KT - 1) for kt in range(KT)], R=[sqk, 'ones'], W=[pk])
            P.act(rstd[:, 0:n], ps[:, 0:n], AF.Sqrt, bias=EPS, scale=1.0 / D, R=[pk], W=[Wk])
            P.add('dve', lambda e: e.reciprocal(rstd[:, 0:n], rstd[:, 0:n]), R=[Wk], W=[Wk])

        def interleave(a_th, b_th):
            na, nb = len(a_th), len(b_th)
            bi_ = 0
            for k_, th in enumerate(a_th):
                th()
                upto = ((k_ + 1) * nb) // max(na, 1)
                while bi_ < upto:
                    b_th[bi_]()
                    bi_ += 1
            while bi_ < nb:
                b_th[bi_]()
                bi_ += 1

        def phase1(l):
            P.barrier()
            ar.release(0)
            win = ar.bf16(KT * D)
            winv = win.rearrange("p (k n) -> p k n", k=KT)
            for (a, n) in blocks(0, D, 512):
                cdma(winv[:, :, a:a + n], w_in[l][:, a:a + n].rearrange("(k p) n -> p k n", p=128), W=['win'])
            NE = 256
            sets = [(ar.f32(KT * NE), ar.bf16(KT * NE), ar.bf16(KT * NE), ar.f32(NE)) for _ in range(2)]
            def ctx1(bi):
                (c0, n, s_) = tb256[bi]
                b = bi % 2
                xs, sq, hb, rstd = sets[b]
                kx, kq, kh, kr = ('xs', b), ('sq', b), ('hb', b), ('rstd', b)
                xsv = xs[:, 0:KT * n].rearrange("p (k n) -> p k n", k=KT)
                hbv = hb[:, 0:KT * n].rearrange("p (k n) -> p k n", k=KT)
                return c0, n, s_, xs, sq, hb, rstd, kx, kq, kh, kr, xsv, hbv

            def prologue(bi):
                c0, n, s_, xs, sq, hb, rstd, kx, kq, kh, kr, xsv, hbv = ctx1(bi)
                th = []
                th.append(lambda: P.dma('sp', xsv, Rr[:, c0:c0 + n].rearrange("(k p) n -> p k n", p=128), R=['Rr'], W=[kx]))
                th.append(lambda: rms_stats(xsv, n, sq, rstd, kx, kr, kq))
                th.append(lambda: P.tt('dve', xsv, xsv, rstd[:, 0:n].unsqueeze(1).to_broadcast([128, KT, n]), ALU.mult, R=[kx, kr], W=[kx]))
                for kt in range(KT):
                    th.append(lambda kt=kt: P.act(hbv[:, kt, :], xsv[:, kt, :], AF.Identity, scale=A1v[:, l, kt, s_:s_ + 1],
                                                  bias=modv[:, l, kt, s_:s_ + 1], R=[kx, 'A1', 'mod'], W=[kh]))
                return th

            def main(bi):
                c0, n, s_, xs, sq, hb, rstd, kx, kq, kh, kr, xsv, hbv = ctx1(bi)
                th = []
                for mt in range(KT):
                    def tm(mt=mt):
                        ps, pk = nps()
                        P.mm([(ps[:, 0:n], winv[:, kt, mt * 128:(mt + 1) * 128], hbv[:, kt, :], kt == 0, kt == KT - 1) for kt in range(KT)],
                             R=['win', kh], W=[pk])
                        P.copy('act' if mt % 2 else 'dve', xsv[:, mt, :], ps[:, 0:n], R=[pk], W=[kx])
                    th.append(tm)
                th.append(lambda: P.dma('sp', U[:, c0:c0 + n].rearrange("(k p) n -> p k n", p=128), xsv, R=[kx], W=['U']))
                return th
            for th in prologue(0):
                th()
            for bi in range(len(tb256)):
                interleave(main(bi), prologue(bi + 1) if bi + 1 < len(tb256) else [])

        def phase2(l):
            P.barrier()
            ar.release(0)
            PAD = 16
            oA = PAD
            oB = PAD + CTX + 2 * PAD
            WP = CTX + L + 4 * PAD
            segs = [(oA, 0, CTX), (oB, CTX, L)]
            wp = ar.bf16(PGT * PG); wpv = wp.rearrange("p (k n) -> p k n", k=PGT)
            pb = ar.bf16(PGT * W); pbv = pb.rearrange("p (k n) -> p k n", k=PGT)
            psets = [(ar.f32(WP), ar.f32(WP), ar.f32(WP)) for _ in range(2)]
            pob = ar.bf16(W)
            iet = ar.f32(64); ietv = iet.rearrange("p (w e) -> p w e", w=4)
            psc = ar.f32(POOLW // 128)
            t8 = ar.f32(8)
            P.dma('sp', iet, ietab_d, W=['iet'])
            P.dma('sp', psc, pscT[l], W=['psc'])
            for b_ in range(2):
                P.memset('dve', psets[b_][0], 0.0, W=[('up', b_)])
            for g in range(4):
                w = (2, 4, 8, 16)[g]
                cdma(wpv, w_pool[l][g].rearrange("(k p) n -> p k n", p=128), W=['wp'])
                for k3 in range(PGT):
                    ct = g * PGT + k3
                    up, ta, tb_ = psets[ct % 2]
                    ku, ka, kb = ('up', ct % 2), ('ta', ct % 2), ('tb', ct % 2)
                    for (o, c0, n) in segs:
                        P.dma('sp', up[:, o:o + n], U[ct * 128:(ct + 1) * 128, c0:c0 + n], R=['U'], W=[ku])
                    eng = 'dve' if ct % 2 == 0 else 'pool'
                    P.tt(eng, ta[:, 1:WP], up[:, 1:WP], up[:, 0:WP - 1], ALU.add, R=[ku], W=[ka])
                    if w == 2:
                        S = ta; Sk = ka
                    elif w == 4:
                        P.tt(eng, tb_[:, 2:WP - 1], ta[:, 1:WP - 2], ta[:, 3:WP], ALU.add, R=[ka], W=[kb])
                        S = tb_; Sk = kb
                    elif w == 8:
                        P.tt(eng, tb_[:, 3:WP], ta[:, 3:WP], ta[:, 1:WP - 2], ALU.add, R=[ka], W=[kb])
                        P.tt(eng, ta[:, 4:WP - 3], tb_[:, 3:WP - 4], tb_[:, 7:WP], ALU.add, R=[kb], W=[ka])
                        S = ta; Sk = ka
                    else:
                        P.tt(eng, tb_[:, 3:WP], ta[:, 3:WP], ta[:, 1:WP - 2], ALU.add, R=[ka], W=[kb])
                        P.tt(eng, ta[:, 7:WP], tb_[:, 7:WP], tb_[:, 3:WP - 4], ALU.add, R=[kb], W=[ka])
                        P.tt(eng, tb_[:, 8:WP - 7], ta[:, 7:WP - 8], ta[:, 15:WP], ALU.add, R=[ka], W=[kb])
                        S = tb_; Sk = kb
                    for (o, c0, n) in segs:
                        P.stt(pbv[:, k3, c0:c0 + n], S[:, o:o + n], 1.0 / w, up[:, o:o + n], ALU.mult, ALU.subtract,
                              R=[Sk, ku], W=['pb'])
                        for (eo, to) in ((0, 0), (n - 8, 8)):
                            P.tt('dve', t8, S[:, o + eo:o + eo + 8], ietv[:, g, to:to + 8], ALU.mult, R=[Sk, 'iet'], W=['t8'])
                            P.tt('dve', pbv[:, k3, c0 + eo:c0 + eo + 8], t8, up[:, o + eo:o + eo + 8], ALU.subtract,
                                 R=['t8', ku], W=['pb'])
                for m3 in range(PGT):
                    for (c0, n, s) in tb512:
                        ps, pk = nps()
                        P.mm([(ps[:, 0:n], wpv[:, k3, m3 * 128:(m3 + 1) * 128], pbv[:, k3, c0:c0 + n], k3 == 0, k3 == PGT - 1)
                              for k3 in range(PGT)], R=['wp', 'pb'], W=[pk])
                        P.act(pob[:, c0:c0 + n], ps[:, 0:n], AF.Identity, scale=psc[:, g * PGT + m3:g * PGT + m3 + 1],
                              R=[pk, 'psc'], W=['pob'])
                    r0 = (g * PGT + m3) * 128
                    P.dma('sp', MIXIN[r0:r0 + 128, :], pob, R=['pob'], W=['MIXIN'])

        def phase3(l):
            P.barrier()
            ar.release(0)
            N2 = 2 * NPAIR
            CZ = ar.bf16(2 * 2 * NG * 128); CZv = CZ.rearrange("p (d r g m) -> p d r g m", d=2, r=2, g=NG)
            BJ = ar.bf16(2 * 2 * NG * 64); BJv = BJ.rearrange("p (d r g m) -> p d r g m", d=2, r=2, g=NG)
            Mg = ar.bf16(NG * 128); Mgv = Mg.rearrange("p (g m) -> p g m", g=NG)
            L1 = [[ar.f32(2 * NPAIR) for _ in range(LV + 1)] for _ in range(2)]; L2 = [[ar.f32(2 * NPAIR) for _ in range(LV + 1)] for _ in range(2)]
            pm = ar.f32(2)
            mk0 = ar.mark()

            def sm(n=N2):
                return ar.f32(n)
            are = sm(); aim = sm(); ldt = sm(); dt = sm(); th = sm(); mag = sm()
            c_ = sm(); s_ = sm(); t1 = sm(); t2 = sm(); t3 = sm(); lre = sm(); lim = sm(); fre = sm(); fim = sm()
            P.dma('sp', are, a_reT[l], W=['are']); P.dma('sp', aim, a_imT[l], W=['aim']); P.dma('sp', ldt, ldtT[l], W=['ldt'])
            K = 'gen'

            def tt(o, a, b, op, eng='dve'):
                P.tt(eng, o, a, b, op, R=[K], W=[K])
            P.act(dt, ldt, AF.Exp, R=['ldt'], W=[K])
            P.tt('dve', th, aim, dt, ALU.mult, R=['aim', K], W=[K])
            P.tt('dve', t1, are, dt, ALU.mult, R=['are', K], W=[K])
            P.act(mag, t1, AF.Exp, R=[K], W=[K])
            hp = ar.f32(1)
            P.memset('dve', hp, math.pi / 2, W=[K])
            P.act(c_, th, AF.Sin, scale=1.0 / 16, bias=hp[:, 0:1], R=[K], W=[K])
            P.act(s_, th, AF.Sin, scale=1.0 / 16, R=[K], W=[K])
            for _ in range(4):
                tt(t1, c_, c_, ALU.mult); tt(t2, s_, s_, ALU.mult); tt(t3, c_, s_, ALU.mult)
                tt(c_, t1, t2, ALU.subtract)
                P.ts('dve', s_, t3, 2.0, None, ALU.mult, R=[K], W=[K])
            tt(lre, mag, c_, ALU.mult); tt(lim, mag, s_, ALU.mult)
            if sub[0] <= 0.1:
                return
            nr = sm(); den = sm()
            P.ts('dve', nr, lre, -1.0, None, ALU.add, R=[K], W=[K])
            tt(t1, are, are, ALU.mult); tt(t2, aim, aim, ALU.mult); tt(den, t1, t2, ALU.add)
            P.add('dve', lambda e: e.reciprocal(den, den), R=[K], W=[K])
            tt(t1, nr, are, ALU.mult); tt(t2, lim, aim, ALU.mult); tt(t1, t1, t2, ALU.add); tt(fre, t1, den, ALU.mult)
            tt(t1, lim, are, ALU.mult); tt(t2, nr, aim, ALU.mult); tt(t1, t1, t2, ALU.subtract); tt(fim, t1, den, ALU.mult)

            def cmul(ore, oim, ar_, ai_, br_, bi_):
                tt(t1, ar_, br_, ALU.mult); tt(t2, ai_, bi_, ALU.mult); tt(ore, t1, t2, ALU.subtract)
                tt(t1, ar_, bi_, ALU.mult); tt(t2, ai_, br_, ALU.mult); tt(oim, t1, t2, ALU.add)
            Zr = [sm() for _ in range(9)]; Zi = [sm() for _ in range(9)]
            ZNr = [sm() for _ in range(8)]; ZNi = [sm() for _ in range(8)]
            P.memset('dve', Zr[0], 1.0, W=[K]); P.memset('dve', Zi[0], 0.0, W=[K])
            P.memset('dve', ZNr[0], 1.0, W=[K]); P.memset('dve', ZNi[0], 0.0, W=[K])
            for e_ in range(1, 9):
                cmul(Zr[e_], Zi[e_], Zr[e_ - 1], Zi[e_ - 1], lre, lim)
            lir = sm(); lii = sm()
            tt(t1, lre, lre, ALU.mult); tt(t2, lim, lim, ALU.mult); tt(t3, t1, t2, ALU.add)
            P.add('dve', lambda e: e.reciprocal(t3, t3), R=[K], W=[K])
            tt(lir, lre, t3, ALU.mult)
            P.stt(lii, lim, -1.0, t3, ALU.mult, ALU.mult, R=[K], W=[K])
            for e_ in range(1, 8):
                cmul(ZNr[e_], ZNi[e_], ZNr[e_ - 1], ZNi[e_ - 1], lir, lii)
            ZFr = [sm() for _ in range(8)]; ZFi = [sm() for _ in range(8)]
            ZNFr = [sm() for _ in range(8)]; ZNFi = [sm() for _ in range(8)]
            for e_ in range(8):
                cmul(ZFr[e_], ZFi[e_], Zr[e_], Zi[e_], fre, fim)
                cmul(ZNFr[e_], ZNFi[e_], ZNr[e_], ZNi[e_], fre, fim)
            LPr = [Zr[8]] + [sm() for _ in range(LV)]; LPi = [Zi[8]] + [sm() for _ in range(LV)]
            for k_ in range(1, LV + 1):
                cmul(LPr[k_], LPi[k_], LPr[k_ - 1], LPi[k_ - 1], LPr[k_ - 1], LPi[k_ - 1])
            for d in range(2):
                sl = slice(d * NPAIR, (d + 1) * NPAIR)
                for k_ in range(LV + 1):
                    P.copy('dve', L1[d][k_][:, 0:NPAIR], LPr[k_][:, sl], R=[K], W=[K])
                    P.copy('dve', L1[d][k_][:, NPAIR:], LPr[k_][:, sl], R=[K], W=[K])
                    P.ts('dve', L2[d][k_][:, 0:NPAIR], LPi[k_][:, sl], -1.0, None, ALU.mult, R=[K], W=[K])
                    P.copy('dve', L2[d][k_][:, NPAIR:], LPi[k_][:, sl], R=[K], W=[K])
            if sub[0] <= 0.2:
                return
            Bre = ar.f32(N2 * 16); Bim = ar.f32(N2 * 16); Cre = ar.f32(N2 * 16); Cim = ar.f32(N2 * 16)
            P.dma('sp', Bre, bT_re[l], W=[K]); P.dma('sp', Bim, bT_im[l], W=[K])
            P.dma('sp', Cre, cT_re[l], W=[K]); P.dma('sp', Cim, cT_im[l], W=[K])
            b3 = lambda t_, d: t_.rearrange("p (d k h) -> p d k h", d=2, h=16)[:, d]
            MS = NPAIR * 128

            def mat():
                return ar.bf16(MS)
            mv = lambda m_: m_.rearrange("p (k j h) -> p k j h", k=NPAIR, j=8)
            o1 = ar.f32(NPAIR * 16); o2 = ar.f32(NPAIR * 16)
            o1v = o1.rearrange("p (k h) -> p k h", h=16); o2v = o2.rearrange("p (k h) -> p k h", h=16)

            def outer(dst_re, dst_im, d, j, fr, fi, Xre, Xim, neg_im):
                sl = slice(d * NPAIR, (d + 1) * NPAIR)
                frb = fr[:, sl].unsqueeze(2).to_broadcast([128, NPAIR, 16])
                fib = fi[:, sl].unsqueeze(2).to_broadcast([128, NPAIR, 16])
                xr = b3(Xre, d); xi = b3(Xim, d)
                tt(o1v, xr, frb, ALU.mult); tt(o2v, xi, fib, ALU.mult)
                tt(mv(dst_re)[:, :, j, :], o1v, o2v, ALU.subtract)
                tt(o1v, xi, frb, ALU.mult); tt(o2v, xr, fib, ALU.mult)
                if neg_im:
                    P.stt(mv(dst_im)[:, :, j, :], o1v, -1.0, o2v, ALU.mult, ALU.subtract, R=[K], W=[K])
                else:
                    tt(mv(dst_im)[:, :, j, :], o1v, o2v, ALU.add)
            CJr = [mat() for _ in range(2)]; CJn = [mat() for _ in range(2)]
            P.memset('dve', pm, 0.0, W=[K])
            P.memset('dve', pm[0:64, 0:1], 1.0, W=[K])
            P.memset('dve', pm[64:128, 1:2], 1.0, W=[K])
            BTr = [mat() for _ in range(2)]; BTi = [mat() for _ in range(2)]
            BCr = mat(); BCi = mat()
            CCr = [mat() for _ in range(2)]; CCn = [mat() for _ in range(2)]
            mkf = ar.f32(128); mkb = ar.f32(128); tm1 = ar.f32(128); tm2 = ar.f32(128)
            P.dma('sp', mkf, maskf_d, W=['mkf']); P.dma('sp', mkb, maskb_d, W=['mkb'])
            for j in range(8):
                outer(BTr[0], BTi[0], 0, j, ZFr[7 - j], ZFi[7 - j], Bre, Bim, False)
                outer(BTr[1], BTi[1], 1, j, ZFr[j], ZFi[j], Bre, Bim, False)
                outer(BCr, BCi, 0, j, ZNFr[j], ZNFi[j], Bre, Bim, False)
                outer(CJr[0], CJn[0], 0, j, Zr[j + 1], Zi[j + 1], Cre, Cim, True)
                outer(CJr[1], CJn[1], 1, j, Zr[8 - j], Zi[8 - j], Cre, Cim, True)
                outer(CCr[0], CCn[0], 0, j, Zr[j], Zi[j], Cre, Cim, True)
                outer(CCr[1], CCn[1], 1, j, ZNr[j], ZNi[j], Cre, Cim, True)
            m3 = lambda m_: m_.rearrange("p (k x) -> p k x", k=NPAIR)
            for d in range(2):
                for r_, src in ((0, CJr[d]), (1, CJn[d])):
                    for g2 in range(2):
                        dst = CZv[:, d, r_].rearrange("p (k t) m -> p k t m", t=2)[:, :, g2, :]
                        P.ts('dve', dst, m3(src), pm[:, g2:g2 + 1], None, ALU.mult, R=[K], W=[K])
            if sub[0] <= 0.3:
                return
            for d in range(2):
                for r_, src in ((0, BTr[d]), (1, BTi[d])):
                    for g2 in range(2):
                        rs = slice(g2 * 64, g2 * 64 + 64)
                        for (k0, nk) in blocks(0, NPAIR, 8):
                            ps, pk = nps()
                            P.mm([(ps[:, ki * 64:(ki + 1) * 64], m3(src)[rs, k0 + ki, :], ident_b[rs, rs], True, True) for ki in range(nk)],
                                 R=[K, 'ident_b'], W=[pk])
                            dst = BJv[:, d, r_].rearrange("p (k t) m -> p k t m", t=2)[:, k0:k0 + nk, g2, :]
                            P.copy('act', dst, ps[:, 0:nk * 64].rearrange("p (a m) -> p a m", m=64), R=[pk], W=['BJ'])
            if sub[0] <= 0.4:
                return
            BCL = [(BCr, BCi), (BTr[1], BTi[1])]
            for g in range(NG):
                k, g2 = g // 2, g % 2
                rs = slice(g2 * 64, g2 * 64 + 64)
                pss = []
                for d in range(2):
                    ps, pk = nps()
                    P.mm([(ps[:, 0:128], m3(BCL[d][0])[rs, k, :], m3(CCr[d])[rs, k, :], True, False),
                          (ps[:, 0:128], m3(BCL[d][1])[rs, k, :], m3(CCn[d])[rs, k, :], False, True)], R=[K], W=[pk])
                    pss.append((ps, pk))
                P.tt('dve', tm1, pss[0][0][:, 0:128], mkf, ALU.mult, R=[pss[0][1], 'mkf'], W=['tm1'])
                P.tt('dve', tm2, pss[1][0][:, 0:128], mkb, ALU.mult, R=[pss[1][1], 'mkb'], W=['tm2'])
                P.tt('dve', Mgv[:, g, :], tm1, tm2, ALU.add, R=['tm1', 'tm2'], W=['Mg'])
            if sub[0] <= 1:
                return
            P.barrier()
            ar.release(mk0)
            U8 = ar.bf16(NG * NCH); U8v = U8.rearrange("p (g c) -> p g c", g=NG)
            mk1 = ar.mark()
            es = ar.bf16(64 * 128); esv = es.rearrange("p (g j m) -> p g j m", g=8, j=8)
            cdma(es, esel_d, W=['es'])
            usb = ar.bf16(W + CTX)
            usv = usb.rearrange("p (c j) -> p c j", j=8)
            for ut in range(UT):
                r0 = POOLW + ut * 128
                cdma(usb[:, 0:W], U[r0:r0 + 128, :], R=['U'], W=['usb'])
                cdma(usb[:, W:W + CTX], U[r0:r0 + 128, 0:CTX], R=['U'], W=['usb'])
                for g8 in range(8):
                    g = ut * 8 + g8
                    for (cb, n) in blocks(0, NCH, 512):
                        ps, pk = nps()
                        P.mm([(ps[:, 0:n], esv[:, g8, j, :], usv[:, cb:cb + n, j], j == 0, j == 7) for j in range(8)],
                             R=['es', 'usb'], W=[pk])
                        P.copy('act' if g8 % 2 else 'dve', U8v[:, g, cb:cb + n], ps[:, 0:n], R=[pk], W=['U8'])
            if sub[0] <= 2:
                return
            P.barrier()
            ar.release(mk1)
            YDv = YD.rearrange("p (g c) -> p g c", g=NG)
            yst = [ar.bf16(NC1) for _ in range(2)]
            yc = [0]

            def ystage():
                i = yc[0] % 2
                yc[0] += 1
                return yst[i], ('yst', i)
            if sub[0] <= 3:
                return
            S = ar.f32(2 * PB * NC1); Sv = S.rearrange("p (r k c) -> p r k c", r=2, k=PB)
            Hb = [ar.bf16(2 * PB * (NC1 + 1)) for _ in range(2)]
            Hv = [h_.rearrange("p (r k c) -> p r k c", r=2, k=PB) for h_ in Hb]
            CM = 68
            tA = ar.f32(2 * PB * CM); tB = ar.f32(2 * PB * CM)

            def tv(t_, cnt):
                return t_[:, 0:2 * PB * cnt].rearrange("p (r k c) -> p r k c", r=2, k=PB)

            def cma(d, dst_pos, src_pos, cnt, step, Lk, bt):
                L1v = L1[d][Lk].rearrange("p (r k) -> p r k", r=2)[:, :, bt * PB:(bt + 1) * PB]
                L2v = L2[d][Lk].rearrange("p (r k) -> p r k", r=2)[:, :, bt * PB:(bt + 1) * PB]
                for m0 in range(0, cnt, CM):
                    c_ = min(CM, cnt - m0)

                    def sl(p0):
                        a = p0 + m0 * step
                        if d == 0:
                            return slice(a, a + (c_ - 1) * step + 1, step)
                        a = NC1 - 1 - a
                        e = a - (c_ - 1) * step - 1
                        return slice(a, e if e >= 0 else None, -step)
                    src = Sv[:, :, :, sl(src_pos)]
                    dst = Sv[:, :, :, sl(dst_pos)]
                    b1 = L1v.unsqueeze(3).to_broadcast([128, 2, PB, c_])
                    b2 = L2v.unsqueeze(3).to_broadcast([128, 2, PB, c_])
                    P.tt('dve', tv(tA, c_), src, b1, ALU.mult, R=['S', K], W=['tA'])
                    P.tt('pool' if c_ > 8 else 'dve', tv(tB, c_), src[:, ::-1], b2, ALU.mult, R=['S', K], W=['tB'])
                    P.tt('dve', tv(tA, c_), tv(tA, c_), tv(tB, c_), ALU.add, R=['tA', 'tB'], W=['tA'])
                    P.tt('dve', dst, dst, tv(tA, c_), ALU.add, R=['S', 'tA'], W=['S'])
            for bt in range(NB):
                for d in range(2):
                    cs = 0 if d == 0 else CCH
                    for kl in range(PB):
                        k = bt * PB + kl
                        for r_ in range(2):
                            for (cb, n) in blocks(0, NC1, 512):
                                ps, pk = nps()
                                P.mm([(ps[g2 * 64:(g2 + 1) * 64, 0:n], BJv[:, d, r_, 2 * k + g2, :], U8v[:, 2 * k + g2, cs + cb:cs + cb + n], True, True)
                                      for g2 in range(2)], R=['BJ', 'U8'], W=[pk])
                                P.copy('act' if r_ else 'dve', Sv[:, r_, kl, cb:cb + n], ps[:, 0:n], R=[pk], W=['S'])
                    for lv in range(LV):
                        s_ = 1 << lv
                        cnt = NC1 // (2 * s_)
                        cma(d, 2 * s_ - 1, s_ - 1, cnt, 2 * s_, lv, bt)
                    TT = 1 << LV
                    for m in range(1, NC1 // TT):
                        cma(d, TT * m + TT - 1, TT * m - 1, 1, TT, LV, bt)
                    for lv in range(LV - 1, -1, -1):
                        s_ = 1 << lv
                        cnt = NC1 // (2 * s_) - 1
                        if cnt > 0:
                            cma(d, 3 * s_ - 1, 2 * s_ - 1, cnt, 2 * s_, lv, bt)
                    if d == 0:
                        P.memset('pool', Hv[0][:, :, :, 0:1], 0.0, W=[('Hb', 0)])
                        P.copy('pool', Hv[0][:, :, :, 1:NC1 + 1], Sv, R=['S'], W=[('Hb', 0)])
                    else:
                        P.memset('pool', Hv[1][:, :, :, NC1:NC1 + 1], 0.0, W=[('Hb', 1)])
                        P.copy('pool', Hv[1][:, :, :, 0:NC1], Sv, R=['S'], W=[('Hb', 1)])
                for kl in range(PB):
                    k = bt * PB + kl
                    for g2 in range(2):
                        g = 2 * k + g2
                        ys, yk = ystage()
                        oblks = [(0, CCH)] + [(a, n) for a, n in blocks(CCH, NC1, 512)]
                        for (cb, n) in oblks:
                            hb0 = NC1 - CCH + 1 if cb < CCH else cb + 1 - CCH
                            ps, pk = nps()
                            P.mm([(ps[:, 0:n], Mgv[:, g, :], U8v[:, g, cb:cb + n], True, False),
                                  (ps[:, 0:n], CZv[:, 0, 0, g, :], Hv[0][:, 0, kl, cb:cb + n], False, False),
                                  (ps[:, 0:n], CZv[:, 0, 1, g, :], Hv[0][:, 1, kl, cb:cb + n], False, False),
                                  (ps[:, 0:n], CZv[:, 1, 0, g, :], Hv[1][:, 0, kl, hb0:hb0 + n], False, False),
                                  (ps[:, 0:n], CZv[:, 1, 1, g, :], Hv[1][:, 1, kl, hb0:hb0 + n], False, True)],
                                 R=[K, 'Mg', 'U8', ('Hb', 0), ('Hb', 1)], W=[pk])
                            P.copy('act', ys[:, cb:cb + n], ps[:, 0:n], R=[pk], W=[yk])
                        P.dma('sp', YDv[:, g, :], ys, R=[yk], W=['YD'])
            if sub[0] <= 4:
                return
            P.barrier()
            ar.release(0)
            Y8 = ar.bf16(NG * NC1); Y8v = Y8.rearrange("p (g c) -> p g c", g=NG)
            P.dma('sp', Y8v, YDv, R=['YD'], W=['Y8'])
            et = ar.bf16(64 * 128); etv = et.rearrange("p (g j m) -> p g j m", g=8, j=8)
            cdma(et, etsel_d, W=['et'])
            wgl = ar.bf16(UT * SSMW); wglv = wgl.rearrange("p (k n) -> p k n", k=UT)
            cdma(wglv, w_glu[l].rearrange("(k p) n -> p k n", p=128), W=['wgl'])
            sd = ar.f32(UT)
            P.dma('sp', sd, ssm_dT[l], W=['sd'])
            NE = 512
            u32 = ar.f32(UT * NE); yf = ar.f32(UT * NE); wv_ = ar.f32(UT * NE); sgm = ar.f32(UT * NE)
            geb = ar.bf16(UT * NE); sg2 = ar.f32(NE); sob = ar.bf16(UT * NE)
            for (c0, n, s) in tb512:
                r3 = lambda t_: t_[:, 0:UT * n].rearrange("p (k n) -> p k n", k=UT)
                cb0, ncq = c0 // 8, n // 8
                P.dma('sp', r3(u32), U[POOLW:D, c0:c0 + n].rearrange("(k p) n -> p k n", p=128), R=['U'], W=['u32'])
                for ut in range(UT):
                    ps, pk = nps()
                    psv = ps[:, 0:n].rearrange("p (c j) -> p c j", j=8)
                    mms = []
                    for j in range(8):
                        for g8 in range(8):
                            mms.append((psv[:, :, j], etv[:, g8, j, :], Y8v[:, ut * 8 + g8, cb0:cb0 + ncq], g8 == 0, g8 == 7))
                    P.mm(mms, R=['et', 'Y8'], W=[pk])
                    P.stt(r3(yf)[:, ut, :], r3(u32)[:, ut, :], sd[:, ut:ut + 1], ps[:, 0:n], ALU.mult, ALU.add,
                          R=['u32', 'sd', pk], W=['yf'])
                P.act(r3(wv_), r3(yf), AF.Square, R=['yf'], W=['wv'])
                P.ts('dve', r3(wv_), r3(wv_), 0.044715, 1.0, ALU.mult, ALU.add, R=['wv'], W=['wv'])
                P.tt('dve', r3(wv_), r3(wv_), r3(yf), ALU.mult, R=['wv', 'yf'], W=['wv'])
                P.act(r3(sgm), r3(wv_), AF.Sigmoid, scale=1.5957691216, R=['wv'], W=['sgm'])
                P.tt('dve', r3(yf), r3(yf), r3(sgm), ALU.mult, R=['yf', 'sgm'], W=['yf'])
                P.copy('pool', r3(geb), r3(yf), R=['yf'], W=['geb'])
                for uo in range(UT):
                    ps, pk = nps()
                    P.mm([(ps[:, 0:n], wglv[:, ui, uo * 128:(uo + 1) * 128], r3(geb)[:, ui, :], ui == 0, ui == UT - 1) for ui in range(UT)],
                         R=['wgl', 'geb'], W=[pk])
                    P.act(sg2[:, 0:n], ps[:, 0:n], AF.Sigmoid, R=[pk], W=['sg2'])
                    P.tt('dve', r3(sob)[:, uo, :], r3(yf)[:, uo, :], sg2[:, 0:n], ALU.mult, R=['yf', 'sg2'], W=['sob'])
                P.dma('sp', MIXIN[POOLW:D, c0:c0 + n].rearrange("(k p) n -> p k n", p=128), r3(sob), R=['sob'], W=['MIXIN'])

        def phase4(l):
            P.barrier()
            ar.release(0)
            wo = ar.bf16(KT * D); wov = wo.rearrange("p (k n) -> p k n", k=KT)
            for (a, n) in blocks(0, D, 512):
                cdma(wov[:, :, a:a + n], w_out[l][:, a:a + n].rearrange("(k p) n -> p k n", p=128), W=['wo'])
            NE = 256
            sets = [(ar.bf16(KT * NE), ar.f32(KT * NE), ar.f32(KT * NE)) for _ in range(2)]
            sq = ar.bf16(KT * NE); h2b = ar.bf16(KT * NE)
            rstd = ar.f32(NE)
            def ctx4(bi):
                (c0, n, s_) = tb256[bi]
                b = bi % 2
                mi, mix, xs = sets[b]
                r3 = lambda t_: t_[:, 0:KT * n].rearrange("p (k n) -> p k n", k=KT)
                return c0, n, s_, r3(mi), r3(mix), r3(xs), ('mi', b), ('mix', b), ('xs', b)

            def stageA(bi):
                c0, n, s_, miv, mixv, xsv, kmi, kmx, kxs = ctx4(bi)
                th = []

                def t0():
                    P.dma('sp', miv, MIXIN[:, c0:c0 + n].rearrange("(k p) n -> p k n", p=128), R=['MIXIN'], W=[kmi])
                    P.dma('sp', xsv, Rr[:, c0:c0 + n].rearrange("(k p) n -> p k n", p=128), R=['Rr'], W=[kxs])
                th.append(t0)
                for mt in range(KT):
                    def tm(mt=mt):
                        ps, pk = nps()
                        P.mm([(ps[:, 0:n], wov[:, kt, mt * 128:(mt + 1) * 128], miv[:, kt, :], kt == 0, kt == KT - 1) for kt in range(KT)],
                             R=['wo', kmi], W=[pk])
                        P.copy('act' if mt % 2 else 'dve', mixv[:, mt, :], ps[:, 0:n], R=[pk], W=[kmx])
                    th.append(tm)
                return th

            def stageB(bi):
                c0, n, s_, miv, mixv, xsv, kmi, kmx, kxs = ctx4(bi)
                r3 = lambda t_: t_[:, 0:KT * n].rearrange("p (k n) -> p k n", k=KT)
                rb = rstd[:, 0:n].unsqueeze(1).to_broadcast([128, KT, n])
                th = []
                th.append(lambda: rms_stats(mixv, n, sq, rstd, kmx, 'rstd'))
                th.append(lambda: P.tt('dve', mixv, mixv, rb, ALU.mult, R=[kmx, 'rstd'], W=[kmx]))
                for mt in range(KT):
                    th.append(lambda mt=mt: P.stt(xsv[:, mt, :], mixv[:, mt, :], G1v[:, l, mt, s_:s_ + 1], xsv[:, mt, :], ALU.mult, ALU.add,
                                                  R=[kmx, 'G1', kxs], W=[kxs]))
                th.append(lambda: P.dma('sp', Rr[:, c0:c0 + n].rearrange("(k p) n -> p k n", p=128), xsv, R=[kxs], W=['Rr']))
                th.append(lambda: rms_stats(xsv, n, sq, rstd, kxs, 'rstd'))
                th.append(lambda: P.tt('dve', mixv, xsv, rb, ALU.mult, R=[kxs, 'rstd'], W=[kmx]))
                for kt in range(KT):
                    th.append(lambda kt=kt: P.act(r3(h2b)[:, kt, :], mixv[:, kt, :], AF.Identity, scale=A2v[:, l, kt, s_:s_ + 1],
                                                  bias=modv[:, l, 3 * KT + kt, s_:s_ + 1], R=[kmx, 'A2', 'mod'], W=['h2b']))
                th.append(lambda: P.dma('sp', H2[:, c0:c0 + n].rearrange("(k p) n -> p k n", p=128), r3(h2b), R=['h2b'], W=['H2']))
                return th
            nb4 = len(tb256)
            pend = []
            for bi in range(nb4 + 1):
                interleave(stageA(bi) if bi < nb4 else [], pend)
                pend = stageB(bi) if bi < nb4 else []

        def phase5(l):
            P.barrier()
            ar.release(0)
            P.dma('sp', wconv[:], w_convT[l], W=['wconv'])
            ZT = (RG + 2) * GW
            h2g = ar.bf16(KT * ZT)
            wvg = [ar.bf16(2 * KT * 128) for _ in range(2)]
            dg = [ar.bf16(2 * 9 * 128) for _ in range(2)]
            acts = [ar.bf16(max(RG * GW, CTX)) for _ in range(2)]
            sgb = [ar.f32(512) for _ in range(2)]
            groups = [('ctx', 0, 0)] + [('lat', gi, gi * RG) for gi in range(NGRP)]
            qc = [0]
            pend_conv = [None]
            zpq = [[[ar.bf16(ZPW) for v in range(2)] for g in range(NGRP)] for _ in range(2)]
            for b_ in range(2):
                for g in range(NGRP):
                    for v in range(2):
                        P.memset('pool', zpq[b_][g][v], 0.0, W=[('zpq', b_, g, v)])
            for (kind, gi, r0) in groups:
                if kind == 'ctx':
                    ntz = CTX
                    zc0 = 0
                else:
                    zr0 = max(r0 - 1, 0); zr1 = min(r0 + RG + 1, ROWS)
                    ntz = (zr1 - zr0) * GW
                    zc0 = CTX + zr0 * GW
                h2v = h2g[:, 0:KT * ntz].rearrange("p (k n) -> p k n", k=KT)
                P.dma('sp', h2v, H2[:, zc0:zc0 + ntz].rearrange("(k p) n -> p k n", p=128), R=['H2'], W=['h2g'])
                first_grp = (kind == 'ctx')
                for q in range(NQ):
                    b = qc[0] % 2
                    qc[0] += 1
                    wt = wvg[b]; wk = ('wvg', b)
                    wtv = wt.rearrange("p (v k m) -> p v k m", v=2, k=KT)
                    if first_grp:
                        for v in range(2):
                            cdma(wtv[:, v], w_up[l][:, v * DFF + q * 128:v * DFF + (q + 1) * 128].rearrange("(k p) m -> p k m", p=128), W=[wk])
                        P.dma('sp', WUs[q], wt, R=[wk], W=[('WUs', q)])
                    else:
                        P.dma('sp', wt, WUs[q], R=[('WUs', q)], W=[wk])
                    dgt = dg[b]; dk = ('dg', b)
                    dgv = dgt.rearrange("p (v t m) -> p v t m", v=2, t=9)
                    for v in range(2):
                        for t_ in range(9):
                            if kind == 'ctx' and t_ // 3 != 1:
                                continue
                            P.ts('dve', dgv[:, v, t_, :], ident_f[:], wconvv[:, v * NQ + q, t_:t_ + 1], None, ALU.mult,
                                 R=['ident_f', 'wconv'], W=[dk])
                    at = acts[b]; ak = ('acts', b)
                    if kind == 'ctx':
                        for v in range(2):
                            ps, pk = nps()
                            P.mm([(ps[:, 0:CTX], wtv[:, v, kt, :], h2v[:, kt, :], kt == 0, kt == KT - 1) for kt in range(KT)],
                                 R=[wk, 'h2g'], W=[pk])
                            P.copy('act', zpc[v][:, 1:CTX + 1], ps[:, 0:CTX], R=[pk], W=[('zpc', v)])
                        pcs = []
                        for v in range(2):
                            ps, pk = nps()
                            P.mm([(ps[:, 0:CTX], dgv[:, v, 3 + dx, :], zpc[v][:, dx:dx + CTX], dx == 0, dx == 2) for dx in range(3)],
                                 R=[dk, ('zpc', v)], W=[pk])
                            pcs.append((ps, pk))
                        sb_ = sgb[0]
                        P.act(sb_[:, 0:CTX], pcs[1][0][:, 0:CTX], AF.Silu, R=[pcs[1][1]], W=[('sgb', 0)])
                        P.tt('dve', at[:, 0:CTX], pcs[0][0][:, 0:CTX], sb_[:, 0:CTX], ALU.mult, R=[pcs[0][1], ('sgb', 0)], W=[ak])
                        P.dma('pool', ACTS[q * 128:(q + 1) * 128, 0:CTX], at[:, 0:CTX], R=[ak], W=['ACTS'])
                    else:
                        zpv = [zpq[b][gi][v].rearrange("p (r c) -> p r c", c=GW + 2) for v in range(2)]
                        zk = [('zpq', b, gi, v) for v in range(2)]

                        def up_part(v, wtv=wtv, wk=wk, zpv=zpv, zk=zk):
                            for (ra, nr_) in blocks(zr0, zr1, 8):
                                n = nr_ * GW
                                ta_ = (ra - zr0) * GW
                                ps, pk = nps()
                                P.mm([(ps[:, 0:n], wtv[:, v, kt, :], h2v[:, kt, ta_:ta_ + n], kt == 0, kt == KT - 1) for kt in range(KT)],
                                     R=[wk, 'h2g'], W=[pk])
                                sl0 = ra - (r0 - 1)
                                P.copy('act', zpv[v][:, sl0:sl0 + nr_, 1:GW + 1], ps[:, 0:n].rearrange("p (r c) -> p r c", c=GW),
                                       R=[pk], W=[zk[v]])

                        def conv_part(q=q, dgv=dgv, dk=dk, zpv=zpv, zk=zk, at=at, ak=ak):
                            for oi, (ra, nr_) in enumerate(blocks(r0, r0 + RG, 8)):
                                n = nr_ * GW
                                pcs = []
                                for v in range(2):
                                    ps, pk = nps()
                                    mms = []
                                    for t_ in range(9):
                                        dy, dx = t_ // 3, t_ % 3
                                        sl0 = ra - (r0 - 1) + dy - 1
                                        mms.append((ps[:, 0:n].rearrange("p (r c) -> p r c", c=GW), dgv[:, v, t_, :],
                                                    zpv[v][:, sl0:sl0 + nr_, dx:dx + GW], t_ == 0, t_ == 8))
                                    P.mm(mms, R=[dk, zk[v]], W=[pk])
                                    pcs.append((ps, pk))
                                sb_ = sgb[oi % 2]; sk = ('sgb', oi % 2)
                                P.act(sb_[:, 0:n], pcs[1][0][:, 0:n], AF.Silu, R=[pcs[1][1]], W=[sk])
                                to = (ra - r0) * GW
                                P.tt('dve', at[:, to:to + n], pcs[0][0][:, 0:n], sb_[:, 0:n], ALU.mult, R=[pcs[0][1], sk], W=[ak])
                            c0 = CTX + r0 * GW
                            P.dma('pool', ACTS[q * 128:(q + 1) * 128, c0:c0 + RG * GW], at[:, 0:RG * GW], R=[ak], W=['ACTS'])
                        up_part(0)
                        if pend_conv[0] is not None:
                            pend_conv[0]()
                        up_part(1)
                        pend_conv[0] = conv_part
                if pend_conv[0] is not None:
                    pend_conv[0]()
                    pend_conv[0] = None

        def phase6(l):
            P.barrier()
            ar.release(0)
            last = (l == DEPTH - 1)
            ab = [ar.bf16(NQ * 512) for _ in range(2)]
            wd = [ar.bf16(NQ * 128) for _ in range(2)]
            f = ar.f32(KT * 512); xs = ar.f32(KT * 512); rstd = ar.f32(512)
            sqs = [ar.bf16(512) for _ in range(2)]
            wc = [0]
            for bi, (c0, n, s) in enumerate(tb512):
                pb_ = bi % 2
                abv = ab[pb_][:, 0:NQ * n].rearrange("p (q n) -> p q n", q=NQ)
                r3 = lambda t_: t_[:, 0:KT * n].rearrange("p (k n) -> p k n", k=KT)
                qparts = blocks(0, NQ, (NQ + 3) // 4)
                abk = [('ab', pb_, qi) for qi in range(len(qparts))]

                def load_ab(bj):
                    (cc0, nn, _s) = tb512[bj]
                    pj = bj % 2
                    av = ab[pj][:, 0:NQ * nn].rearrange("p (q n) -> p q n", q=NQ)
                    for qi, (q0, nq) in enumerate(qparts):
                        P.dma('sp', av[:, q0:q0 + nq, :], ACTS[q0 * 128:(q0 + nq) * 128, cc0:cc0 + nn].rearrange("(q p) n -> p q n", p=128),
                              R=['ACTS'], W=[('ab', pj, qi)])
                if bi == 0:
                    load_ab(0)
                if bi + 1 < len(tb512):
                    load_ab(bi + 1)
                fk = [('f', m_) for m_ in range(KT)]
                pend = None
                for mt in range(KT):
                    b = wc[0] % 2
                    wc[0] += 1
                    wt = wd[b]; wk = ('wd', b)
                    wtv = wt.rearrange("p (q m) -> p q m", q=NQ)
                    if bi == 0:
                        cdma(wtv, w_down[l][:, mt * 128:(mt + 1) * 128].rearrange("(q p) m -> p q m", p=128), W=[wk])
                        P.dma('sp', WDs[mt], wt, R=[wk], W=[('WDs', mt)])
                    else:
                        P.dma('sp', wt, WDs[mt], R=[('WDs', mt)], W=[wk])
                    ps, pk = nps()
                    P.mm([(ps[:, 0:n], wtv[:, q, :], abv[:, q, :], q == 0, q == NQ - 1) for q in range(NQ)], R=[wk] + abk, W=[pk])
                    P.copy('act' if mt % 2 else 'dve', r3(f)[:, mt, :], ps[:, 0:n], R=[pk], W=[fk[mt]])
                    if pend is not None:
                        pend()

                    def pend(mt=mt):
                        sq_ = sqs[mt % 2]; sk_ = ('sqs', mt % 2)
                        P.act(sq_[:, 0:n], r3(f)[:, mt, :], AF.Square, R=[fk[mt]], W=[sk_])
                        P.mm([(pss[:, 0:n], ones_bf[:], sq_[:, 0:n], mt == 0, mt == KT - 1)], R=[sk_, 'ones'], W=['pss'])
                pend()
                P.dma('pool', r3(xs), Rr[:, c0:c0 + n].rearrange("(k p) n -> p k n", p=128), R=['Rr'], W=['xs'])
                P.act(rstd[:, 0:n], pss[:, 0:n], AF.Sqrt, bias=EPS, scale=1.0 / D, R=['pss'], W=['rstd'])
                P.add('dve', lambda e, n=n: e.reciprocal(rstd[:, 0:n], rstd[:, 0:n]), R=['rstd'], W=['rstd'])
                P.tt('dve', r3(f), r3(f), rstd[:, 0:n].unsqueeze(1).to_broadcast([128, KT, n]), ALU.mult, R=fk + ['rstd'], W=fk)
                for mt in range(KT):
                    P.stt(r3(xs)[:, mt, :], r3(f)[:, mt, :], G2v[:, l, mt, s:s + 1], r3(xs)[:, mt, :], ALU.mult, ALU.add,
                          R=[fk[mt], 'G2', 'xs'], W=['xs'])
                if last and s == 0:
                    P.dma('pool', out[:, c0 - CTX:c0 - CTX + n].rearrange("(k p) n -> p k n", p=128), r3(xs), R=['xs'], W=['out'])
                else:
                    P.dma('pool', Rr[:, c0:c0 + n].rearrange("(k p) n -> p k n", p=128), r3(xs), R=['xs'], W=['Rr'])

        stop = cfg_stop[0]
        phase0()
        for l in range(DEPTH):
            for ph, fn in enumerate((phase1, phase2, phase3, phase4, phase5, phase6)):
                if stop is not None and (l, ph + 1) > stop:
                    break
                fn(l)
        info = P.emit(st)
        info['arena_peak'] = ar.peak
        print("build info", info, flush=True)
    return nc


cfg_stop = [None]
RG_MAX = [32]
AW_OVR = [45600]
sub = [9]


def _consts():
    ident = np.eye(128, dtype=np.float32)
    esel = np.zeros((128, 8, 8, 128), np.float32)
    etsel = np.zeros((128, 8, 8, 128), np.float32)
    for g8 in range(8):
        for j in range(8):
            for h in range(16):
                esel[g8 * 16 + h, g8, j, j * 16 + h] = 1.0
                etsel[j * 16 + h, g8, j, g8 * 16 + h] = 1.0
    jj = np.arange(128) // 16
    maskf = (jj[None, :] >= jj[:, None]).astype(np.float32)
    maskb = (jj[:, None] >= jj[None, :]).astype(np.float32)
    iet = np.zeros((4, 16), np.float32)
    for wi, w in enumerate((2, 4, 8, 16)):
        for t in range(8):
            iet[wi, t] = 1.0 / (t + w // 2 if t < w // 2 else w)
            e = 7 - t
            iet[wi, 8 + t] = 1.0 / (e + 1 + w // 2 if e <= w // 2 - 1 else w)
    ietab = np.broadcast_to(iet.reshape(1, 64), (128, 64)).copy()
    return dict(ident=ident, esel=esel.reshape(128, -1), etsel=etsel.reshape(128, -1), maskf=maskf, maskb=maskb, ietab=ietab)


def _prep_shared(cfg, inp):
    D, L, CTX, DFF, DEPTH, GW = cfg
    KT = D // 128
    f = lambda a: np.ascontiguousarray(np.asarray(a, dtype=np.float32))
    NG = (D // 4) // 16
    NPAIR = NG // 2
    sh = {}
    for k in ("w_ada", "w_in", "w_pool", "w_glu", "w_out", "w_up", "w_down"):
        sh[k] = f(inp[k])
    sh["b_adaT"] = f(np.asarray(inp["b_ada"]).reshape(DEPTH, 6 * KT, 128).transpose(0, 2, 1))
    sh["pscT"] = f(np.asarray(inp["pool_scale"]).reshape(DEPTH, -1, 128).transpose(0, 2, 1))

    def statelay(a):
        a = np.asarray(a).reshape(DEPTH, 2, NPAIR, 2, 64)
        return f(a.transpose(0, 3, 4, 1, 2).reshape(DEPTH, 128, 2 * NPAIR))
    sh["a_reT"] = statelay(inp["ssm_a_re"])
    sh["a_imT"] = statelay(inp["ssm_a_im"])
    ldt = np.broadcast_to(np.asarray(inp["ssm_log_dt"])[..., None], (DEPTH, 2, NG, 64))
    sh["ldtT"] = statelay(ldt)

    def blay(a):
        a = np.asarray(a).reshape(DEPTH, 2, NPAIR, 2, 64, 16)
        return f(a.transpose(0, 3, 4, 1, 2, 5).reshape(DEPTH, 128, 2 * NPAIR * 16))

    def clay(a):
        a = np.asarray(a).reshape(DEPTH, 2, NPAIR, 2, 16, 64)
        return f(a.transpose(0, 3, 5, 1, 2, 4).reshape(DEPTH, 128, 2 * NPAIR * 16))
    sh["bT_re"] = blay(inp["ssm_b_re"]); sh["bT_im"] = blay(inp["ssm_b_im"])
    sh["cT_re"] = clay(inp["ssm_c_re"]); sh["cT_im"] = clay(inp["ssm_c_im"])
    sh["ssm_dT"] = f(np.asarray(inp["ssm_d"]).reshape(DEPTH, -1, 128).transpose(0, 2, 1))
    g = np.stack([np.asarray(inp[k]) for k in ("g_pre_mix", "g_post_mix", "g_pre_ffn", "g_post_ffn")], 1)
    sh["gT"] = f(g.reshape(DEPTH, 4, KT, 128).transpose(3, 0, 1, 2).reshape(128, DEPTH * 4 * KT))
    wc = np.asarray(inp["w_conv"]).reshape(DEPTH, 9, -1, 128)
    sh["w_convT"] = f(wc.transpose(0, 3, 2, 1).reshape(DEPTH, 128, -1))
    sh.update(_consts())
    return sh


def run_cfg(cfg, inp, ncores, dbg=False, stop=None):
    D, L, CTX, DFF, DEPTH, GW = cfg
    KT = D // 128
    cfg_stop[0] = stop
    nc = build(cfg, dbg=dbg)
    sh = _prep_shared(cfg, inp)
    x = np.asarray(inp["x"], dtype=np.float32); c = np.asarray(inp["c"], dtype=np.float32)
    ctx = np.asarray(inp["ctx"], dtype=np.float32); c_ctx = np.asarray(inp["c_ctx"], dtype=np.float32)
    in_maps = []
    for b in range(ncores):
        m = dict(sh)
        m["xT"] = np.ascontiguousarray(x[b].T)
        m["ctxT"] = np.ascontiguousarray(ctx[b].T)
        ccv = np.stack([c[b].reshape(KT, 128), c_ctx.reshape(KT, 128)], -1)
        m["cc"] = np.ascontiguousarray(ccv.transpose(1, 0, 2).reshape(128, KT * 2))
        in_maps.append(m)
    res = run_bass_kernel_spmd(nc, in_maps, core_ids=list(range(ncores)))
    return res


def kernel(**inputs):
    cfg = (2048, 4096, 256, 5632, 4, 64)
    res = run_cfg(cfg, inputs, 8)
    outs = [np.ascontiguousarray(res.results[b]["out"].T) for b in range(8)]
    return np.stack(outs, 0).astype(np.float32)
```
